# Optimizing a Trainium2 kernel written in Bass

```python
import math
import jax, jax.numpy as jnp
from jax import lax
import numpy as np

D_MODEL = 1024
BATCH = 16
SEQ = 256
DEPTH = 2
DEC_BATCH = 2
DEC_SEQ = 2048
PAST_LEN = 256

GRID_W = 64
BR_W = 512
N_BRANCH = 4
N_HEADS = 8
N_KV = 2
HEAD_DIM = 64
Q_PER_KV = N_HEADS // N_KV
WINDOW = 128
BLOCK = 128
ROPE_THETA = 10000.0
NEG_INF = -1e30
CONV_K = 31
POOL_WINDOWS = (2, 4, 8, 16)
POOL_GROUPS = 4
POOL_GW = BR_W // POOL_GROUPS
SSM_CH = 16
SSM_GROUPS = BR_W // SSM_CH
SSM_P = 64
EPS = 1e-6

IN_SIZES = (N_HEADS * HEAD_DIM, N_KV * HEAD_DIM, N_KV * HEAD_DIM, BR_W,
            2 * BR_W, BR_W,
            BR_W, BR_W,
            BR_W, BR_W,
            N_BRANCH * D_MODEL)
IN_COLS = sum(IN_SIZES)

kernel_name = "hybrid_diffusion_prefix_step"

f32 = jnp.float32


def _rmsnorm(x, g):
    xf = x.astype(f32)
    xf = xf * lax.rsqrt(jnp.mean(xf * xf, axis=-1, keepdims=True) + EPS)
    return xf.astype(x.dtype) * g


def _layernorm(x, g, b):
    xf = x.astype(f32)
    mu = jnp.mean(xf, axis=-1, keepdims=True)
    xc = xf - mu
    xn = xc * lax.rsqrt(jnp.mean(xc * xc, axis=-1, keepdims=True) + EPS)
    return xn.astype(x.dtype) * g + b


def _split_cols(u):
    idx, s = [], 0
    for n in IN_SIZES[:-1]:
        s += n
        idx.append(s)
    return jnp.split(u, idx, axis=-1)


def _axial_rope(x):
    L = x.shape[1]
    rows = L // GRID_W
    row = jnp.repeat(jnp.arange(rows, dtype=f32), GRID_W)
    col = jnp.tile(jnp.arange(GRID_W, dtype=f32), rows)
    n_freq = HEAD_DIM // 4
    inv = ROPE_THETA ** (-jnp.arange(n_freq, dtype=f32) / n_freq)
    ang = jnp.concatenate([row[:, None] * inv, col[:, None] * inv], axis=-1)
    shape = (1, L) + (1,) * (x.ndim - 3) + (HEAD_DIM // 2,)
    cos = jnp.cos(ang).reshape(shape)
    sin = jnp.sin(ang).reshape(shape)
    xf = x.astype(f32)
    x1, x2 = xf[..., :HEAD_DIM // 2], xf[..., HEAD_DIM // 2:]
    return jnp.concatenate([x1 * cos - x2 * sin, x1 * sin + x2 * cos], axis=-1).astype(x.dtype)


def _attend(q, kv_groups, sink):
    scale = HEAD_DIM ** -0.5
    scores = []
    for k, v, mask in kv_groups:
        s = jnp.einsum('bqkgd,bskd->bkgqs', q, k).astype(f32) * scale
        if mask is not None:
            s = jnp.where(mask, s, NEG_INF)
        scores.append(s)
    sink_col = jnp.broadcast_to(sink.astype(f32).reshape(N_KV, Q_PER_KV, 1, 1), scores[0].shape[:-1] + (1,))
    p = jax.nn.softmax(jnp.concatenate(scores + [sink_col], axis=-1), axis=-1)
    out, start = None, 0
    for (k, v, _), s in zip(kv_groups, scores):
        n = s.shape[-1]
        o = jnp.einsum('bkgqs,bskd->bqkgd', p[..., start:start + n].astype(v.dtype), v)
        out = o if out is None else out + o
        start += n
    return out


def _context_attention(q, k, v, sink):
    B, S = q.shape[:2]
    nb = S // BLOCK
    qb = q.reshape(B, nb, BLOCK, N_KV, Q_PER_KV, HEAD_DIM).transpose(1, 0, 2, 3, 4, 5)
    out = lax.map(lambda qi: _attend(qi, [(k, v, None)], sink), qb)
    return out.transpose(1, 0, 2, 3, 4, 5).reshape(B, S, N_HEADS * HEAD_DIM)


def _latent_attention(q, k, v, ck, cv, sink):
    B, L = q.shape[:2]
    nb = L // BLOCK
    pad = ((0, 0), (BLOCK, BLOCK), (0, 0), (0, 0))
    kp, vp = jnp.pad(k, pad), jnp.pad(v, pad)

    def block(i):
        q_i = lax.dynamic_slice_in_dim(q, i * BLOCK, BLOCK, axis=1)
        k_i = lax.dynamic_slice_in_dim(kp, i * BLOCK, 3 * BLOCK, axis=1)
        v_i = lax.dynamic_slice_in_dim(vp, i * BLOCK, 3 * BLOCK, axis=1)
        qpos = i * BLOCK + jnp.arange(BLOCK)
        kpos = (i - 1) * BLOCK + jnp.arange(3 * BLOCK)
        mask = (jnp.abs(qpos[:, None] - kpos[None, :]) <= WINDOW) & (kpos >= 0)[None, :] & (kpos < L)[None, :]
        return _attend(q_i, [(ck, cv, None), (k_i, v_i, mask)], sink)

    out = lax.map(block, jnp.arange(nb))
    return out.transpose(1, 0, 2, 3, 4, 5).reshape(B, L, N_HEADS * HEAD_DIM)


def _conformer_conv(a, w_dw, b_dw, ln_g, ln_b):
    x = a[..., :BR_W] * jax.nn.sigmoid(a[..., BR_W:])
    y = lax.conv_general_dilated(x.astype(w_dw.dtype), w_dw[:, None, :], window_strides=(1,),
                                 padding=[(CONV_K // 2, CONV_K // 2)],
                                 dimension_numbers=('NWC', 'WIO', 'NWC'),
                                 feature_group_count=BR_W) + b_dw
    return jax.nn.silu(_layernorm(y, ln_g, ln_b))


def _multiscale_pool(x, w_pool, scale):
    B, L, _ = x.shape
    xg = x.reshape(B, L, POOL_GROUPS, POOL_GW).astype(f32)
    cs = jnp.pad(jnp.cumsum(xg, axis=1), ((0, 0), (1, 0), (0, 0), (0, 0)))
    t = jnp.arange(L)
    pooled = []
    for g, w in enumerate(POOL_WINDOWS):
        lo = jnp.clip(t - w // 2, 0, L)
        hi = jnp.clip(t + w - w // 2, 0, L)
        cnt = (hi - lo).astype(f32)[None, :, None]
        pooled.append((cs[:, hi, g] - cs[:, lo, g]) / cnt)
    d = (jnp.stack(pooled, axis=2) - xg).astype(x.dtype)
    y = jnp.einsum('blgc,gcd->blgd', d, w_pool)
    return y.reshape(B, L, BR_W) * scale


def _lin_combine(e1, e2):
    a1, b1 = e1
    a2, b2 = e2
    return a1 * a2, a2 * b1 + b2


def _diag_scan(xg, lam_bar, b_bar, c_re, c_im, h0):
    bu = lax.complex(jnp.einsum('blgc,gpc->blgp', xg, jnp.real(b_bar)),
                     jnp.einsum('blgc,gpc->blgp', xg, jnp.imag(b_bar)))
    if h0 is not None:
        bu = bu.at[:, 0].add(lam_bar * h0)
    a = jnp.broadcast_to(lam_bar, bu.shape)
    _, h = lax.associative_scan(_lin_combine, (a, bu), axis=1)
    y = jnp.einsum('blgp,gcp->blgc', jnp.real(h), c_re) - jnp.einsum('blgp,gcp->blgc', jnp.imag(h), c_im)
    return y, h[:, -1]


def _ssm_branch(x, p, state):
    B, L, _ = x.shape
    xg = x.astype(f32).reshape(B, L, SSM_GROUPS, SSM_CH)
    y_sum, finals = None, []
    for d in range(2):
        lam = lax.complex(p['lam_re'][d].astype(f32), p['lam_im'][d].astype(f32))
        dt = jnp.exp(p['log_dt'][d].astype(f32))[:, None]
        lam_bar = jnp.exp(lam * dt)
        bmat = lax.complex(p['b_re'][d].astype(f32), p['b_im'][d].astype(f32))
        b_bar = ((lam_bar - 1.0) / lam)[..., None] * bmat
        h0 = None if state is None else lax.complex(state[:, d, 0].astype(f32), state[:, d, 1].astype(f32))
        xd = xg if d == 0 else jnp.flip(xg, axis=1)
        y, h_last = _diag_scan(xd, lam_bar, b_bar, p['c_re'][d].astype(f32), p['c_im'][d].astype(f32), h0)
        if d == 1:
            y = jnp.flip(y, axis=1)
        y_sum = y if y_sum is None else y_sum + y
        finals.append(h_last)
    y = y_sum + p['ssm_d'].astype(f32).reshape(SSM_GROUPS, SSM_CH) * xg
    y = jax.nn.gelu(y.reshape(B, L, BR_W)).astype(x.dtype)
    z = y @ p['glu_w']
    out = z[..., :BR_W] * jax.nn.sigmoid(z[..., BR_W:])
    if state is None:
        packed = jnp.stack([jnp.stack([jnp.real(h), jnp.imag(h)], axis=1) for h in finals], axis=1)
        return out, packed
    return out, None


def _layer(x, mod, p, cache):
    B, L, _ = x.shape
    shift, scale, gate = jnp.split(mod, 3, axis=-1)
    h = _rmsnorm(x, p['norm_g']) * (1.0 + scale) + shift
    u = h @ p['w_in']
    q, k, v, g_att, a_conv, g_conv, x_pool, g_pool, x_ssm, g_ssm, g_mrg = _split_cols(u)
    q = q.reshape(B, L, N_KV, Q_PER_KV, HEAD_DIM)
    k = k.reshape(B, L, N_KV, HEAD_DIM)
    v = v.reshape(B, L, N_KV, HEAD_DIM)
    if cache is None:
        o_att = _context_attention(q, k, v, p['sink'])
        o_ssm, st = _ssm_branch(x_ssm, p, None)
        new = (k, v, st)
    else:
        ck, cv, st0 = cache
        o_att = _latent_attention(_axial_rope(q), _axial_rope(k), v, ck, cv, p['sink'])
        o_ssm, _ = _ssm_branch(x_ssm, p, st0)
        new = None
    o_conv = _conformer_conv(a_conv, p['conv_dw'], p['conv_db'], p['conv_ln_g'], p['conv_ln_b'])
    o_pool = _multiscale_pool(x_pool, p['pool_w'], p['pool_scale'])
    br = jnp.stack([o_att * jax.nn.silu(g_att), o_conv * jax.nn.silu(g_conv),
                    o_pool * jax.nn.silu(g_pool), o_ssm * jax.nn.silu(g_ssm)], axis=2)
    proj = jnp.einsum('blnw,nwd->blnd', br, p['w_br'])
    gates = jax.nn.sigmoid(g_mrg.reshape(B, L, N_BRANCH, D_MODEL))
    merged = jnp.sum(gates * proj, axis=2)
    return x + gate * (merged @ p['w_out']), new


def setup_inputs(seed: int = 0) -> dict:
    key = jax.random.key(seed)
    ks = jax.random.split(key, 32)
    W, G, P, C, D = BR_W, SSM_GROUPS, SSM_P, SSM_CH, D_MODEL

    def nrm(k, shape, s):
        return jax.random.normal(k, shape, f32) * s

    return {
        "x_prompt": nrm(ks[0], (BATCH, SEQ, D), 1.0),
        "x_sample": nrm(ks[1], (DEC_BATCH, DEC_SEQ, D), 1.0),
        "cache_k": nrm(ks[2], (DEC_BATCH, DEPTH, PAST_LEN, N_KV, HEAD_DIM), 1.0),
        "cache_v": nrm(ks[3], (DEC_BATCH, DEPTH, PAST_LEN, N_KV, HEAD_DIM), 1.0),
        "state_ssm": nrm(ks[4], (DEC_BATCH, DEPTH, 2, 2, G, P), 0.3),
        "c": nrm(ks[5], (DEC_BATCH, D), 1.0),
        "c_ctx": nrm(ks[6], (D,), 1.0),
        "norm_g": 1.0 + nrm(ks[7], (DEPTH, D), 0.02),
        "w_ada": nrm(ks[8], (DEPTH, D, 3 * D), 0.5 * D ** -0.5),
        "b_ada": nrm(ks[9], (DEPTH, 3 * D), 0.02),
        "w_in": nrm(ks[10], (DEPTH, D, IN_COLS), D ** -0.5),
        "attn_sink": nrm(ks[11], (DEPTH, N_HEADS), 0.5),
        "conv_dw": nrm(ks[12], (DEPTH, CONV_K, W), CONV_K ** -0.5),
        "conv_db": nrm(ks[13], (DEPTH, W), 0.02),
        "conv_ln_g": 1.0 + nrm(ks[14], (DEPTH, W), 0.02),
        "conv_ln_b": nrm(ks[15], (DEPTH, W), 0.02),
        "pool_w": nrm(ks[16], (DEPTH, POOL_GROUPS, POOL_GW, POOL_GW), POOL_GW ** -0.5),
        "pool_scale": 1.0 + nrm(ks[17], (DEPTH, W), 0.1),
        "ssm_lam_re": -0.5 + nrm(ks[18], (DEPTH, 2, G, P), 0.01),
        "ssm_lam_im": math.pi * jnp.arange(P, dtype=f32) + nrm(ks[19], (DEPTH, 2, G, P), 0.01),
        "ssm_log_dt": jax.random.uniform(ks[20], (DEPTH, 2, G), f32, math.log(1e-3), math.log(1e-1)),
        "ssm_b_re": nrm(ks[21], (DEPTH, 2, G, P, C), (2 * C) ** -0.5),
        "ssm_b_im": nrm(ks[22], (DEPTH, 2, G, P, C), (2 * C) ** -0.5),
        "ssm_c_re": nrm(ks[23], (DEPTH, 2, G, C, P), (2 * P) ** -0.5),
        "ssm_c_im": nrm(ks[24], (DEPTH, 2, G, C, P), (2 * P) ** -0.5),
        "ssm_d": nrm(ks[25], (DEPTH, W), 1.0),
        "ssm_glu_w": nrm(ks[26], (DEPTH, W, 2 * W), W ** -0.5),
        "w_br": nrm(ks[27], (DEPTH, N_BRANCH, W, D), W ** -0.5),
        "w_out": nrm(ks[28], (DEPTH, D, D), D ** -0.5),
        "final_g": 1.0 + nrm(ks[29], (D,), 0.02),
    }


def reference(x_prompt, x_sample, cache_k, cache_v, state_ssm, c, c_ctx, norm_g, w_ada, b_ada, w_in,
              attn_sink, conv_dw, conv_db, conv_ln_g, conv_ln_b, pool_w, pool_scale, ssm_lam_re, ssm_lam_im,
              ssm_log_dt, ssm_b_re, ssm_b_im, ssm_c_re, ssm_c_im, ssm_d, ssm_glu_w, w_br, w_out, final_g):
    xp, xs = x_prompt, x_sample
    new_k, new_v, new_st = [], [], []
    for l in range(DEPTH):
        p = {
            'norm_g': norm_g[l], 'w_in': w_in[l], 'sink': attn_sink[l],
            'conv_dw': conv_dw[l], 'conv_db': conv_db[l], 'conv_ln_g': conv_ln_g[l], 'conv_ln_b': conv_ln_b[l],
            'pool_w': pool_w[l], 'pool_scale': pool_scale[l],
            'lam_re': ssm_lam_re[l], 'lam_im': ssm_lam_im[l], 'log_dt': ssm_log_dt[l],
            'b_re': ssm_b_re[l], 'b_im': ssm_b_im[l], 'c_re': ssm_c_re[l], 'c_im': ssm_c_im[l],
            'ssm_d': ssm_d[l], 'glu_w': ssm_glu_w[l], 'w_br': w_br[l], 'w_out': w_out[l],
        }
        mod_p = (jax.nn.silu(c_ctx) @ w_ada[l] + b_ada[l])[None, None, :]
        mod_s = (jax.nn.silu(c) @ w_ada[l] + b_ada[l])[:, None, :]
        xp, (k_l, v_l, st_l) = _layer(xp, mod_p, p, None)
        new_k.append(k_l)
        new_v.append(v_l)
        new_st.append(st_l)
        xs, _ = _layer(xs, mod_s, p, (cache_k[:, l], cache_v[:, l], state_ssm[:, l]))
    y_prompt = _rmsnorm(xp, final_g)
    y_sample = _rmsnorm(xs, final_g)
    new_cache_k = jnp.stack(new_k, axis=1)
    new_cache_v = jnp.stack(new_v, axis=1)
    new_state_ssm = jnp.stack(new_st, axis=1)
    return (y_prompt, y_sample, new_cache_k, new_cache_v, new_state_ssm)
```

```python
import math
from contextlib import ExitStack
import numpy as np
import concourse.bass as bass
import concourse.mybir as mybir
from concourse.bass_utils import run_bass_kernel_spmd
from concourse.ap import AP

F32 = mybir.dt.float32
BF16 = mybir.dt.bfloat16
I32 = mybir.dt.int32
ALU = mybir.AluOpType
ACT = mybir.ActivationFunctionType

D = 1024
NCOL = 8960
LS = 2048
LP = 256
EPS = 1e-6
NDMASLOT = 24

C_Q, C_K, C_V, C_GATT = 0, 512, 640, 768
C_ACONV, C_GCONV = 1280, 2304
C_XPOOL, C_GPOOL = 2816, 3328
C_XSSM, C_GSSM = 3840, 4352
C_MRG = 4864


class Prog:
    ENGS = ("pe", "act", "dve", "pool", "sp")

    def __init__(self, nc):
        self.nc = nc
        self.ops = []
        self.cnt = {e: 0 for e in self.ENGS}
        self.dma_n = {"sp": 0, "pool": 0}
        self.slot_uses = {}
        self.slot_last = {}
        self.lastw = {}
        self.readers = {}
        self.info = []
        self.waited = {e: {} for e in self.ENGS}
        self.barrier_vals = {}

    def _deps(self, reads, writes):
        d = set()
        for k in list(reads) + list(writes):
            if k in self.lastw:
                d.add(self.lastw[k])
        for k in writes:
            for r in self.readers.get(k, ()):
                d.add(r)
        return d

    def _commit(self, oid, reads, writes):
        for k in reads:
            self.readers.setdefault(k, []).append(oid)
        for k in writes:
            self.lastw[k] = oid
            self.readers[k] = []

    def barrier(self):
        for s, u in self.slot_uses.items():
            self.barrier_vals[("d", s)] = 16 * u
        for e in self.ENGS:
            if self.cnt[e]:
                self.barrier_vals[("c", e)] = self.cnt[e]

    def _waits_for(self, eng, deps):
        ws = {}
        for semkey, val in self.barrier_vals.items():
            if self.waited[eng].get(semkey, 0) < val:
                ws[semkey] = val
        for d in deps:
            semkey, val = self.info[d]
            if semkey == ("c", eng) and eng in self.NO_SELF_WAIT:
                continue
            if self.waited[eng].get(semkey, 0) >= val:
                continue
            ws[semkey] = max(ws.get(semkey, 0), val)
        for k, v in ws.items():
            self.waited[eng][k] = v
        return sorted(ws.items(), key=lambda kv: str(kv[0]))

    NO_SELF_WAIT = ("pe",)
    PSUM_KEYS = ("pA", "pB", "pC", "pD", "pT", "pO0", "pO1", "pF")

    def op(self, eng, fn, reads=(), writes=()):
        extra = [k for k in reads if k in self.PSUM_KEYS and k not in writes]
        if extra:
            writes = list(writes) + extra
        deps = self._deps(reads, writes)
        waits = self._waits_for(eng, deps)
        self.cnt[eng] += 1
        oid = len(self.ops)
        self.info.append((("c", eng), self.cnt[eng]))
        self.ops.append((eng, fn, waits, ("c", eng), 1))
        self._commit(oid, reads, writes)
        return oid

    def dma(self, eng, fn, reads=(), writes=()):
        slot = (eng, self.dma_n[eng] % NDMASLOT)
        self.dma_n[eng] += 1
        deps = self._deps(reads, writes)
        if self.slot_last.get(slot) is not None:
            deps.add(self.slot_last[slot])
        waits = self._waits_for(eng, deps)
        self.slot_uses[slot] = self.slot_uses.get(slot, 0) + 1
        oid = len(self.ops)
        self.info.append((("d", slot), 16 * self.slot_uses[slot]))
        self.ops.append((eng, fn, waits, ("d", slot), 16))
        self.slot_last[slot] = oid
        self._commit(oid, reads, writes)
        return oid

    def finish(self, final_eng="sp"):
        waits = []
        for s, u in self.slot_uses.items():
            waits.append((("d", s), 16 * u))
        for e in self.ENGS:
            if self.cnt[e]:
                waits.append((("c", e), self.cnt[e]))
        self.ops.append((final_eng, None, waits, None, 0))

    def emit(self, stack):
        nc = self.nc
        sems = {}
        for e in self.ENGS:
            sems[("c", e)] = stack.enter_context(nc.semaphore("c_" + e))
        for q in ("sp", "pool"):
            for s in range(NDMASLOT):
                sems[("d", (q, s))] = stack.enter_context(nc.semaphore("d_%s_%d" % (q, s)))
        block = stack.enter_context(nc.Block())
        engmap = {"pe": block.tensor, "act": block.scalar, "dve": block.vector,
                  "pool": block.gpsimd, "sp": block.sync}
        for e in self.ENGS:
            mine = [o for o in self.ops if o[0] == e]
            if not mine:
                continue

            def body(engine, mine=mine):
                for (_, fn, waits, sig, inc) in mine:
                    for (k, v) in waits:
                        engine.wait_ge(sems[k], v)
                    if fn is not None:
                        ins = fn(engine)
                        ins.then_inc(sems[sig], inc)
            engmap[e](body)


def rev_free(ap2d, n):
    return AP(ap2d.tensor, ap2d.offset + (n - 1), [[ap2d.ap[0][0], ap2d.ap[0][1]], [-1, n]])


class _Stop(Exception):
    pass


def build_program(stage=None):
    nc = bass.Bass("TRN2", target_bir_lowering=False)

    def din(name, shape):
        return nc.dram_tensor(name, shape, F32, kind="ExternalInput").ap()

    def dout(name, shape):
        return nc.dram_tensor(name, shape, F32, kind="ExternalOutput").ap()

    xp = din("xp", [2, LP, D])
    xs = din("xs", [LS, D])
    ck = din("ck", [2, 256, 128])
    cv = din("cv", [2, 256, 128])
    st0 = din("st0", [2, 2, 2, 32, 64])
    cvec = din("cvec", [2, D])
    norm_g = din("norm_g", [2, D])
    w_ada = din("w_ada", [2, D, 3 * D])
    b_ada = din("b_ada", [2, 3 * D])
    w_in = din("w_in", [2, D, NCOL])
    sink = din("attn_sink", [2, 8])
    conv_dw = din("conv_dw", [2, 31, 512])
    conv_db = din("conv_db", [2, 512])
    conv_ln_g = din("conv_ln_g", [2, 512])
    conv_ln_b = din("conv_ln_b", [2, 512])
    pool_w = din("pool_w", [2, 4, 128, 128])
    pool_scale = din("pool_scale", [2, 512])
    lam_re = din("ssm_lam_re", [2, 2, 32, 64])
    lam_im = din("ssm_lam_im", [2, 2, 32, 64])
    log_dt = din("ssm_log_dt", [2, 2, 32])
    b_re = din("ssm_b_re", [2, 2, 32, 64, 16])
    b_im = din("ssm_b_im", [2, 2, 32, 64, 16])
    c_re = din("ssm_c_re", [2, 2, 32, 16, 64])
    c_im = din("ssm_c_im", [2, 2, 32, 16, 64])
    ssm_d = din("ssm_d", [2, 512])
    glu_w = din("ssm_glu_w", [2, 512, 1024])
    w_br = din("w_br", [2, 4, 512, D])
    w_out = din("w_out", [2, D, D])
    final_g = din("final_g", [D])
    ropec = din("ropec", [128, LS])
    ropes = din("ropes", [128, LS])
    poolcorr = din("poolcorr", [1, 64])

    yp = dout("yp", [2, LP, D])
    ys = dout("ys", [LS, D])
    nk = dout("nk", [2, 2, LP, 128])
    nv = dout("nv", [2, 2, LP, 128])
    nst = dout("nst", [2, 2, 2, 2, 32, 64])
    dbg = dout("dbg", [128, 4, LS]) if stage is not None else None

    x1p = nc.dram_tensor("x1p", [2, LP, D], F32, kind="Internal").ap()
    x1s = nc.dram_tensor("x1s", [LS, D], F32, kind="Internal").ap()
    brS = nc.dram_tensor("brS", [4, 128, 4, LS], BF16, kind="Internal").ap()
    moddram = nc.dram_tensor("moddram", [2, 3 * D], F32, kind="Internal").ap()

    P = Prog(nc)
    st = ExitStack()

    def sb(name, shape, dt=F32):
        return st.enter_context(nc.sbuf_tensor(name, shape, dt))

    def ps(name, shape, dt=F32):
        return st.enter_context(nc.psum_tensor(name, shape, dt))

    def V(eng, method, reads, writes, *a, **kw):
        return P.op(eng, lambda e: getattr(e, method)(*a, **kw), reads=reads, writes=writes)

    def DMA(eng, out, in_, reads, writes, **kw):
        return P.dma(eng, lambda e: e.dma_start(out=out, in_=in_, **kw), reads=reads, writes=writes)

    def MM(out, lhsT, rhs, start, stop, reads, writes):
        return P.op("pe", lambda e: e.matmul(out, lhsT=lhsT, rhs=rhs, start=start, stop=stop),
                    reads=reads, writes=writes)

    identf = sb("identf", [128, 128])
    identb = sb("identb", [128, 128], BF16)
    onesf = sb("onesf", [128, 128])
    V("pool", "memset", [], ["identf"], identf[:], 1.0)
    V("pool", "affine_select", ["identf"], ["identf"], out=identf[:], in_=identf[:], pattern=[[-1, 128]],
      compare_op=ALU.is_equal, fill=0.0, base=0, channel_multiplier=1)
    V("dve", "tensor_copy", ["identf"], ["identb"], out=identb[:], in_=identf[:])
    V("pool", "memset", [], ["onesf"], onesf[:], 1.0)
    mprev = sb("mprev", [128, 128], BF16)
    mnext = sb("mnext", [128, 128], BF16)
    mtmp = sb("mtmp", [128, 128])
    V("pool", "memset", [], ["mtmp"], mtmp[:], 1.0)
    V("pool", "affine_select", ["mtmp"], ["mtmp"], out=mtmp[:], in_=mtmp[:], pattern=[[-1, 128]],
      compare_op=ALU.is_ge, fill=0.0, base=0, channel_multiplier=1)
    V("dve", "tensor_copy", ["mtmp"], ["mprev"], out=mprev[:], in_=mtmp[:])
    V("pool", "memset", ["mtmp"], ["mtmp"], mtmp[:], 1.0)
    V("pool", "affine_select", ["mtmp"], ["mtmp"], out=mtmp[:], in_=mtmp[:], pattern=[[1, 128]],
      compare_op=ALU.is_ge, fill=0.0, base=0, channel_multiplier=-1)
    V("dve", "tensor_copy", ["mtmp"], ["mnext"], out=mnext[:], in_=mtmp[:])
    bmask = sb("bmask", [128, 8])
    V("pool", "memset", [], ["bmask"], bmask[:], 1.0)
    V("pool", "affine_select", ["bmask"], ["bmask"], out=bmask[:], in_=bmask[:], pattern=[[-16, 8]],
      compare_op=ALU.is_ge, fill=0.0, base=0, channel_multiplier=1)
    V("pool", "affine_select", ["bmask"], ["bmask"], out=bmask[:], in_=bmask[:], pattern=[[16, 8]],
      compare_op=ALU.is_ge, fill=0.0, base=15, channel_multiplier=-1)

    pA = ps("pA", [128, 512])
    pB = ps("pB", [128, 512])
    pC = ps("pC", [128, 512])
    pD = ps("pD", [128, 512])
    pT = ps("pT", [128, 1024], BF16)
    pO0 = ps("pO0", [128, 512])
    pO1 = ps("pO1", [128, 512])
    pF = ps("pF", [128, 512])

    hT = sb("hT", [128, 8, LS], BF16)
    G1 = sb("G1", [128, 4, LS + 64], BF16)
    G2 = sb("G2", [128, 4, LS], BF16)
    G3 = sb("G3", [128, 4, LS], BF16)
    WT = sb("WT", [128, 8, 512], BF16)
    WT2 = sb("WT2", [128, 8, 512], BF16)
    XT = sb("XT", [128, 2, D])
    xtok = XT[:, 0, :]
    xtok2 = XT[:, 1, :]
    hb = sb("hb", [128, D], BF16)
    MODT = sb("MODT", [128, 3, D])
    SC = MODT[:, 0, :]
    SH = MODT[:, 1, :]
    GT = MODT[:, 2, :]
    tmpA = sb("tmpA", [128, 512])
    tmpB = sb("tmpB", [128, 512])
    tmpC = sb("tmpC", [128, 512], BF16)
    stat = sb("stat", [128, 8])
    T2s = sb("T2s", [128, 2048])
    JT = sb("JT", [128, 128])
    V("pool", "iota", [], ["JT"], JT[:], pattern=[[1, 128]], base=0, channel_multiplier=0,
      allow_small_or_imprecise_dtypes=True)
    csil = sb("csil", [128, 8, 2], BF16)
    cld = sb("cld", [128, 8, 2])
    ARB = 69632
    ARENA = sb("ARENA", [128, ARB // 2], BF16)

    class Carver:
        def __init__(self):
            self.off = 0

        def reset(self):
            self.off = 0

        def get(self, shape, dt=F32):
            n = 1
            for s_ in shape[1:]:
                n *= s_
            nb = n * (2 if dt == BF16 else 4)
            nb = (nb + 63) // 64 * 64
            assert self.off + nb <= ARB, ("arena overflow", self.off, nb)
            a = ARENA[0:shape[0], self.off // 2:(self.off + nb) // 2]
            self.off += nb
            if dt == BF16:
                a = a[:, 0:n]
            else:
                a = a.bitcast(dt)[:, 0:n]
            if len(shape) == 3:
                a = a.rearrange("p (a b) -> p a b", a=shape[1])
            elif len(shape) == 4:
                a = a.rearrange("p (a b c) -> p a b c", a=shape[1], b=shape[2])
            return a
    CV = Carver()

    for v_ in range(2):
        DMA("sp", cld[:, :, v_], cvec[v_].rearrange("(k p) -> p k", p=128), [], ["cld"],
            allow_slow_non_contiguous=True)
    V("act", "activation", ["cld"], ["csil"], out=csil[:], in_=cld[:], func=ACT.Silu)

    def row_bc(vec_ap, n):
        return vec_ap.rearrange("(o d) -> o d", o=1).partition_broadcast(n)

    def layer_mod(l):
        ngb = CV.get([128, D])
        for blk in range(6):
            Wm, Wmn = [(WT, "WT"), (WT2, "WT2")][blk % 2]
            DMA("pool", Wm[:], w_ada[l, :, blk * 512:(blk + 1) * 512].rearrange("(k p) c -> p k c", p=128),
                [], [Wmn])
            for k in range(8):
                MM(pA[0:2, :], csil[:, k, :], Wm[:, k, :], k == 0, k == 7, [Wmn, "csil"], ["pA"])
            DMA("sp", tmpA[0:2, :], row_bc(b_ada[l, blk * 512:(blk + 1) * 512], 2), [], ["tmpA"])
            V("dve", "tensor_add", ["tmpA", "pA"], ["tmpB"], out=tmpB[0:2, :], in0=pA[0:2, :], in1=tmpA[0:2, :])
            DMA("sp", moddram[:, blk * 512:(blk + 1) * 512], tmpB[0:2, :], ["tmpB"], ["moddram"])
        return ngb

    def set_mod(l, v, ngb):
        DMA("sp", ngb, row_bc(norm_g[l], 128), [], ["ngb"])
        DMA("sp", SH, row_bc(moddram[v, 0:D], 128), ["moddram"], ["SH"])
        DMA("sp", SC, row_bc(moddram[v, D:2 * D], 128), ["moddram"], ["SC"])
        DMA("sp", GT, row_bc(moddram[v, 2 * D:3 * D], 128), ["moddram"], ["GT"])
        V("dve", "scalar_tensor_tensor", ["SC", "ngb"], ["SC"], out=SC, in0=SC, scalar=1.0, in1=ngb,
          op0=ALU.add, op1=ALU.mult)

    def rms_stats(xb=None, xk="xtok"):
        if xb is None:
            xb = xtok
        V("act", "activation", [xk], ["xtok2", "stat"], out=xtok2, in_=xb, func=ACT.Square,
          accum_out=stat[:, 0:1])
        V("dve", "tensor_scalar", ["stat"], ["stat"], out=stat[:, 1:2], in0=stat[:, 0:1], scalar1=1.0 / D,
          scalar2=EPS, op0=ALU.mult, op1=ALU.add)
        V("act", "activation", ["stat"], ["stat"], out=stat[:, 3:4], in_=stat[:, 1:2], func=ACT.Sqrt)
        V("dve", "reciprocal", ["stat"], ["stat"], out=stat[:, 2:3], in_=stat[:, 3:4])

    def xbuf(t):
        return (xtok, "xtok") if t % 2 == 0 else (T2s[:, 0:D], "xB")

    def load_norm(src_rows, t=0):
        xb, xk = xbuf(t)
        DMA("sp", xb, src_rows, ["xsrc"], [xk])
        rms_stats(xb, xk)
        V("dve", "scalar_tensor_tensor", [xk, "stat", "SC"], ["xtok2"], out=xtok2, in0=xb,
          scalar=stat[:, 2:3], in1=SC, op0=ALU.mult, op1=ALU.mult)
        V("dve", "tensor_add", ["xtok2", "SH"], ["hb"], out=hb[:], in0=xtok2, in1=SH)

    def to_hT(t0):
        for k in range(8):
            P.op("pe", lambda e, k=k: e.transpose(pT[:, k * 128:(k + 1) * 128], hb[:, k * 128:(k + 1) * 128], identb[:]),
                 reads=["hb", "identb"], writes=["pT"])
        V("act", "activation", ["pT"], ["hT"], out=hT[:, :, t0:t0 + 128],
          in_=pT[:].rearrange("p (k t) -> p k t", k=8), func=ACT.Copy)

    def load_w(dst, dname, l, col0, ncols, dcol0=0):
        DMA("pool", dst[:, :, dcol0:dcol0 + ncols],
            w_in[l, :, col0:col0 + ncols].rearrange("(k p) c -> p k c", p=128), [], [dname])

    def load_w_rot(dst, dname, l, col0, nheads, dcol0=0):
        for k in range(8):
            for two in range(2):
                src = w_in[l, k * 128:(k + 1) * 128, col0:col0 + nheads * 64].rearrange(
                    "p (h two j) -> p h two j", two=2, j=32)[:, :, 1 - two, :]
                dd = dst[:, k, dcol0:dcol0 + nheads * 64].rearrange("p (h two j) -> p h two j", two=2, j=32)[:, :, two, :]
                DMA("pool", dd, src, [], [dname])

    psum_rr = [0]

    def proj(wt, wname, ncols, L, consume, ps_list=None, col_off=0):
        if ps_list is None:
            ps_list = [(pA, "pA"), (pB, "pB"), (pC, "pC"), (pD, "pD")]
        T = min(512, L)
        for t0 in range(0, L, T):
            for j in range(ncols // 128):
                pt_, pn = ps_list[psum_rr[0] % len(ps_list)]
                psum_rr[0] += 1
                for k in range(8):
                    MM(pt_[:, 0:T], wt[:, k, col_off + j * 128:col_off + (j + 1) * 128], hT[:, k, t0:t0 + T],
                       k == 0, k == 7, [wname, "hT"], [pn])
                consume(j, t0, T, pt_[:, 0:T], pn)

    nstage = [0]

    def chk(label):
        if stage == label:
            raise _Stop()

    def stage_point(L):
        nstage[0] += 1
        if stage is not None and nstage[0] == stage:
            DMA("pool", dbg[:, :, 0:L], G3[:, :, 0:L], ["G3", "G1", "G2"], [])
            raise _Stop()

    def seq_pass(l, kind, nseq, Ls, x_srcs, x_dsts, sidx0, ngb):
        L = nseq * Ls

        def xrow(aps, t):
            s_ = (t * 128) // Ls
            lo = t * 128 - s_ * Ls
            return aps[s_][lo:lo + 128, :]

        def seq_split(t0, T_):
            out = []
            a = t0
            while a < t0 + T_:
                s_ = a // Ls
                e_ = min(t0 + T_, (s_ + 1) * Ls)
                out.append((s_, a - s_ * Ls, a, e_ - a))
                a = e_
            return out
        SEG = Ls + 32
        SEGP = Ls + 16
        Tc = min(512, Ls)
        nt = L // 128
        T = min(512, L)
        for t in range(nt):
            load_norm(xrow(x_srcs, t), t)
            to_hT(t * 128)

        chk("ph1")
        P.barrier()
        CV.reset()
        vaug = CV.get([128, 16, 2, 65], BF16)
        ETall = CV.get([128, 5, 512], BF16)
        obf = CV.get([128, 8, 64], BF16)
        kvtok = CV.get([128, 256])
        esink = CV.get([128, 8])
        den = CV.get([128, 8])
        V("dve", "memset", [], ["vaug"], vaug[:], 1.0)
        DMA("sp", esink, row_bc(sink[l], 128), [], ["esink"])
        V("act", "activation", ["esink"], ["esink"], out=esink, in_=esink, func=ACT.Exp)
        V("dve", "memset", [], ["G2"], G2[:], 0.0)
        if kind == "s":
            ROC = CV.get([128, LS])
            ROS = CV.get([128, LS])
            CKv = CV.get([128, 4, 256], BF16)
            cvaug = CV.get([128, 2, 2, 65], BF16)
            cktok = CV.get([128, 2, 2, 128])
            DMA("sp", ROC, ropec, [], ["ROC"])
            DMA("sp", ROS, ropes, [], ["ROS"])
            V("pool", "memset", [], ["CKv"], CKv[:], 0.0)
            V("pool", "memset", [], ["cvaug"], cvaug[:], 1.0)
            for stl in range(2):
                DMA("sp", cktok[:, 0, stl, :], ck[l, stl * 128:(stl + 1) * 128, :], [], ["cktok"])
                DMA("sp", cktok[:, 1, stl, 0:64], ck[l, stl * 128:(stl + 1) * 128, 64:128], [], ["cktok"])
                DMA("sp", cktok[:, 1, stl, 64:128], ck[l, stl * 128:(stl + 1) * 128, 0:64], [], ["cktok"])
                DMA("pool", cvaug[:, stl, :, 0:64],
                    cv[l, stl * 128:(stl + 1) * 128, :].rearrange("s (k d) -> s k d", k=2), ["cvaug"], ["cvaug"])
            for stl in range(2):
                for var in range(2):
                    P.op("pe", lambda e, stl=stl, var=var: e.transpose(pF[:, 0:128], cktok[:, var, stl, :], identf[:]),
                         reads=["cktok", "identf"], writes=["pF"])
                    if var == 0:
                        V("dve", "tensor_copy", ["pF"], ["CKv"], out=CKv[0:64, 0, stl * 128:(stl + 1) * 128], in_=pF[0:64, 0:128])
                        V("dve", "tensor_copy", ["pF"], ["CKv"], out=CKv[64:128, 3, stl * 128:(stl + 1) * 128], in_=pF[64:128, 0:128])
                    else:
                        V("dve", "tensor_copy", ["pF"], ["CKv"], out=CKv[0:64, 2, stl * 128:(stl + 1) * 128], in_=pF[0:64, 0:128])
                        V("dve", "tensor_copy", ["pF"], ["CKv"], out=CKv[64:128, 1, stl * 128:(stl + 1) * 128], in_=pF[64:128, 0:128])

        chk("attsetup")
        load_w(WT, "WT", l, C_Q, 512)
        if kind == "p":
            def cq(j, t0, T_, p_, pn):
                V("act", "activation", [pn], ["G1"], out=G1[:, j, t0:t0 + T_], in_=p_, func=ACT.Copy)
            proj(WT, "WT", 512, L, cq)
        else:
            load_w_rot(WT2, "WT2", l, C_Q, 8)
            for t0 in range(0, L, T):
                for j in range(4):
                    for k in range(8):
                        MM(pA[:, 0:T], WT[:, k, j * 128:(j + 1) * 128], hT[:, k, t0:t0 + T], k == 0, k == 7, ["WT", "hT"], ["pA"])
                    for k in range(8):
                        MM(pB[:, 0:T], WT2[:, k, j * 128:(j + 1) * 128], hT[:, k, t0:t0 + T], k == 0, k == 7, ["WT2", "hT"], ["pB"])
                    V("dve", "tensor_tensor", ["pA", "ROC"], ["tmpA"], out=tmpA[:, 0:T], in0=pA[:, 0:T], in1=ROC[:, t0:t0 + T], op=ALU.mult)
                    V("dve", "tensor_tensor", ["pB", "ROS"], ["tmpB"], out=tmpB[:, 0:T], in0=pB[:, 0:T], in1=ROS[:, t0:t0 + T], op=ALU.mult)
                    V("dve", "tensor_tensor", ["tmpA", "tmpB"], ["G1"], out=G1[:, j, t0:t0 + T], in0=tmpA[:, 0:T], in1=tmpB[:, 0:T], op=ALU.add)
        chk("q")
        load_w(WT, "WT", l, C_K, 128, 0)
        load_w(WT, "WT", l, C_K + 64, 64, 128)
        load_w(WT, "WT", l, C_K, 64, 192)
        if kind == "s":
            load_w_rot(WT2, "WT2", l, C_K, 2, 0)
            load_w_rot(WT2, "WT2", l, C_K + 64, 1, 128)
            load_w_rot(WT2, "WT2", l, C_K, 1, 192)
        for t0 in range(0, L, T):
            for var in range(2):
                for k in range(8):
                    MM(pA[:, 0:T], WT[:, k, var * 128:(var + 1) * 128], hT[:, k, t0:t0 + T], k == 0, k == 7, ["WT", "hT"], ["pA"])
                if kind == "s":
                    for k in range(8):
                        MM(pB[:, 0:T], WT2[:, k, var * 128:(var + 1) * 128], hT[:, k, t0:t0 + T], k == 0, k == 7, ["WT2", "hT"], ["pB"])
                    V("dve", "tensor_tensor", ["pA", "ROC"], ["tmpA"], out=tmpA[:, 0:T], in0=pA[:, 0:T], in1=ROC[:, t0:t0 + T], op=ALU.mult)
                    V("dve", "tensor_tensor", ["pB", "ROS"], ["tmpB"], out=tmpB[:, 0:T], in0=pB[:, 0:T], in1=ROS[:, t0:t0 + T], op=ALU.mult)
                    V("dve", "tensor_tensor", ["tmpA", "tmpB"], ["tmpA"], out=tmpA[:, 0:T], in0=tmpA[:, 0:T], in1=tmpB[:, 0:T], op=ALU.add)
                    src, sn = tmpA, "tmpA"
                else:
                    src, sn = pA, "pA"
                lo_slot, hi_slot = (0, 3) if var == 0 else (2, 1)
                V("dve", "tensor_copy", [sn], ["G2"], out=G2[0:64, lo_slot, t0:t0 + T], in_=src[0:64, 0:T])
                V("dve", "tensor_copy", [sn], ["G2"], out=G2[64:128, hi_slot, t0:t0 + T], in_=src[64:128, 0:T])
        chk("kfm")
        load_w(WT2, "WT2", l, C_K, 256)
        for t in range(nt):
            for k in range(8):
                MM(pB[:, 0:256], hT[:, k, t * 128:(t + 1) * 128], WT2[:, k, 0:256], k == 0, k == 7, ["hT", "WT2"], ["pB"])
            V("dve", "tensor_copy", ["pB"], ["vaug"], out=vaug[:, t, :, 0:64],
              in_=pB[:, 128:256].rearrange("p (k d) -> p k d", k=2))
            if kind == "p":
                V("dve", "tensor_copy", ["pB"], ["kvtok"], out=kvtok, in_=pB[:, 0:256])
                s_ = (t * 128) // Ls
                lo_ = t * 128 - s_ * Ls
                DMA("sp", nk[sidx0 + s_, l, lo_:lo_ + 128, :], kvtok[:, 0:128], ["kvtok"], [])
                DMA("sp", nv[sidx0 + s_, l, lo_:lo_ + 128, :], kvtok[:, 128:256], ["kvtok"], [])
        chk("kvtok")
        load_w(WT, "WT", l, C_GATT, 512)

        def cg(j, t0, T_, p_, pn):
            V("act", "activation", [pn], ["G3"], out=G3[:, j, t0:t0 + T_], in_=p_, func=ACT.Silu)
        proj(WT, "WT", 512, L, cg)

        chk("gate")
        rr = [0]
        for qb in range(nt):
            q0 = qb * 128
            chunks = []
            if kind == "p":
                s_ = q0 // Ls
                for c in range(s_ * Ls // 128, (s_ + 1) * Ls // 128):
                    chunks.append(("G2", c * 128, ("v", c), None))
            else:
                for c in range(2):
                    chunks.append(("CK", c * 128, ("c", c), None))
                if qb > 0:
                    chunks.append(("G2", (qb - 1) * 128, ("v", qb - 1), "prev"))
                chunks.append(("G2", qb * 128, ("v", qb), None))
                if qb < nt - 1:
                    chunks.append(("G2", (qb + 1) * 128, ("v", qb + 1), "next"))
            for kv in range(2):
                pO, pOn = (pO0, "pO0") if kv == 0 else (pO1, "pO1")
                for ci, (ksrc, koff, vsel, msk) in enumerate(chunks):
                    pS, pSn = [(pC, "pC"), (pD, "pD")][rr[0] % 2]
                    et = ETall[:, ci, :]
                    etn = "ET%d" % ci
                    rr[0] += 1
                    for slot in range(4):
                        h = 4 * kv + slot
                        j, half = h // 2, h % 2
                        if ksrc == "G2":
                            lhs = G2[:, 2 * kv + half, koff:koff + 128]
                            lname = "G2"
                        else:
                            lhs = CKv[:, 2 * kv + half, koff:koff + 128]
                            lname = "CKv"
                        MM(pS[:, slot * 128:(slot + 1) * 128], lhs, G1[:, j, q0:q0 + 128], True, True,
                           [lname, "G1"], [pSn])
                    V("act", "activation", [pSn], [etn], out=et, in_=pS[:], func=ACT.Exp, scale=0.125)
                    if msk is not None:
                        m_ = mprev if msk == "prev" else mnext
                        mb = AP(m_[:].tensor, m_[:].offset, [[m_[:].ap[0][0], 128], [0, 4], [1, 128]])
                        V("dve", "tensor_tensor", [etn, "mprev", "mnext"], [etn],
                          out=et.rearrange("p (s q) -> p s q", s=4), in0=et.rearrange("p (s q) -> p s q", s=4),
                          in1=mb, op=ALU.mult)
                for slot in range(4):
                    for ci, (ksrc, koff, vsel, msk) in enumerate(chunks):
                        if vsel[0] == "v":
                            rhs = vaug[:, vsel[1], kv, :]
                            rn = "vaug"
                        else:
                            rhs = cvaug[:, vsel[1], kv, :]
                            rn = "cvaug"
                        MM(pO[:, slot * 65:(slot + 1) * 65], ETall[:, ci, slot * 128:(slot + 1) * 128], rhs,
                           ci == 0, ci == len(chunks) - 1, ["ET%d" % ci, rn], [pOn])
                pOv = pO[:, 0:260].rearrange("p (s e) -> p s e", s=4)
                V("dve", "tensor_tensor", [pOn, "esink"], ["den"], out=den[:, 4 * kv:4 * kv + 4],
                  in0=pOv[:, :, 64], in1=esink[:, 4 * kv:4 * kv + 4], op=ALU.add)
                V("dve", "reciprocal", ["den"], ["den"], out=den[:, 4 * kv:4 * kv + 4], in_=den[:, 4 * kv:4 * kv + 4])
                dsl = den[:, 4 * kv:4 * kv + 4]
                db = AP(dsl.tensor, dsl.offset, [[dsl.ap[0][0], 128], [1, 4], [0, 64]])
                V("dve", "tensor_tensor", [pOn, "den"], ["obf"], out=obf[:, 4 * kv:4 * kv + 4, :],
                  in0=pOv[:, :, 0:64], in1=db, op=ALU.mult)
            for j in range(4):
                P.op("pe", lambda e, j=j: e.transpose(pT[:, j * 128:(j + 1) * 128],
                                                      obf[:, 2 * j:2 * j + 2, :].rearrange("p a b -> p (a b)"), identb[:]),
                     reads=["obf", "identb"], writes=["pT"])
            V("dve", "tensor_tensor", ["pT", "G3"], ["G3"], out=G3[:, :, q0:q0 + 128], in0=G3[:, :, q0:q0 + 128],
              in1=pT[:, 0:512].rearrange("p (j t) -> p j t", j=4), op=ALU.mult)
        DMA("sp", brS[0, :, :, 0:L], G3[:, :, 0:L], ["G3"], ["brS"])
        stage_point(L)

        P.barrier()
        CV.reset()
        DG = CV.get([128, 4, 31, 128], BF16)
        YC = CV.get([128, 4, 512])
        YQ = CV.get([128, 4, 512])
        cw = CV.get([128, 4, 31])
        cpar = CV.get([128, 3, 4])
        for f in range(4):
            DMA("sp", cw[:, f, :], conv_dw[l, :, f * 128:(f + 1) * 128].rearrange("k p -> p k"), [], ["cw"],
                allow_slow_non_contiguous=True)
        for i_, src_ in enumerate((conv_db, conv_ln_g, conv_ln_b)):
            DMA("sp", cpar[:, i_, :], src_[l].rearrange("(f p) -> p f", p=128), [], ["cpar"], allow_slow_non_contiguous=True)
        for f in range(4):
            for k in range(31):
                V("dve", "tensor_scalar", ["identf", "cw"], ["DG"], out=DG[:, f, k, :], in0=identf[:],
                  scalar1=cw[:, f, k:k + 1], scalar2=None, op0=ALU.mult)
        V("dve", "memset", [], ["G1"], G1[:], 0.0)
        load_w(WT, "WT", l, C_ACONV + 512, 512)

        def cs(j, t0, T_, p_, pn):
            V("act", "activation", [pn], ["G2"], out=G2[:, j, t0:t0 + T_], in_=p_, func=ACT.Sigmoid)
        proj(WT, "WT", 512, L, cs)
        load_w(WT2, "WT2", l, C_ACONV, 512)

        def cx(j, t0, T_, p_, pn):
            for (s_, lo_, g0_, n_) in seq_split(t0, T_):
                V("dve", "tensor_tensor", [pn, "G2"], ["G1"], out=G1[:, j, s_ * SEG + 16 + lo_:s_ * SEG + 16 + lo_ + n_],
                  in0=p_[:, g0_ - t0:g0_ - t0 + n_], in1=G2[:, j, g0_:g0_ + n_], op=ALU.mult)
        proj(WT2, "WT2", 512, L, cx)
        load_w(WT, "WT", l, C_GCONV, 512)
        proj(WT, "WT", 512, L, cg)
        for (s_, tl0) in [(s__, a__) for s__ in range(nseq) for a__ in range(0, Ls, Tc)]:
            t0 = s_ * Ls + tl0
            cb = s_ * SEG + tl0
            T = Tc
            for f in range(4):
                for k in range(31):
                    MM(pA[:, 0:T], DG[:, f, k, :], G1[:, f, cb + k + 1:cb + k + 1 + T], k == 0, k == 30, ["DG", "G1"], ["pA"])
                V("act", "activation", ["pA", "cpar"], ["YC"], out=YC[:, f, 0:T], in_=pA[:, 0:T], func=ACT.Identity,
                  bias=cpar[:, 0, f:f + 1])
                V("dve", "tensor_tensor", ["YC"], ["YQ"], out=YQ[:, f, 0:T], in0=YC[:, f, 0:T], in1=YC[:, f, 0:T], op=ALU.mult)
            for f in range(4):
                MM(pB[:, 0:T], onesf[:], YC[:, f, 0:T], f == 0, f == 3, ["onesf", "YC"], ["pB"])
            for f in range(4):
                MM(pC[:, 0:T], onesf[:], YQ[:, f, 0:T], f == 0, f == 3, ["onesf", "YQ"], ["pC"])
            V("dve", "tensor_scalar", ["pB"], ["tmpA"], out=tmpA[:, 0:T], in0=pB[:, 0:T], scalar1=1.0 / 512, scalar2=None, op0=ALU.mult)
            V("dve", "tensor_tensor", ["tmpA"], ["tmpB"], out=tmpB[:, 0:T], in0=tmpA[:, 0:T], in1=tmpA[:, 0:T], op=ALU.mult)
            V("dve", "scalar_tensor_tensor", ["pC", "tmpB"], ["tmpB"], out=tmpB[:, 0:T], in0=pC[:, 0:T], scalar=1.0 / 512,
              in1=tmpB[:, 0:T], op0=ALU.mult, op1=ALU.subtract)
            V("dve", "tensor_scalar", ["tmpB"], ["tmpB"], out=tmpB[:, 0:T], in0=tmpB[:, 0:T], scalar1=EPS, scalar2=None, op0=ALU.add)
            V("act", "activation", ["tmpB"], ["tmpB"], out=tmpB[:, 0:T], in_=tmpB[:, 0:T], func=ACT.Sqrt)
            V("dve", "reciprocal", ["tmpB"], ["tmpB"], out=tmpB[:, 0:T], in_=tmpB[:, 0:T])
            for f in range(4):
                V("dve", "tensor_tensor", ["YC", "tmpA"], ["YC"], out=YC[:, f, 0:T], in0=YC[:, f, 0:T], in1=tmpA[:, 0:T], op=ALU.subtract)
                V("dve", "tensor_tensor", ["YC", "tmpB"], ["YC"], out=YC[:, f, 0:T], in0=YC[:, f, 0:T], in1=tmpB[:, 0:T], op=ALU.mult)
                V("act", "activation", ["YC", "cpar"], ["YQ"], out=YQ[:, f, 0:T], in_=YC[:, f, 0:T], func=ACT.Silu,
                  scale=cpar[:, 1, f:f + 1], bias=cpar[:, 2, f:f + 1])
                V("dve", "tensor_tensor", ["YQ", "G3"], ["G3"], out=G3[:, f, t0:t0 + T], in0=G3[:, f, t0:t0 + T],
                  in1=YQ[:, f, 0:T], op=ALU.mult)
        T = min(512, L)
        DMA("sp", brS[1, :, :, 0:L], G3[:, :, 0:L], ["G3"], ["brS"])
        stage_point(L)

        P.barrier()
        CV.reset()
        PW = CV.get([128, 4, 128], BF16)
        pcorr = CV.get([128, 4, 16])
        pscale = CV.get([128, 4])
        DMA("pool", PW, pool_w[l].rearrange("g c d -> c g d"), [], ["PW"])
        DMA("sp", pcorr.rearrange("p a b -> p (a b)"), poolcorr.partition_broadcast(128), [], ["pcorr"])
        DMA("sp", pscale, pool_scale[l].rearrange("(f p) -> p f", p=128), [], ["pscale"], allow_slow_non_contiguous=True)
        V("dve", "memset", [], ["G1"], G1[:], 0.0)
        load_w(WT, "WT", l, C_XPOOL, 512)

        def cp_(j, t0, T_, p_, pn):
            for (s_, lo_, g0_, n_) in seq_split(t0, T_):
                V("act", "activation", [pn], ["G1"], out=G1[:, j, s_ * SEGP + 8 + lo_:s_ * SEGP + 8 + lo_ + n_],
                  in_=p_[:, g0_ - t0:g0_ - t0 + n_], func=ACT.Copy)
        proj(WT, "WT", 512, L, cp_)
        load_w(WT2, "WT2", l, C_GPOOL, 512)
        proj(WT2, "WT2", 512, L, cg)
        for (s_, tl0) in [(s__, a__) for s__ in range(nseq) for a__ in range(0, Ls, Tc)]:
            t0 = s_ * Ls + tl0
            cb = s_ * SEGP + 8 + tl0
            T = Tc
            for f, w in enumerate((2, 4, 8, 16)):
                for i_, kk in enumerate(range(-(w // 2), w // 2)):
                    MM(pA[:, 0:T], identb[:], G1[:, f, cb + kk:cb + kk + T], i_ == 0, i_ == w - 1, ["identb", "G1"], ["pA"])
                V("dve", "tensor_scalar", ["pA"], ["tmpA"], out=tmpA[:, 0:T], in0=pA[:, 0:T], scalar1=1.0 / w, scalar2=None, op0=ALU.mult)
                if tl0 == 0:
                    V("dve", "tensor_tensor", ["tmpA", "pcorr"], ["tmpA"], out=tmpA[:, 0:8], in0=tmpA[:, 0:8], in1=pcorr[:, f, 0:8], op=ALU.mult)
                if tl0 + T == Ls:
                    V("dve", "tensor_tensor", ["tmpA", "pcorr"], ["tmpA"], out=tmpA[:, T - 8:T], in0=tmpA[:, T - 8:T], in1=pcorr[:, f, 8:16], op=ALU.mult)
                V("dve", "tensor_tensor", ["tmpA", "G1"], ["tmpC"], out=tmpC[:, 0:T], in0=tmpA[:, 0:T],
                  in1=G1[:, f, cb:cb + T], op=ALU.subtract)
                MM(pB[:, 0:T], PW[:, f, :], tmpC[:, 0:T], True, True, ["PW", "tmpC"], ["pB"])
                V("dve", "scalar_tensor_tensor", ["pB", "pscale", "G3"], ["G3"], out=G3[:, f, t0:t0 + T], in0=pB[:, 0:T],
                  scalar=pscale[:, f:f + 1], in1=G3[:, f, t0:t0 + T], op0=ALU.mult, op1=ALU.mult)
        T = min(512, L)
        DMA("sp", brS[2, :, :, 0:L], G3[:, :, 0:L], ["G3"], ["brS"])
        stage_point(L)

        P.barrier()
        CV.reset()
        ssm_phase(l, kind, nseq, Ls, sidx0)
        DMA("sp", brS[3, :, :, 0:L], G3[:, :, 0:L], ["G3"], ["brS"])
        stage_point(L)

        P.barrier()
        CV.reset()
        if L <= 512:
            BR = CV.get([128, 16, 512], BF16)
            WBN = [CV.get([128, 4, 1024], BF16), CV.get([128, 4, 1024], BF16)]
            ACC = CV.get([128, 8, 512])
            WO = CV.get([128, 8, 1024], BF16)
            FG = CV.get([128, D])
            for n in range(4):
                DMA("sp", BR[:, n * 4:(n + 1) * 4, 0:L], brS[n, :, :, 0:L], ["brS"], ["BR0"])
            load_w(WT, "WT", l, C_MRG, 512, 0)
            load_w(WT2, "WT2", l, C_MRG + 512, 512, 0)
            DMA("pool", WBN[0], w_br[l, 0].rearrange("(kc p) c -> p kc c", p=128), [], ["WBN0"])
            for half in range(2):
                DMA("pool", WO[:, :, half * 512:(half + 1) * 512],
                    w_out[l, :, half * 512:(half + 1) * 512].rearrange("(k p) c -> p k c", p=128), [], ["WO"])
            if l == 1:
                DMA("sp", FG, row_bc(final_g, 128), [], ["FG"])
            for n in range(4):
                WBn_, WBk = WBN[n % 2], "WBN%d" % (n % 2)
                for f in range(8):
                    Wg, Wgn = (WT, "WT") if f < 4 else (WT2, "WT2")
                    col = (f % 4) * 128
                    pg, pgn = [(pA, "pA"), (pC, "pC")][f % 2]
                    pp, ppn = [(pB, "pB"), (pD, "pD")][f % 2]
                    sg = T2s[:, (f % 2) * 512:(f % 2) * 512 + 512]
                    sgn = "sg%d" % (f % 2)
                    for k in range(8):
                        MM(pg[:, 0:L], Wg[:, k, col:col + 128], hT[:, k, 0:L], k == 0, k == 7, [Wgn, "hT"], [pgn])
                    V("act", "activation", [pgn], [sgn], out=sg[:, 0:L], in_=pg[:, 0:L], func=ACT.Sigmoid)
                    for kc in range(4):
                        MM(pp[:, 0:L], WBn_[:, kc, f * 128:(f + 1) * 128], BR[:, n * 4 + kc, 0:L], kc == 0, kc == 3, [WBk, "BR0"], [ppn])
                    ak = "ACC%d" % f
                    if n == 0:
                        V("dve", "tensor_tensor", [sgn, ppn], [ak], out=ACC[:, f, 0:L], in0=sg[:, 0:L], in1=pp[:, 0:L], op=ALU.mult)
                    else:
                        V("dve", "tensor_tensor", [sgn, ppn], [sgn], out=sg[:, 0:L], in0=sg[:, 0:L], in1=pp[:, 0:L], op=ALU.mult)
                        V("dve", "tensor_tensor", [sgn, ak], [ak], out=ACC[:, f, 0:L], in0=sg[:, 0:L], in1=ACC[:, f, 0:L], op=ALU.add)
                    if n == 3:
                        if f < 4:
                            V("act", "activation", [ak], ["G1"], out=G1[:, f, 0:L], in_=ACC[:, f, 0:L], func=ACT.Copy)
                        else:
                            V("act", "activation", [ak], ["G2"], out=G2[:, f - 4, 0:L], in_=ACC[:, f, 0:L], func=ACT.Copy)
                    if n < 3 and f == 3:
                        load_w(WT, "WT", l, C_MRG + (n + 1) * 1024, 512, 0)
                        DMA("pool", WBN[(n + 1) % 2], w_br[l, n + 1].rearrange("(kc p) c -> p kc c", p=128), [], ["WBN%d" % ((n + 1) % 2)])
                    if n < 3 and f == 7:
                        load_w(WT2, "WT2", l, C_MRG + (n + 1) * 1024 + 512, 512, 0)
        else:
            BRs = [CV.get([128, 16, 512], BF16), CV.get([128, 16, 512], BF16)]
            WBs = [CV.get([128, 16, 128], BF16), CV.get([128, 16, 128], BF16)]
            WO = CV.get([128, 8, 1024], BF16)
            FG = CV.get([128, D])
            for half in range(2):
                DMA("pool", WO[:, :, half * 512:(half + 1) * 512],
                    w_out[l, :, half * 512:(half + 1) * 512].rearrange("(k p) c -> p k c", p=128), [], ["WO"])
            if l == 1:
                DMA("sp", FG, row_bc(final_g, 128), [], ["FG"])
            brr = [0]
            for f in range(8):
                WTf, WTn = [(WT, "WT"), (WT2, "WT2")][f % 2]
                WB, WBn = WBs[f % 2], "WB%d" % (f % 2)
                for n in range(4):
                    load_w(WTf, WTn, l, C_MRG + n * 1024 + f * 128, 128, n * 128)
                    DMA("pool", WB[:, n * 4:(n + 1) * 4, :],
                        w_br[l, n, :, f * 128:(f + 1) * 128].rearrange("(kc p) c -> p kc c", p=128), [], [WBn])
                for t0 in range(0, L, T):
                    BR, BRn = BRs[brr[0] % 2], "BR%d" % (brr[0] % 2)
                    brr[0] += 1
                    for n in range(4):
                        DMA("sp", BR[:, n * 4:(n + 1) * 4, 0:T], brS[n, :, :, t0:t0 + T], ["brS"], [BRn])
                    for n in range(4):
                        pg, pgn = [(pA, "pA"), (pC, "pC")][n % 2]
                        pp, ppn = [(pB, "pB"), (pD, "pD")][n % 2]
                        sg = T2s[:, (n % 2) * 512:(n % 2) * 512 + 512]
                        sgn = "sg%d" % (n % 2)
                        for k in range(8):
                            MM(pg[:, 0:T], WTf[:, k, n * 128:(n + 1) * 128], hT[:, k, t0:t0 + T], k == 0, k == 7, [WTn, "hT"], [pgn])
                        V("act", "activation", [pgn], [sgn], out=sg[:, 0:T], in_=pg[:, 0:T], func=ACT.Sigmoid)
                        for kc in range(4):
                            MM(pp[:, 0:T], WB[:, n * 4 + kc, :], BR[:, n * 4 + kc, 0:T], kc == 0, kc == 3, [WBn, BRn], [ppn])
                        if n == 0:
                            V("dve", "tensor_tensor", [sgn, ppn], ["tmpB"], out=tmpB[:, 0:T], in0=sg[:, 0:T], in1=pp[:, 0:T], op=ALU.mult)
                        else:
                            V("dve", "tensor_tensor", [sgn, ppn], [sgn], out=sg[:, 0:T], in0=sg[:, 0:T], in1=pp[:, 0:T], op=ALU.mult)
                            V("dve", "tensor_tensor", [sgn, "tmpB"], ["tmpB"], out=tmpB[:, 0:T], in0=sg[:, 0:T], in1=tmpB[:, 0:T], op=ALU.add)
                    if f < 4:
                        V("act", "activation", ["tmpB"], ["G1"], out=G1[:, f, t0:t0 + T], in_=tmpB[:, 0:T], func=ACT.Copy)
                    else:
                        V("act", "activation", ["tmpB"], ["G2"], out=G2[:, f - 4, t0:t0 + T], in_=tmpB[:, 0:T], func=ACT.Copy)


        V("dve", "tensor_copy", ["G1"], ["G3"], out=G3[:, :, 0:L], in_=G1[:, :, 0:L]) if stage is not None else None
        stage_point(L)
        P.barrier()
        for t in range(nt):
            xb, xk = xbuf(t)
            DMA("sp", xb, xrow(x_srcs, t), ["xsrc"], [xk])
            for half in range(2):
                po_, pon = [(pA, "pA"), (pB, "pB")][half]
                for k in range(8):
                    mg = G1[:, k, t * 128:(t + 1) * 128] if k < 4 else G2[:, k - 4, t * 128:(t + 1) * 128]
                    MM(po_[:], mg, WO[:, k, half * 512:(half + 1) * 512], k == 0, k == 7, ["G1", "G2", "WO"], [pon])
                V("dve", "tensor_tensor", [pon, "GT"], ["tmpA"], out=tmpA[:], in0=po_[:], in1=GT[:, half * 512:(half + 1) * 512], op=ALU.mult)
                V("dve", "tensor_tensor", ["tmpA", xk], [xk], out=xb[:, half * 512:(half + 1) * 512], in0=tmpA[:],
                  in1=xb[:, half * 512:(half + 1) * 512], op=ALU.add)
            if l == 0:
                DMA("sp", xrow(x_dsts, t), xb, [xk], ["xdst"])
            else:
                rms_stats(xb, xk)
                V("dve", "scalar_tensor_tensor", [xk, "stat", "FG"], ["xtok2"], out=xtok2, in0=xb,
                  scalar=stat[:, 2:3], in1=FG, op0=ALU.mult, op1=ALU.mult)
                DMA("sp", xrow(x_dsts, t), xtok2, ["xtok2"], ["xdst"])
        P.barrier()

    def ssm_phase(l, kind, nseq, Ls, sidx0):
        L = nseq * Ls
        T = min(512, L)
        BTr = CV.get([128, 32, 128], BF16)
        BTi = CV.get([128, 32, 128], BF16)
        CPr = CV.get([128, 32, 128], BF16)
        CPi = CV.get([128, 32, 128], BF16)
        GW = CV.get([128, 4, 1024], BF16)
        YG = xtok2.bitcast(BF16).rearrange("p (a b) -> p a b", a=4)
        S = {}
        for nm in ("LR", "LI", "DT", "a", "th", "er", "cs", "sn", "lr", "li", "kr", "ki", "t1", "t2", "t3", "t4",
                   "cr", "ci", "q1", "q2", "q3", "q4"):
            S[nm] = CV.get([128, 32])
        ti = CV.get([128, 32], I32)
        RREG = CV.get([128, 3, 2048])
        rflat = RREG.rearrange("p a b -> p (a b)")
        Bt = [rflat[:, i * 512:(i + 1) * 512].rearrange("p (g c) -> p g c", g=32) for i in range(2)]
        Be = [rflat[:, (2 + i) * 512:(3 + i) * 512].rearrange("p (g c) -> p g c", g=32) for i in range(2)]
        Bx = rflat[:, 4 * 512:5 * 512].rearrange("p (g c) -> p g c", g=32)
        CN = [rflat[:, (5 + i) * 512:(6 + i) * 512].rearrange("p (f c) -> p f c", f=4) for i in range(2)]
        dsk = CV.get([128, 4])

        def tt(out, a, b, op, eng="dve"):
            V(eng, "tensor_tensor", ["ssmS"], ["ssmS"], out=out, in0=a, in1=b, op=op)

        for d in range(2):
            DMA("sp", S["LR"][d * 64:(d + 1) * 64, :], lam_re[l, d].rearrange("g p -> p g"), [], ["ssmS"], allow_slow_non_contiguous=True)
            DMA("sp", S["LI"][d * 64:(d + 1) * 64, :], lam_im[l, d].rearrange("g p -> p g"), [], ["ssmS"], allow_slow_non_contiguous=True)
            DMA("sp", S["DT"][d * 64:(d + 1) * 64, :], row_bc(log_dt[l, d], 64), [], ["ssmS"])
            for ri, src_ in enumerate((b_re, b_im)):
                DMA("sp", Bt[ri][d * 64:(d + 1) * 64, :, :], src_[l, d].rearrange("g p c -> p g c"), [], ["ssmS"],
                    allow_slow_non_contiguous=True)
            for ri, src_ in enumerate((c_re, c_im)):
                for f in range(4):
                    DMA("sp", CN[ri][:, f, d * 64:(d + 1) * 64], src_[l, d, 8 * f:8 * f + 8].rearrange("g c p -> (g c) p"), [], ["ssmS"])
        DMA("sp", dsk, ssm_d[l].rearrange("(f p) -> p f", p=128), [], ["ssmS"], allow_slow_non_contiguous=True)
        DMA("pool", GW, glu_w[l].rearrange("(k p) c -> p k c", p=128), [], ["GW"])
        V("act", "activation", ["ssmS"], ["ssmS"], out=S["DT"], in_=S["DT"], func=ACT.Exp)
        tt(S["a"], S["LR"], S["DT"], ALU.mult)
        tt(S["th"], S["LI"], S["DT"], ALU.mult)
        V("act", "activation", ["ssmS"], ["ssmS"], out=S["er"], in_=S["a"], func=ACT.Exp)
        TWO_PI = 2.0 * math.pi
        for nm, shift in (("sn", 0.0), ("cs", 0.25)):
            V("dve", "tensor_scalar", ["ssmS"], ["ssmS"], out=S["t1"], in0=S["th"], scalar1=1.0 / TWO_PI, scalar2=shift,
              op0=ALU.mult, op1=ALU.add)
            V("dve", "tensor_copy", ["ssmS"], ["ssmS"], out=ti, in_=S["t1"])
            V("dve", "tensor_copy", ["ssmS"], ["ssmS"], out=S["t2"], in_=ti)
            tt(S["t1"], S["t1"], S["t2"], ALU.subtract)
            V("act", "activation", ["ssmS"], ["ssmS"], out=S[nm], in_=S["t1"], func=ACT.Sin, scale=TWO_PI)
        tt(S["lr"], S["er"], S["cs"], ALU.mult)
        tt(S["li"], S["er"], S["sn"], ALU.mult)
        V("dve", "tensor_scalar", ["ssmS"], ["ssmS"], out=S["t1"], in0=S["lr"], scalar1=-1.0, scalar2=None, op0=ALU.add)
        tt(S["t2"], S["LR"], S["LR"], ALU.mult)
        tt(S["t3"], S["LI"], S["LI"], ALU.mult)
        tt(S["t2"], S["t2"], S["t3"], ALU.add)
        V("dve", "reciprocal", ["ssmS"], ["ssmS"], out=S["t2"], in_=S["t2"])
        tt(S["t3"], S["t1"], S["LR"], ALU.mult)
        tt(S["t4"], S["li"], S["LI"], ALU.mult)
        tt(S["t3"], S["t3"], S["t4"], ALU.add)
        tt(S["kr"], S["t3"], S["t2"], ALU.mult)
        tt(S["t3"], S["li"], S["LR"], ALU.mult)
        tt(S["t4"], S["t1"], S["LI"], ALU.mult)
        tt(S["t3"], S["t3"], S["t4"], ALU.subtract)
        tt(S["ki"], S["t3"], S["t2"], ALU.mult)

        def bcast_c(a):
            return AP(a.tensor, a.offset, [[a.ap[0][0], 128], [1, 32], [0, 16]])
        tt(Be[0], Bt[0], bcast_c(S["kr"]), ALU.mult)
        tt(Bx, Bt[1], bcast_c(S["ki"]), ALU.mult)
        tt(Be[0], Be[0], Bx, ALU.subtract)
        tt(Be[1], Bt[1], bcast_c(S["kr"]), ALU.mult)
        tt(Bx, Bt[0], bcast_c(S["ki"]), ALU.mult)
        tt(Be[1], Be[1], Bx, ALU.add)
        for ri, BTx in enumerate((BTr, BTi)):
            for f in range(4):
                P.op("pe", lambda e, ri=ri, f=f: e.transpose(pF[:, 0:128], Be[ri][:, 8 * f:8 * f + 8, :].rearrange("p a b -> p (a b)"), identf[:]),
                     reads=["ssmS", "identf"], writes=["pF"])
                for j in range(8):
                    V("dve", "tensor_scalar", ["pF", "bmask"], ["BT"], out=BTx[:, 8 * f + j, :], in0=pF[:, 0:128],
                      scalar1=bmask[:, j:j + 1], scalar2=None, op0=ALU.mult)
        V("dve", "memset", [], ["CP"], CPr, 0.0)
        V("dve", "memset", [], ["CP"], CPi, 0.0)
        for ri, CPx in enumerate((CPr, CPi)):
            for f in range(4):
                P.op("pe", lambda e, ri=ri, f=f: e.transpose(pF[:, 0:128], CN[ri][:, f, :], identf[:]),
                     reads=["ssmS", "identf"], writes=["pF"])
                base = CPx[:, 8 * f, 0:16]
                dg = AP(base.tensor, base.offset, [[base.ap[0][0], 128], [128 + 16, 8], [1, 16]])
                if ri == 0:
                    V("dve", "tensor_copy", ["pF"], ["CP"], out=dg, in_=pF[:, 0:128].rearrange("p (j c) -> p j c", j=8))
                else:
                    V("dve", "tensor_scalar", ["pF"], ["CP"], out=dg, in0=pF[:, 0:128].rearrange("p (j c) -> p j c", j=8),
                      scalar1=-1.0, scalar2=None, op0=ALU.mult)
        if kind == "s":
            for d in range(2):
                DMA("sp", S["cr"][d * 64:(d + 1) * 64, :], st0[l, d, 0].rearrange("g p -> p g"), ["ssmS"], ["ssmS"], allow_slow_non_contiguous=True)
                DMA("sp", S["ci"][d * 64:(d + 1) * 64, :], st0[l, d, 1].rearrange("g p -> p g"), ["ssmS"], ["ssmS"], allow_slow_non_contiguous=True)

        load_w(WT, "WT", l, C_XSSM, 512)

        def cx_(j, t0, T_, p_, pn):
            V("act", "activation", [pn], ["G1"], out=G1[:, j, t0:t0 + T_], in_=p_, func=ACT.Copy)
        proj(WT, "WT", 512, L, cx_)
        P.barrier()

        NG = 16
        TC = 128
        nC = Ls // TC
        g2f = G2[:].rearrange("p a b -> p (a b)").bitcast(F32)
        v3 = lambda a: a.rearrange("p (g j) -> p g j", g=NG)
        bu = [v3(g2f[:, 0:NG * TC]), v3(g2f[:, NG * TC:2 * NG * TC])]
        bu2 = [v3(MODT[:, 0:2, :].rearrange("p a b -> p (a b)")), v3(XT[:].rearrange("p a b -> p (a b)"))]
        busets = [(bu[0], bu[1], "bur0", "bui0"), (bu2[0], bu2[1], "bur1", "bui1")]
        w2 = WT2[:].rearrange("p a b -> p (a b)")
        hbf = [v3(w2[:, 0:NG * TC]), v3(w2[:, NG * TC:2 * NG * TC])]
        YSb = G3
        T1 = v3(WT[:].rearrange("p a b -> p (a b)").bitcast(F32))
        T2 = v3(T2s[:])
        T2i = v3(T2s[:].bitcast(I32))
        RC, RS, RHO = v3(RREG[:, 0, :]), v3(RREG[:, 1, :]), v3(RREG[:, 2, :])
        fl = lambda a: a.rearrange("p g j -> p (g j)")
        pbanks = [(pA, "pA"), (pB, "pB"), (pC, "pC"), (pD, "pD")]
        ybanks = [(pO0, "pO0"), (pO1, "pO1")]
        yrr = [0]
        for gh in range(2):
            gsl = slice(gh * NG, (gh + 1) * NG)
            ths, ers = S["th"][:, gsl], S["er"][:, gsl]
            thb = AP(ths.tensor, ths.offset, [[ths.ap[0][0], 128], [1, NG], [0, TC]])
            erb = AP(ers.tensor, ers.offset, [[ers.ap[0][0], 128], [1, NG], [0, TC]])
            jb = AP(JT[:].tensor, JT[:].offset, [[JT[:].ap[0][0], 128], [0, NG], [1, TC]])
            for tab, shift in ((RS, 0.0), (RC, 0.25)):
                V("dve", "tensor_tensor", ["ssmS", "JT", "BT", "CP"], ["RT"], out=tab, in0=thb, in1=jb, op=ALU.mult)
                V("dve", "tensor_scalar", ["RT"], ["RT"], out=tab, in0=tab, scalar1=1.0 / TWO_PI, scalar2=shift, op0=ALU.mult, op1=ALU.add)
                V("dve", "tensor_copy", ["RT"], ["T2"], out=T2i, in_=tab)
                V("dve", "tensor_copy", ["T2"], ["T1"], out=T1, in_=T2i)
                V("dve", "tensor_tensor", ["RT", "T1"], ["RT"], out=tab, in0=tab, in1=T1, op=ALU.subtract)
                V("act", "activation", ["RT"], ["RT"], out=tab, in_=tab, func=ACT.Sin, scale=TWO_PI)
            V("dve", "tensor_copy", ["ssmS", "RT"], ["RT"], out=RHO, in_=erb)
            V("dve", "memset", ["RT"], ["RT"], RHO[:, :, 0], 0.0)
            lr_, li_ = S["lr"][:, gsl], S["li"][:, gsl]
            q1, q2, q3, q4 = S["q1"][:, gsl], S["q2"][:, gsl], S["q3"][:, gsl], S["q4"][:, gsl]
            for sq_ in range(nseq):
                tokb = sq_ * Ls
                cr_, ci_ = S["cr"][:, gsl], S["ci"][:, gsl]
                if kind == "p":
                    V("dve", "memset", ["ssmS"], ["ssmS"], cr_, 0.0)
                    V("dve", "memset", ["ssmS"], ["ssmS"], ci_, 0.0)
                def emit_bu(c):
                    tf0 = tokb + c * TC
                    tb0 = tokb + (nC - 1 - c) * TC
                    st_ = busets[c % 2]
                    for ri, BTx in enumerate((BTr, BTi)):
                        bn = st_[2 + ri]
                        for q in range(4):
                            pt_, pn = pbanks[q]
                            for j in range(4):
                                g = gh * NG + 4 * q + j
                                f = g // 8
                                MM(pt_[0:64, j * TC:(j + 1) * TC], BTx[:, g, 0:64], G1[:, f, tf0:tf0 + TC], True, True, ["BT", "G1"], [pn])
                                MM(pt_[64:128, j * TC:(j + 1) * TC], BTx[:, g, 64:128], G1[:, f, tb0:tb0 + TC], True, True, ["BT", "G1"], [pn])
                            V("act", "activation", [pn], [bn], out=st_[ri][0:64, 4 * q:4 * q + 4, :],
                              in_=pt_[0:64, :].rearrange("p (g j) -> p g j", g=4), func=ACT.Copy)
                            ob = st_[ri][64:128, 4 * q:4 * q + 4, :]
                            obr = AP(ob.tensor, ob.offset + (TC - 1), [[ob.ap[0][0], 64], [TC, 4], [-1, TC]])
                            V("act", "activation", [pn], [bn], out=obr, in_=pt_[64:128, :].rearrange("p (g j) -> p g j", g=4), func=ACT.Copy)

                def emit_chain(c):
                    bur, bui, kr_, ki_ = busets[c % 2]
                    V("dve", "tensor_tensor", [kr_, "RT"], ["T1"], out=fl(T1), in0=fl(bur), in1=fl(RC), op=ALU.mult)
                    V("dve", "tensor_tensor", [ki_, "RT"], ["T2"], out=fl(T2), in0=fl(bui), in1=fl(RS), op=ALU.mult)
                    V("dve", "tensor_tensor", ["T1", "T2"], ["T1"], out=fl(T1), in0=fl(T1), in1=fl(T2), op=ALU.add)
                    V("dve", "tensor_tensor", [kr_, "RT"], [kr_], out=fl(bur), in0=fl(bur), in1=fl(RS), op=ALU.mult)
                    V("dve", "tensor_tensor", [ki_, "RT", "T2"], [ki_], out=fl(bui), in0=fl(bui), in1=fl(RC), op=ALU.mult)
                    V("dve", "tensor_tensor", [kr_, ki_], [ki_], out=fl(bui), in0=fl(bui), in1=fl(bur), op=ALU.subtract)
                    V("dve", "tensor_tensor", ["ssmS"], ["q1"], out=q1, in0=lr_, in1=cr_, op=ALU.mult)
                    V("dve", "tensor_tensor", ["ssmS"], ["q2"], out=q2, in0=li_, in1=ci_, op=ALU.mult)
                    V("dve", "tensor_tensor", ["q1", "q2"], ["q1"], out=q1, in0=q1, in1=q2, op=ALU.subtract)
                    V("dve", "tensor_tensor", ["q1", "T1"], ["T1"], out=T1[:, :, 0], in0=T1[:, :, 0], in1=q1, op=ALU.add)
                    V("dve", "tensor_tensor", ["ssmS"], ["q3"], out=q3, in0=li_, in1=cr_, op=ALU.mult)
                    V("dve", "tensor_tensor", ["ssmS"], ["q4"], out=q4, in0=lr_, in1=ci_, op=ALU.mult)
                    V("dve", "tensor_tensor", ["q3", "q4"], ["q3"], out=q3, in0=q3, in1=q4, op=ALU.add)
                    V("dve", "tensor_tensor", ["q3", ki_], [ki_], out=bui[:, :, 0], in0=bui[:, :, 0], in1=q3, op=ALU.add)
                    V("dve", "tensor_tensor_scan", ["T1", "RT", kr_], [kr_], out=fl(bur), data0=fl(RHO), data1=fl(T1), initial=0.0,
                      op0=ALU.mult, op1=ALU.add)
                    V("dve", "tensor_tensor_scan", [ki_, "RT", "T2", "T1"], ["T2"], out=fl(T2), data0=fl(RHO), data1=fl(bui), initial=0.0,
                      op0=ALU.mult, op1=ALU.add)

                def emit_chainB(c):
                    bur, bui, kr_, ki_ = busets[c % 2]
                    V("dve", "tensor_tensor", [kr_, "RT"], ["T1"], out=fl(T1), in0=fl(bur), in1=fl(RC), op=ALU.mult)
                    V("dve", "tensor_tensor", ["T2", "RT"], [ki_], out=fl(bui), in0=fl(T2), in1=fl(RS), op=ALU.mult)
                    V("dve", "tensor_tensor", ["T1", ki_], ["T1"], out=fl(T1), in0=fl(T1), in1=fl(bui), op=ALU.subtract)
                    V("dve", "tensor_tensor", [kr_, "RT"], [kr_], out=fl(bur), in0=fl(bur), in1=fl(RS), op=ALU.mult)
                    V("dve", "tensor_tensor", ["T2", "RT"], ["T2"], out=fl(T2), in0=fl(T2), in1=fl(RC), op=ALU.mult)
                    V("dve", "tensor_tensor", ["T2", kr_], ["T2"], out=fl(T2), in0=fl(T2), in1=fl(bur), op=ALU.add)
                    V("dve", "tensor_copy", ["T1"], ["ssmS"], out=cr_, in_=T1[:, :, TC - 1])
                    V("dve", "tensor_copy", ["T2"], ["ssmS"], out=ci_, in_=T2[:, :, TC - 1])
                    for ri, Hh, hn in ((0, T1, "T1"), (1, T2, "T2")):
                        V("act", "activation", [hn], ["hbf"], out=hbf[ri][0:64, :, :], in_=Hh[0:64, :, :], func=ACT.Copy)
                        ib = Hh[64:128, :, :]
                        ibr = AP(ib.tensor, ib.offset + (TC - 1), [[ib.ap[0][0], 64], [TC, NG], [-1, TC]])
                        V("act", "activation", [hn], ["hbf"], out=hbf[ri][64:128, :, :], in_=ibr, func=ACT.Copy)

                def emit_y(c):
                    tf0 = tokb + c * TC
                    tb0 = tokb + (nC - 1 - c) * TC
                    pend = []
                    for half, tk0 in ((0, tf0), (1, tb0)):
                        first = (c < nC - 1 - c)
                        pr = slice(64 * half, 64 * half + 64)
                        for f in (2 * gh, 2 * gh + 1):
                            pt_, pn = ybanks[yrr[0] % 2]
                            yrr[0] += 1
                            n_ = 0
                            for j in range(8):
                                g = 8 * f + j
                                gl = g - gh * NG
                                for ri, CPx in enumerate((CPr, CPi)):
                                    MM(pt_[:, 0:TC], CPx[pr, g, :], hbf[ri][pr, gl, :], n_ == 0, n_ == 15, ["CP", "hbf"], [pn])
                                    n_ += 1
                            yo = yrr[0] % 4
                            yt = tmpA[:, yo * TC:(yo + 1) * TC]
                            ytn = "ytmp%d" % yo
                            V("act", "activation", [pn], [ytn], out=yt, in_=pt_[:, 0:TC], func=ACT.Copy)
                            pend.append((first, f, tk0, yo, yt, ytn))
                    return pend

                def emit_ys(pend):
                    for (first, f, tk0, yo, yt, ytn) in pend:
                        if first:
                            yx = tmpB[:, (yo % 2) * TC:(yo % 2) * TC + TC]
                            yxn = "yx%d" % (yo % 2)
                            V("act", "activation", ["G1", "ssmS"], [yxn], out=yx, in_=G1[:, f, tk0:tk0 + TC],
                              func=ACT.Copy, scale=dsk[:, f:f + 1])
                            V("dve", "tensor_tensor", [yxn, ytn], ["YS", "G3"], out=YSb[:, f, tk0:tk0 + TC], in0=yx, in1=yt, op=ALU.add)
                        else:
                            V("dve", "tensor_tensor", [ytn, "YS", "G3"], ["YS", "G3"], out=YSb[:, f, tk0:tk0 + TC],
                              in0=YSb[:, f, tk0:tk0 + TC], in1=yt, op=ALU.add)

                emit_bu(0)
                pend_ = None
                for c in range(nC):
                    if c + 1 < nC:
                        emit_bu(c + 1)
                    emit_chain(c)
                    if pend_ is not None:
                        emit_ys(pend_)
                    emit_chainB(c)
                    pend_ = emit_y(c)
                emit_ys(pend_)
                if kind == "p":
                    for d in range(2):
                        DMA("sp", nst[sidx0 + sq_, l, d, 0, gsl, :].rearrange("g p -> p g"), S["cr"][d * 64:(d + 1) * 64, gsl], ["ssmS"], ["nstout"], allow_slow_non_contiguous=True)
                        DMA("sp", nst[sidx0 + sq_, l, d, 1, gsl, :].rearrange("g p -> p g"), S["ci"][d * 64:(d + 1) * 64, gsl], ["ssmS"], ["nstout"], allow_slow_non_contiguous=True)
        P.barrier()
        load_w(WT, "WT", l, C_GSSM, 512)

        def cg2(j, t0, T_, p_, pn):
            V("act", "activation", [pn], ["G1"], out=G1[:, j, t0:t0 + T_], in_=p_, func=ACT.Silu)
        proj(WT, "WT", 512, L, cg2)
        for t0 in range(0, L, T):
            V("act", "activation", ["YS", "G3"], ["YG", "xtok2"], out=YG[:, :, 0:T], in_=YSb[:, :, t0:t0 + T], func=ACT.Gelu)
            for m in range(4):
                for k in range(4):
                    MM(pA[:, 0:T], GW[:, k, m * 128:(m + 1) * 128], YG[:, k, 0:T], k == 0, k == 3, ["GW", "YG", "xtok2"], ["pA"])
                for k in range(4):
                    MM(pB[:, 0:T], GW[:, k, 512 + m * 128:512 + (m + 1) * 128], YG[:, k, 0:T], k == 0, k == 3, ["GW", "YG", "xtok2"], ["pB"])
                V("act", "activation", ["pB"], ["tmpA"], out=tmpA[:, 0:T], in_=pB[:, 0:T], func=ACT.Sigmoid)
                V("dve", "tensor_tensor", ["pA", "tmpA"], ["tmpA"], out=tmpA[:, 0:T], in0=pA[:, 0:T], in1=tmpA[:, 0:T], op=ALU.mult)
                V("dve", "tensor_tensor", ["tmpA", "G1", "YG"], ["G3"], out=G3[:, m, t0:t0 + T], in0=G1[:, m, t0:t0 + T], in1=tmpA[:, 0:T], op=ALU.mult)

    try:
      for l in range(2):
          CV.reset()
          ngb = layer_mod(l)
          set_mod(l, 0, ngb)
          chk("mod")
          srcs = [xp[0], xp[1]] if l == 0 else [x1p[0], x1p[1]]
          dsts = [x1p[0], x1p[1]] if l == 0 else [yp[0], yp[1]]
          seq_pass(l, "p", 2, LP, srcs, dsts, 0, ngb)
          CV.reset()
          ngb = CV.get([128, D])
          set_mod(l, 1, ngb)
          src = xs if l == 0 else x1s
          dst = x1s if l == 0 else ys
          seq_pass(l, "s", 1, LS, [src], [dst], 0, ngb)
    except _Stop:
        pass
    P.finish("sp")
    P.emit(st)
    st.close()
    return nc


_IN_NAMES = ["norm_g", "w_ada", "b_ada", "w_in", "attn_sink", "conv_dw", "conv_db", "conv_ln_g", "conv_ln_b",
             "pool_w", "pool_scale", "ssm_lam_re", "ssm_lam_im", "ssm_log_dt", "ssm_b_re", "ssm_b_im",
             "ssm_c_re", "ssm_c_im", "ssm_d", "ssm_glu_w", "w_br", "w_out", "final_g"]


def _rope_tables():
    t = np.arange(LS)
    row = (t // 64).astype(np.float32)
    col = (t % 64).astype(np.float32)
    inv = (10000.0 ** (-np.arange(16, dtype=np.float32) / 16)).astype(np.float32)
    ang = np.concatenate([row[:, None] * inv, col[:, None] * inv], axis=-1)
    c = np.cos(ang).T.astype(np.float32)
    s = np.sin(ang).T.astype(np.float32)
    cos64 = np.concatenate([c, c], axis=0)
    sin64 = np.concatenate([-s, s], axis=0)
    return (np.ascontiguousarray(np.concatenate([cos64, cos64], 0)),
            np.ascontiguousarray(np.concatenate([sin64, sin64], 0)))


def _pool_corr():
    pc = np.ones((4, 16), np.float32)
    Lq = 64
    for g, w in enumerate((2, 4, 8, 16)):
        for e in range(16):
            tt = e if e < 8 else Lq - 16 + e
            lo = max(tt - w // 2, 0)
            hi = min(tt + w - w // 2, Lq)
            pc[g, e] = w / float(hi - lo)
    return pc.reshape(1, 64)


_NC_CACHE = {}


def kernel(**inputs):
    f = {k: np.ascontiguousarray(np.asarray(v, dtype=np.float32)) for k, v in inputs.items()}
    if "nc" not in _NC_CACHE:
        _NC_CACHE["nc"] = build_program()
    nc = _NC_CACHE["nc"]
    rc, rs = _rope_tables()
    pc = _pool_corr()
    in_maps = []
    for c in range(8):
        sq = c % 2
        m = {n: f[n] for n in _IN_NAMES}
        m["xp"] = f["x_prompt"][2 * c:2 * c + 2]
        m["xs"] = f["x_sample"][sq]
        m["ck"] = f["cache_k"][sq].reshape(2, 256, 128)
        m["cv"] = f["cache_v"][sq].reshape(2, 256, 128)
        m["st0"] = f["state_ssm"][sq]
        m["cvec"] = np.stack([f["c_ctx"], f["c"][sq]], 0)
        m["ropec"] = rc
        m["ropes"] = rs
        m["poolcorr"] = pc
        in_maps.append({k: np.ascontiguousarray(v) for k, v in m.items()})
    res = run_bass_kernel_spmd(nc, in_maps, core_ids=list(range(8)))
    R = res.results
    y_prompt = np.concatenate([R[c]["yp"] for c in range(8)], 0).astype(np.float32)
    y_sample = np.stack([R[0]["ys"], R[1]["ys"]], 0).astype(np.float32)
    new_k = np.concatenate([R[c]["nk"] for c in range(8)], 0).reshape(16, 2, 256, 2, 64).astype(np.float32)
    new_v = np.concatenate([R[c]["nv"] for c in range(8)], 0).reshape(16, 2, 256, 2, 64).astype(np.float32)
    new_st = np.concatenate([R[c]["nst"] for c in range(8)], 0).astype(np.float32)
    return (y_prompt, y_sample, new_k, new_v, new_st)
```

```python
import math
from contextlib import ExitStack
import numpy as np
import concourse.bass as bass
import concourse.mybir as mybir
from concourse.bass_utils import run_bass_kernel_spmd
from concourse.ap import AP

F32 = mybir.dt.float32
BF16 = mybir.dt.bfloat16
I32 = mybir.dt.int32
ALU = mybir.AluOpType
ACT = mybir.ActivationFunctionType

D = 1024
NCOL = 8960
LS = 2048
LP = 256
EPS = 1e-6
NDMASLOT = 24

C_Q, C_K, C_V, C_GATT = 0, 512, 640, 768
C_ACONV, C_GCONV = 1280, 2304
C_XPOOL, C_GPOOL = 2816, 3328
C_XSSM, C_GSSM = 3840, 4352
C_MRG = 4864


class Prog:
    ENGS = ("pe", "act", "dve", "pool", "sp")

    def __init__(self, nc):
        self.nc = nc
        self.ops = []
        self.cnt = {e: 0 for e in self.ENGS}
        self.dma_n = {"sp": 0, "pool": 0}
        self.slot_uses = {}
        self.slot_last = {}
        self.lastw = {}
        self.readers = {}
        self.info = []
        self.waited = {e: {} for e in self.ENGS}
        self.barrier_vals = {}

    def _deps(self, reads, writes):
        d = set()
        for k in list(reads) + list(writes):
            if k in self.lastw:
                d.add(self.lastw[k])
        for k in writes:
            for r in self.readers.get(k, ()):
                d.add(r)
        return d

    def _commit(self, oid, reads, writes):
        for k in reads:
            self.readers.setdefault(k, []).append(oid)
        for k in writes:
            self.lastw[k] = oid
            self.readers[k] = []

    def barrier(self):
        for s, u in self.slot_uses.items():
            self.barrier_vals[("d", s)] = 16 * u
        for e in self.ENGS:
            if self.cnt[e]:
                self.barrier_vals[("c", e)] = self.cnt[e]

    def _waits_for(self, eng, deps):
        ws = {}
        for semkey, val in self.barrier_vals.items():
            if self.waited[eng].get(semkey, 0) < val:
                ws[semkey] = val
        for d in deps:
            semkey, val = self.info[d]
            if semkey == ("c", eng) and eng in self.NO_SELF_WAIT:
                continue
            if self.waited[eng].get(semkey, 0) >= val:
                continue
            ws[semkey] = max(ws.get(semkey, 0), val)
        for k, v in ws.items():
            self.waited[eng][k] = v
        return sorted(ws.items(), key=lambda kv: str(kv[0]))

    NO_SELF_WAIT = ("pe",)
    PSUM_KEYS = ("pA", "pB", "pC", "pD", "pT", "pO0", "pO1", "pF")

    def op(self, eng, fn, reads=(), writes=()):
        extra = [k for k in reads if k in self.PSUM_KEYS and k not in writes]
        if extra:
            writes = list(writes) + extra
        deps = self._deps(reads, writes)
        waits = self._waits_for(eng, deps)
        self.cnt[eng] += 1
        oid = len(self.ops)
        self.info.append((("c", eng), self.cnt[eng]))
        self.ops.append((eng, fn, waits, ("c", eng), 1))
        self._commit(oid, reads, writes)
        return oid

    def dma(self, eng, fn, reads=(), writes=()):
        slot = (eng, self.dma_n[eng] % NDMASLOT)
        self.dma_n[eng] += 1
        deps = self._deps(reads, writes)
        if self.slot_last.get(slot) is not None:
            deps.add(self.slot_last[slot])
        waits = self._waits_for(eng, deps)
        self.slot_uses[slot] = self.slot_uses.get(slot, 0) + 1
        oid = len(self.ops)
        self.info.append((("d", slot), 16 * self.slot_uses[slot]))
        self.ops.append((eng, fn, waits, ("d", slot), 16))
        self.slot_last[slot] = oid
        self._commit(oid, reads, writes)
        return oid

    def finish(self, final_eng="sp"):
        waits = []
        for s, u in self.slot_uses.items():
            waits.append((("d", s), 16 * u))
        for e in self.ENGS:
            if self.cnt[e]:
                waits.append((("c", e), self.cnt[e]))
        self.ops.append((final_eng, None, waits, None, 0))

    def emit(self, stack):
        nc = self.nc
        sems = {}
        for e in self.ENGS:
            sems[("c", e)] = stack.enter_context(nc.semaphore("c_" + e))
        for q in ("sp", "pool"):
            for s in range(NDMASLOT):
                sems[("d", (q, s))] = stack.enter_context(nc.semaphore("d_%s_%d" % (q, s)))
        block = stack.enter_context(nc.Block())
        engmap = {"pe": block.tensor, "act": block.scalar, "dve": block.vector,
                  "pool": block.gpsimd, "sp": block.sync}
        for e in self.ENGS:
            mine = [o for o in self.ops if o[0] == e]
            if not mine:
                continue

            def body(engine, mine=mine):
                for (_, fn, waits, sig, inc) in mine:
                    for (k, v) in waits:
                        engine.wait_ge(sems[k], v)
                    if fn is not None:
                        ins = fn(engine)
                        ins.then_inc(sems[sig], inc)
            engmap[e](body)


def rev_free(ap2d, n):
    return AP(ap2d.tensor, ap2d.offset + (n - 1), [[ap2d.ap[0][0], ap2d.ap[0][1]], [-1, n]])


class _Stop(Exception):
    pass


def build_program(stage=None):
    nc = bass.Bass("TRN2", target_bir_lowering=False)

    def din(name, shape):
        return nc.dram_tensor(name, shape, F32, kind="ExternalInput").ap()

    def dout(name, shape):
        return nc.dram_tensor(name, shape, F32, kind="ExternalOutput").ap()

    xp = din("xp", [2, LP, D])
    xs = din("xs", [LS, D])
    ck = din("ck", [2, 256, 128])
    cv = din("cv", [2, 256, 128])
    st0 = din("st0", [2, 2, 2, 32, 64])
    cvec = din("cvec", [2, D])
    norm_g = din("norm_g", [2, D])
    w_ada = din("w_ada", [2, D, 3 * D])
    b_ada = din("b_ada", [2, 3 * D])
    w_in = din("w_in", [2, D, NCOL])
    sink = din("attn_sink", [2, 8])
    conv_dw = din("conv_dw", [2, 31, 512])
    conv_db = din("conv_db", [2, 512])
    conv_ln_g = din("conv_ln_g", [2, 512])
    conv_ln_b = din("conv_ln_b", [2, 512])
    pool_w = din("pool_w", [2, 4, 128, 128])
    pool_scale = din("pool_scale", [2, 512])
    lam_re = din("ssm_lam_re", [2, 2, 32, 64])
    lam_im = din("ssm_lam_im", [2, 2, 32, 64])
    log_dt = din("ssm_log_dt", [2, 2, 32])
    b_re = din("ssm_b_re", [2, 2, 32, 64, 16])
    b_im = din("ssm_b_im", [2, 2, 32, 64, 16])
    c_re = din("ssm_c_re", [2, 2, 32, 16, 64])
    c_im = din("ssm_c_im", [2, 2, 32, 16, 64])
    ssm_d = din("ssm_d", [2, 512])
    glu_w = din("ssm_glu_w", [2, 512, 1024])
    w_br = din("w_br", [2, 4, 512, D])
    w_out = din("w_out", [2, D, D])
    final_g = din("final_g", [D])
    ropec = din("ropec", [128, LS])
    ropes = din("ropes", [128, LS])
    poolcorr = din("poolcorr", [1, 64])

    yp = dout("yp", [2, LP, D])
    ys = dout("ys", [LS, D])
    nk = dout("nk", [2, 2, LP, 128])
    nv = dout("nv", [2, 2, LP, 128])
    nst = dout("nst", [2, 2, 2, 2, 32, 64])
    dbg = dout("dbg", [128, 4, LS]) if stage is not None else None

    x1p = nc.dram_tensor("x1p", [2, LP, D], F32, kind="Internal").ap()
    x1s = nc.dram_tensor("x1s", [LS, D], F32, kind="Internal").ap()
    brS = nc.dram_tensor("brS", [4, 128, 4, LS], BF16, kind="Internal").ap()
    moddram = nc.dram_tensor("moddram", [2, 3 * D], F32, kind="Internal").ap()

    P = Prog(nc)
    st = ExitStack()

    def sb(name, shape, dt=F32):
        return st.enter_context(nc.sbuf_tensor(name, shape, dt))

    def ps(name, shape, dt=F32):
        return st.enter_context(nc.psum_tensor(name, shape, dt))

    def V(eng, method, reads, writes, *a, **kw):
        return P.op(eng, lambda e: getattr(e, method)(*a, **kw), reads=reads, writes=writes)

    def DMA(eng, out, in_, reads, writes, **kw):
        return P.dma(eng, lambda e: e.dma_start(out=out, in_=in_, **kw), reads=reads, writes=writes)

    def MM(out, lhsT, rhs, start, stop, reads, writes):
        return P.op("pe", lambda e: e.matmul(out, lhsT=lhsT, rhs=rhs, start=start, stop=stop),
                    reads=reads, writes=writes)

    identf = sb("identf", [128, 128])
    identb = sb("identb", [128, 128], BF16)
    onesf = sb("onesf", [128, 128])
    V("pool", "memset", [], ["identf"], identf[:], 1.0)
    V("pool", "affine_select", ["identf"], ["identf"], out=identf[:], in_=identf[:], pattern=[[-1, 128]],
      compare_op=ALU.is_equal, fill=0.0, base=0, channel_multiplier=1)
    V("dve", "tensor_copy", ["identf"], ["identb"], out=identb[:], in_=identf[:])
    V("pool", "memset", [], ["onesf"], onesf[:], 1.0)
    mprev = sb("mprev", [128, 128], BF16)
    mnext = sb("mnext", [128, 128], BF16)
    mtmp = sb("mtmp", [128, 128])
    V("pool", "memset", [], ["mtmp"], mtmp[:], 1.0)
    V("pool", "affine_select", ["mtmp"], ["mtmp"], out=mtmp[:], in_=mtmp[:], pattern=[[-1, 128]],
      compare_op=ALU.is_ge, fill=0.0, base=0, channel_multiplier=1)
    V("dve", "tensor_copy", ["mtmp"], ["mprev"], out=mprev[:], in_=mtmp[:])
    V("pool", "memset", ["mtmp"], ["mtmp"], mtmp[:], 1.0)
    V("pool", "affine_select", ["mtmp"], ["mtmp"], out=mtmp[:], in_=mtmp[:], pattern=[[1, 128]],
      compare_op=ALU.is_ge, fill=0.0, base=0, channel_multiplier=-1)
    V("dve", "tensor_copy", ["mtmp"], ["mnext"], out=mnext[:], in_=mtmp[:])
    bmask = sb("bmask", [128, 8])
    V("pool", "memset", [], ["bmask"], bmask[:], 1.0)
    V("pool", "affine_select", ["bmask"], ["bmask"], out=bmask[:], in_=bmask[:], pattern=[[-16, 8]],
      compare_op=ALU.is_ge, fill=0.0, base=0, channel_multiplier=1)
    V("pool", "affine_select", ["bmask"], ["bmask"], out=bmask[:], in_=bmask[:], pattern=[[16, 8]],
      compare_op=ALU.is_ge, fill=0.0, base=15, channel_multiplier=-1)

    pA = ps("pA", [128, 512])
    pB = ps("pB", [128, 512])
    pC = ps("pC", [128, 512])
    pD = ps("pD", [128, 512])
    pT = ps("pT", [128, 1024], BF16)
    pO0 = ps("pO0", [128, 512])
    pO1 = ps("pO1", [128, 512])
    pF = ps("pF", [128, 512])

    hT = sb("hT", [128, 8, LS], BF16)
    G1 = sb("G1", [128, 4, LS + 64], BF16)
    G2 = sb("G2", [128, 4, LS], BF16)
    G3 = sb("G3", [128, 4, LS], BF16)
    WT = sb("WT", [128, 8, 512], BF16)
    WT2 = sb("WT2", [128, 8, 512], BF16)
    XT = sb("XT", [128, 2, D])
    xtok = XT[:, 0, :]
    xtok2 = XT[:, 1, :]
    hb = sb("hb", [128, D], BF16)
    MODT = sb("MODT", [128, 3, D])
    SC = MODT[:, 0, :]
    SH = MODT[:, 1, :]
    GT = MODT[:, 2, :]
    tmpA = sb("tmpA", [128, 512])
    tmpB = sb("tmpB", [128, 512])
    tmpC = sb("tmpC", [128, 512], BF16)
    stat = sb("stat", [128, 8])
    T2s = sb("T2s", [128, 2048])
    JT = sb("JT", [128, 128])
    V("pool", "iota", [], ["JT"], JT[:], pattern=[[1, 128]], base=0, channel_multiplier=0,
      allow_small_or_imprecise_dtypes=True)
    csil = sb("csil", [128, 8, 2], BF16)
    cld = sb("cld", [128, 8, 2])
    ARB = 69632
    ARENA = sb("ARENA", [128, ARB // 2], BF16)

    class Carver:
        def __init__(self):
            self.off = 0

        def reset(self):
            self.off = 0

        def get(self, shape, dt=F32):
            n = 1
            for s_ in shape[1:]:
                n *= s_
            nb = n * (2 if dt == BF16 else 4)
            nb = (nb + 63) // 64 * 64
            assert self.off + nb <= ARB, ("arena overflow", self.off, nb)
            a = ARENA[0:shape[0], self.off // 2:(self.off + nb) // 2]
            self.off += nb
            if dt == BF16:
                a = a[:, 0:n]
            else:
                a = a.bitcast(dt)[:, 0:n]
            if len(shape) == 3:
                a = a.rearrange("p (a b) -> p a b", a=shape[1])
            elif len(shape) == 4:
                a = a.rearrange("p (a b c) -> p a b c", a=shape[1], b=shape[2])
            return a
    CV = Carver()

    for v_ in range(2):
        DMA("sp", cld[:, :, v_], cvec[v_].rearrange("(k p) -> p k", p=128), [], ["cld"],
            allow_slow_non_contiguous=True)
    V("act", "activation", ["cld"], ["csil"], out=csil[:], in_=cld[:], func=ACT.Silu)

    def row_bc(vec_ap, n):
        return vec_ap.rearrange("(o d) -> o d", o=1).partition_broadcast(n)

    def layer_mod(l):
        ngb = CV.get([128, D])
        for blk in range(6):
            Wm, Wmn = [(WT, "WT"), (WT2, "WT2")][blk % 2]
            DMA("pool", Wm[:], w_ada[l, :, blk * 512:(blk + 1) * 512].rearrange("(k p) c -> p k c", p=128),
                [], [Wmn])
            for k in range(8):
                MM(pA[0:2, :], csil[:, k, :], Wm[:, k, :], k == 0, k == 7, [Wmn, "csil"], ["pA"])
            DMA("sp", tmpA[0:2, :], row_bc(b_ada[l, blk * 512:(blk + 1) * 512], 2), [], ["tmpA"])
            V("dve", "tensor_add", ["tmpA", "pA"], ["tmpB"], out=tmpB[0:2, :], in0=pA[0:2, :], in1=tmpA[0:2, :])
            DMA("sp", moddram[:, blk * 512:(blk + 1) * 512], tmpB[0:2, :], ["tmpB"], ["moddram"])
        return ngb

    def set_mod(l, v, ngb):
        DMA("sp", ngb, row_bc(norm_g[l], 128), [], ["ngb"])
        DMA("sp", SH, row_bc(moddram[v, 0:D], 128), ["moddram"], ["SH"])
        DMA("sp", SC, row_bc(moddram[v, D:2 * D], 128), ["moddram"], ["SC"])
        DMA("sp", GT, row_bc(moddram[v, 2 * D:3 * D], 128), ["moddram"], ["GT"])
        V("dve", "scalar_tensor_tensor", ["SC", "ngb"], ["SC"], out=SC, in0=SC, scalar=1.0, in1=ngb,
          op0=ALU.add, op1=ALU.mult)

    def rms_stats(xb=None, xk="xtok"):
        if xb is None:
            xb = xtok
        V("act", "activation", [xk], ["xtok2", "stat"], out=xtok2, in_=xb, func=ACT.Square,
          accum_out=stat[:, 0:1])
        V("dve", "tensor_scalar", ["stat"], ["stat"], out=stat[:, 1:2], in0=stat[:, 0:1], scalar1=1.0 / D,
          scalar2=EPS, op0=ALU.mult, op1=ALU.add)
        V("act", "activation", ["stat"], ["stat"], out=stat[:, 3:4], in_=stat[:, 1:2], func=ACT.Sqrt)
        V("dve", "reciprocal", ["stat"], ["stat"], out=stat[:, 2:3], in_=stat[:, 3:4])

    def xbuf(t):
        return (xtok, "xtok") if t % 2 == 0 else (T2s[:, 0:D], "xB")

    def load_norm(src_rows, t=0):
        xb, xk = xbuf(t)
        DMA("sp", xb, src_rows, ["xsrc"], [xk])
        rms_stats(xb, xk)
        V("dve", "scalar_tensor_tensor", [xk, "stat", "SC"], ["xtok2"], out=xtok2, in0=xb,
          scalar=stat[:, 2:3], in1=SC, op0=ALU.mult, op1=ALU.mult)
        V("dve", "tensor_add", ["xtok2", "SH"], ["hb"], out=hb[:], in0=xtok2, in1=SH)

    def to_hT(t0):
        for k in range(8):
            P.op("pe", lambda e, k=k: e.transpose(pT[:, k * 128:(k + 1) * 128], hb[:, k * 128:(k + 1) * 128], identb[:]),
                 reads=["hb", "identb"], writes=["pT"])
        V("act", "activation", ["pT"], ["hT"], out=hT[:, :, t0:t0 + 128],
          in_=pT[:].rearrange("p (k t) -> p k t", k=8), func=ACT.Copy)

    def load_w(dst, dname, l, col0, ncols, dcol0=0):
        DMA("pool", dst[:, :, dcol0:dcol0 + ncols],
            w_in[l, :, col0:col0 + ncols].rearrange("(k p) c -> p k c", p=128), [], [dname])

    def load_w_rot(dst, dname, l, col0, nheads, dcol0=0):
        for k in range(8):
            for two in range(2):
                src = w_in[l, k * 128:(k + 1) * 128, col0:col0 + nheads * 64].rearrange(
                    "p (h two j) -> p h two j", two=2, j=32)[:, :, 1 - two, :]
                dd = dst[:, k, dcol0:dcol0 + nheads * 64].rearrange("p (h two j) -> p h two j", two=2, j=32)[:, :, two, :]
                DMA("pool", dd, src, [], [dname])

    psum_rr = [0]

    def proj(wt, wname, ncols, L, consume, ps_list=None, col_off=0):
        if ps_list is None:
            ps_list = [(pA, "pA"), (pB, "pB"), (pC, "pC"), (pD, "pD")]
        T = min(512, L)
        for t0 in range(0, L, T):
            for j in range(ncols // 128):
                pt_, pn = ps_list[psum_rr[0] % len(ps_list)]
                psum_rr[0] += 1
                for k in range(8):
                    MM(pt_[:, 0:T], wt[:, k, col_off + j * 128:col_off + (j + 1) * 128], hT[:, k, t0:t0 + T],
                       k == 0, k == 7, [wname, "hT"], [pn])
                consume(j, t0, T, pt_[:, 0:T], pn)

    nstage = [0]

    def chk(label):
        if stage == label:
            raise _Stop()

    def stage_point(L):
        nstage[0] += 1
        if stage is not None and nstage[0] == stage:
            DMA("pool", dbg[:, :, 0:L], G3[:, :, 0:L], ["G3", "G1", "G2"], [])
            raise _Stop()

    def seq_pass(l, kind, nseq, Ls, x_srcs, x_dsts, sidx0, ngb):
        L = nseq * Ls

        def xrow(aps, t):
            s_ = (t * 128) // Ls
            lo = t * 128 - s_ * Ls
            return aps[s_][lo:lo + 128, :]

        def seq_split(t0, T_):
            out = []
            a = t0
            while a < t0 + T_:
                s_ = a // Ls
                e_ = min(t0 + T_, (s_ + 1) * Ls)
                out.append((s_, a - s_ * Ls, a, e_ - a))
                a = e_
            return out
        SEG = Ls + 32
        SEGP = Ls + 16
        Tc = min(512, Ls)
        nt = L // 128
        T = min(512, L)
        for t in range(nt):
            load_norm(xrow(x_srcs, t), t)
            to_hT(t * 128)

        chk("ph1")
        CV.reset()
        vaug = CV.get([128, 16, 2, 65], BF16)
        ETall = CV.get([128, 5, 512], BF16)
        obf = CV.get([128, 8, 64], BF16)
        kvtok = CV.get([128, 256])
        esink = CV.get([128, 8])
        den = CV.get([128, 8])
        V("pool", "memset", [], ["vaug", "ngb"], vaug[:], 1.0)
        DMA("sp", esink, row_bc(sink[l], 128), [], ["esink"])
        V("act", "activation", ["esink"], ["esink"], out=esink, in_=esink, func=ACT.Exp)
        V("pool", "memset", [], ["G2"], G2[:], 0.0)
        if kind == "s":
            ROC = CV.get([128, LS])
            ROS = CV.get([128, LS])
            CKv = CV.get([128, 4, 256], BF16)
            cvaug = CV.get([128, 2, 2, 65], BF16)
            cktok = CV.get([128, 2, 2, 128])
            DMA("sp", ROC, ropec, [], ["ROC"])
            DMA("sp", ROS, ropes, [], ["ROS"])
            V("pool", "memset", [], ["CKv"], CKv[:], 0.0)
            V("pool", "memset", [], ["cvaug"], cvaug[:], 1.0)
            for stl in range(2):
                DMA("sp", cktok[:, 0, stl, :], ck[l, stl * 128:(stl + 1) * 128, :], [], ["cktok"])
                DMA("sp", cktok[:, 1, stl, 0:64], ck[l, stl * 128:(stl + 1) * 128, 64:128], [], ["cktok"])
                DMA("sp", cktok[:, 1, stl, 64:128], ck[l, stl * 128:(stl + 1) * 128, 0:64], [], ["cktok"])
                DMA("pool", cvaug[:, stl, :, 0:64],
                    cv[l, stl * 128:(stl + 1) * 128, :].rearrange("s (k d) -> s k d", k=2), ["cvaug"], ["cvaug"])
            for stl in range(2):
                for var in range(2):
                    P.op("pe", lambda e, stl=stl, var=var: e.transpose(pF[:, 0:128], cktok[:, var, stl, :], identf[:]),
                         reads=["cktok", "identf"], writes=["pF"])
                    if var == 0:
                        V("dve", "tensor_copy", ["pF"], ["CKv"], out=CKv[0:64, 0, stl * 128:(stl + 1) * 128], in_=pF[0:64, 0:128])
                        V("dve", "tensor_copy", ["pF"], ["CKv"], out=CKv[64:128, 3, stl * 128:(stl + 1) * 128], in_=pF[64:128, 0:128])
                    else:
                        V("dve", "tensor_copy", ["pF"], ["CKv"], out=CKv[0:64, 2, stl * 128:(stl + 1) * 128], in_=pF[0:64, 0:128])
                        V("dve", "tensor_copy", ["pF"], ["CKv"], out=CKv[64:128, 1, stl * 128:(stl + 1) * 128], in_=pF[64:128, 0:128])

        chk("attsetup")
        load_w(WT, "WT", l, C_Q, 512)
        if kind == "p":
            def cq(j, t0, T_, p_, pn):
                V("act", "activation", [pn], ["G1"], out=G1[:, j, t0:t0 + T_], in_=p_, func=ACT.Copy)
            proj(WT, "WT", 512, L, cq)
        else:
            load_w_rot(WT2, "WT2", l, C_Q, 8)
            for t0 in range(0, L, T):
                for j in range(4):
                    for k in range(8):
                        MM(pA[:, 0:T], WT[:, k, j * 128:(j + 1) * 128], hT[:, k, t0:t0 + T], k == 0, k == 7, ["WT", "hT"], ["pA"])
                    for k in range(8):
                        MM(pB[:, 0:T], WT2[:, k, j * 128:(j + 1) * 128], hT[:, k, t0:t0 + T], k == 0, k == 7, ["WT2", "hT"], ["pB"])
                    V("dve", "tensor_tensor", ["pA", "ROC"], ["tmpA"], out=tmpA[:, 0:T], in0=pA[:, 0:T], in1=ROC[:, t0:t0 + T], op=ALU.mult)
                    V("dve", "tensor_tensor", ["pB", "ROS"], ["tmpB"], out=tmpB[:, 0:T], in0=pB[:, 0:T], in1=ROS[:, t0:t0 + T], op=ALU.mult)
                    V("dve", "tensor_tensor", ["tmpA", "tmpB"], ["G1"], out=G1[:, j, t0:t0 + T], in0=tmpA[:, 0:T], in1=tmpB[:, 0:T], op=ALU.add)
        chk("q")
        load_w(WT, "WT", l, C_K, 128, 0)
        load_w(WT, "WT", l, C_K + 64, 64, 128)
        load_w(WT, "WT", l, C_K, 64, 192)
        if kind == "s":
            load_w_rot(WT2, "WT2", l, C_K, 2, 0)
            load_w_rot(WT2, "WT2", l, C_K + 64, 1, 128)
            load_w_rot(WT2, "WT2", l, C_K, 1, 192)
        for t0 in range(0, L, T):
            for var in range(2):
                for k in range(8):
                    MM(pA[:, 0:T], WT[:, k, var * 128:(var + 1) * 128], hT[:, k, t0:t0 + T], k == 0, k == 7, ["WT", "hT"], ["pA"])
                if kind == "s":
                    for k in range(8):
                        MM(pB[:, 0:T], WT2[:, k, var * 128:(var + 1) * 128], hT[:, k, t0:t0 + T], k == 0, k == 7, ["WT2", "hT"], ["pB"])
                    V("dve", "tensor_tensor", ["pA", "ROC"], ["tmpA"], out=tmpA[:, 0:T], in0=pA[:, 0:T], in1=ROC[:, t0:t0 + T], op=ALU.mult)
                    V("dve", "tensor_tensor", ["pB", "ROS"], ["tmpB"], out=tmpB[:, 0:T], in0=pB[:, 0:T], in1=ROS[:, t0:t0 + T], op=ALU.mult)
                    V("dve", "tensor_tensor", ["tmpA", "tmpB"], ["tmpA"], out=tmpA[:, 0:T], in0=tmpA[:, 0:T], in1=tmpB[:, 0:T], op=ALU.add)
                    src, sn = tmpA, "tmpA"
                else:
                    src, sn = pA, "pA"
                lo_slot, hi_slot = (0, 3) if var == 0 else (2, 1)
                V("dve", "tensor_copy", [sn], ["G2"], out=G2[0:64, lo_slot, t0:t0 + T], in_=src[0:64, 0:T])
                V("dve", "tensor_copy", [sn], ["G2"], out=G2[64:128, hi_slot, t0:t0 + T], in_=src[64:128, 0:T])
        chk("kfm")
        load_w(WT2, "WT2", l, C_K, 256)
        for t in range(nt):
            for k in range(8):
                MM(pB[:, 0:256], hT[:, k, t * 128:(t + 1) * 128], WT2[:, k, 0:256], k == 0, k == 7, ["hT", "WT2"], ["pB"])
            V("dve", "tensor_copy", ["pB"], ["vaug"], out=vaug[:, t, :, 0:64],
              in_=pB[:, 128:256].rearrange("p (k d) -> p k d", k=2))
            if kind == "p":
                V("dve", "tensor_copy", ["pB"], ["kvtok"], out=kvtok, in_=pB[:, 0:256])
                s_ = (t * 128) // Ls
                lo_ = t * 128 - s_ * Ls
                DMA("sp", nk[sidx0 + s_, l, lo_:lo_ + 128, :], kvtok[:, 0:128], ["kvtok"], [])
                DMA("sp", nv[sidx0 + s_, l, lo_:lo_ + 128, :], kvtok[:, 128:256], ["kvtok"], [])
        chk("kvtok")
        load_w(WT, "WT", l, C_GATT, 512)

        def cg(j, t0, T_, p_, pn):
            V("act", "activation", [pn], ["G3"], out=G3[:, j, t0:t0 + T_], in_=p_, func=ACT.Silu)
        proj(WT, "WT", 512, L, cg)

        chk("gate")
        rr = [0]
        for qb in range(nt):
            q0 = qb * 128
            chunks = []
            if kind == "p":
                s_ = q0 // Ls
                for c in range(s_ * Ls // 128, (s_ + 1) * Ls // 128):
                    chunks.append(("G2", c * 128, ("v", c), None))
            else:
                for c in range(2):
                    chunks.append(("CK", c * 128, ("c", c), None))
                if qb > 0:
                    chunks.append(("G2", (qb - 1) * 128, ("v", qb - 1), "prev"))
                chunks.append(("G2", qb * 128, ("v", qb), None))
                if qb < nt - 1:
                    chunks.append(("G2", (qb + 1) * 128, ("v", qb + 1), "next"))
            for kv in range(2):
                pO, pOn = (pO0, "pO0") if kv == 0 else (pO1, "pO1")
                for ci, (ksrc, koff, vsel, msk) in enumerate(chunks):
                    pS, pSn = [(pC, "pC"), (pD, "pD")][rr[0] % 2]
                    et = ETall[:, ci, :]
                    etn = "ET%d" % ci
                    rr[0] += 1
                    for slot in range(4):
                        h = 4 * kv + slot
                        j, half = h // 2, h % 2
                        if ksrc == "G2":
                            lhs = G2[:, 2 * kv + half, koff:koff + 128]
                            lname = "G2"
                        else:
                            lhs = CKv[:, 2 * kv + half, koff:koff + 128]
                            lname = "CKv"
                        MM(pS[:, slot * 128:(slot + 1) * 128], lhs, G1[:, j, q0:q0 + 128], True, True,
                           [lname, "G1"], [pSn])
                    V("act", "activation", [pSn], [etn], out=et, in_=pS[:], func=ACT.Exp, scale=0.125)
                    if msk is not None:
                        m_ = mprev if msk == "prev" else mnext
                        mb = AP(m_[:].tensor, m_[:].offset, [[m_[:].ap[0][0], 128], [0, 4], [1, 128]])
                        V("dve", "tensor_tensor", [etn, "mprev", "mnext"], [etn],
                          out=et.rearrange("p (s q) -> p s q", s=4), in0=et.rearrange("p (s q) -> p s q", s=4),
                          in1=mb, op=ALU.mult)
                for slot in range(4):
                    for ci, (ksrc, koff, vsel, msk) in enumerate(chunks):
                        if vsel[0] == "v":
                            rhs = vaug[:, vsel[1], kv, :]
                            rn = "vaug"
                        else:
                            rhs = cvaug[:, vsel[1], kv, :]
                            rn = "cvaug"
                        MM(pO[:, slot * 65:(slot + 1) * 65], ETall[:, ci, slot * 128:(slot + 1) * 128], rhs,
                           ci == 0, ci == len(chunks) - 1, ["ET%d" % ci, rn], [pOn])
                pOv = pO[:, 0:260].rearrange("p (s e) -> p s e", s=4)
                V("dve", "tensor_tensor", [pOn, "esink"], ["den"], out=den[:, 4 * kv:4 * kv + 4],
                  in0=pOv[:, :, 64], in1=esink[:, 4 * kv:4 * kv + 4], op=ALU.add)
                V("dve", "reciprocal", ["den"], ["den"], out=den[:, 4 * kv:4 * kv + 4], in_=den[:, 4 * kv:4 * kv + 4])
                dsl = den[:, 4 * kv:4 * kv + 4]
                db = AP(dsl.tensor, dsl.offset, [[dsl.ap[0][0], 128], [1, 4], [0, 64]])
                V("dve", "tensor_tensor", [pOn, "den"], ["obf"], out=obf[:, 4 * kv:4 * kv + 4, :],
                  in0=pOv[:, :, 0:64], in1=db, op=ALU.mult)
            for j in range(4):
                P.op("pe", lambda e, j=j: e.transpose(pT[:, j * 128:(j + 1) * 128],
                                                      obf[:, 2 * j:2 * j + 2, :].rearrange("p a b -> p (a b)"), identb[:]),
                     reads=["obf", "identb"], writes=["pT"])
            V("dve", "tensor_tensor", ["pT", "G3"], ["G3"], out=G3[:, :, q0:q0 + 128], in0=G3[:, :, q0:q0 + 128],
              in1=pT[:, 0:512].rearrange("p (j t) -> p j t", j=4), op=ALU.mult)
        DMA("sp", brS[0, :, :, 0:L], G3[:, :, 0:L], ["G3"], ["brS"])
        stage_point(L)

        P.barrier()
        CV.reset()
        DG = CV.get([128, 4, 31, 128], BF16)
        YC = CV.get([128, 4, 512])
        YQ = CV.get([128, 4, 512])
        cw = CV.get([128, 4, 31])
        cpar = CV.get([128, 3, 4])
        for f in range(4):
            DMA("sp", cw[:, f, :], conv_dw[l, :, f * 128:(f + 1) * 128].rearrange("k p -> p k"), [], ["cw"],
                allow_slow_non_contiguous=True)
        for i_, src_ in enumerate((conv_db, conv_ln_g, conv_ln_b)):
            DMA("sp", cpar[:, i_, :], src_[l].rearrange("(f p) -> p f", p=128), [], ["cpar"], allow_slow_non_contiguous=True)
        for f in range(4):
            for k in range(31):
                V("dve", "tensor_scalar", ["identf", "cw"], ["DG"], out=DG[:, f, k, :], in0=identf[:],
                  scalar1=cw[:, f, k:k + 1], scalar2=None, op0=ALU.mult)
        V("pool", "memset", [], ["G1"], G1[:], 0.0)
        load_w(WT, "WT", l, C_ACONV + 512, 512)

        def cs(j, t0, T_, p_, pn):
            V("act", "activation", [pn], ["G2"], out=G2[:, j, t0:t0 + T_], in_=p_, func=ACT.Sigmoid)
        proj(WT, "WT", 512, L, cs)
        load_w(WT2, "WT2", l, C_ACONV, 512)

        def cx(j, t0, T_, p_, pn):
            for (s_, lo_, g0_, n_) in seq_split(t0, T_):
                V("dve", "tensor_tensor", [pn, "G2"], ["G1"], out=G1[:, j, s_ * SEG + 16 + lo_:s_ * SEG + 16 + lo_ + n_],
                  in0=p_[:, g0_ - t0:g0_ - t0 + n_], in1=G2[:, j, g0_:g0_ + n_], op=ALU.mult)
        proj(WT2, "WT2", 512, L, cx)
        load_w(WT, "WT", l, C_GCONV, 512)
        proj(WT, "WT", 512, L, cg)
        for (s_, tl0) in [(s__, a__) for s__ in range(nseq) for a__ in range(0, Ls, Tc)]:
            t0 = s_ * Ls + tl0
            cb = s_ * SEG + tl0
            T = Tc
            for f in range(4):
                for k in range(31):
                    MM(pA[:, 0:T], DG[:, f, k, :], G1[:, f, cb + k + 1:cb + k + 1 + T], k == 0, k == 30, ["DG", "G1"], ["pA"])
                V("act", "activation", ["pA", "cpar"], ["YC"], out=YC[:, f, 0:T], in_=pA[:, 0:T], func=ACT.Identity,
                  bias=cpar[:, 0, f:f + 1])
                V("dve", "tensor_tensor", ["YC"], ["YQ"], out=YQ[:, f, 0:T], in0=YC[:, f, 0:T], in1=YC[:, f, 0:T], op=ALU.mult)
            for f in range(4):
                MM(pB[:, 0:T], onesf[:], YC[:, f, 0:T], f == 0, f == 3, ["onesf", "YC"], ["pB"])
            for f in range(4):
                MM(pC[:, 0:T], onesf[:], YQ[:, f, 0:T], f == 0, f == 3, ["onesf", "YQ"], ["pC"])
            V("dve", "tensor_scalar", ["pB"], ["tmpA"], out=tmpA[:, 0:T], in0=pB[:, 0:T], scalar1=1.0 / 512, scalar2=None, op0=ALU.mult)
            V("dve", "tensor_tensor", ["tmpA"], ["tmpB"], out=tmpB[:, 0:T], in0=tmpA[:, 0:T], in1=tmpA[:, 0:T], op=ALU.mult)
            V("dve", "scalar_tensor_tensor", ["pC", "tmpB"], ["tmpB"], out=tmpB[:, 0:T], in0=pC[:, 0:T], scalar=1.0 / 512,
              in1=tmpB[:, 0:T], op0=ALU.mult, op1=ALU.subtract)
            V("dve", "tensor_scalar", ["tmpB"], ["tmpB"], out=tmpB[:, 0:T], in0=tmpB[:, 0:T], scalar1=EPS, scalar2=None, op0=ALU.add)
            V("act", "activation", ["tmpB"], ["tmpB"], out=tmpB[:, 0:T], in_=tmpB[:, 0:T], func=ACT.Sqrt)
            V("dve", "reciprocal", ["tmpB"], ["tmpB"], out=tmpB[:, 0:T], in_=tmpB[:, 0:T])
            for f in range(4):
                V("dve", "tensor_tensor", ["YC", "tmpA"], ["YC"], out=YC[:, f, 0:T], in0=YC[:, f, 0:T], in1=tmpA[:, 0:T], op=ALU.subtract)
                V("dve", "tensor_tensor", ["YC", "tmpB"], ["YC"], out=YC[:, f, 0:T], in0=YC[:, f, 0:T], in1=tmpB[:, 0:T], op=ALU.mult)
                V("act", "activation", ["YC", "cpar"], ["YQ"], out=YQ[:, f, 0:T], in_=YC[:, f, 0:T], func=ACT.Silu,
                  scale=cpar[:, 1, f:f + 1], bias=cpar[:, 2, f:f + 1])
                V("dve", "tensor_tensor", ["YQ", "G3"], ["G3"], out=G3[:, f, t0:t0 + T], in0=G3[:, f, t0:t0 + T],
                  in1=YQ[:, f, 0:T], op=ALU.mult)
        T = min(512, L)
        DMA("sp", brS[1, :, :, 0:L], G3[:, :, 0:L], ["G3"], ["brS"])
        stage_point(L)

        P.barrier()
        CV.reset()
        PW = CV.get([128, 4, 128], BF16)
        pcorr = CV.get([128, 4, 16])
        pscale = CV.get([128, 4])
        DMA("pool", PW, pool_w[l].rearrange("g c d -> c g d"), [], ["PW"])
        DMA("sp", pcorr.rearrange("p a b -> p (a b)"), poolcorr.partition_broadcast(128), [], ["pcorr"])
        DMA("sp", pscale, pool_scale[l].rearrange("(f p) -> p f", p=128), [], ["pscale"], allow_slow_non_contiguous=True)
        V("pool", "memset", [], ["G1"], G1[:], 0.0)
        load_w(WT, "WT", l, C_XPOOL, 512)

        def cp_(j, t0, T_, p_, pn):
            for (s_, lo_, g0_, n_) in seq_split(t0, T_):
                V("act", "activation", [pn], ["G1"], out=G1[:, j, s_ * SEGP + 8 + lo_:s_ * SEGP + 8 + lo_ + n_],
                  in_=p_[:, g0_ - t0:g0_ - t0 + n_], func=ACT.Copy)
        proj(WT, "WT", 512, L, cp_)
        load_w(WT2, "WT2", l, C_GPOOL, 512)
        proj(WT2, "WT2", 512, L, cg)
        for (s_, tl0) in [(s__, a__) for s__ in range(nseq) for a__ in range(0, Ls, Tc)]:
            t0 = s_ * Ls + tl0
            cb = s_ * SEGP + 8 + tl0
            T = Tc
            for f, w in enumerate((2, 4, 8, 16)):
                for i_, kk in enumerate(range(-(w // 2), w // 2)):
                    MM(pA[:, 0:T], identb[:], G1[:, f, cb + kk:cb + kk + T], i_ == 0, i_ == w - 1, ["identb", "G1"], ["pA"])
                V("dve", "tensor_scalar", ["pA"], ["tmpA"], out=tmpA[:, 0:T], in0=pA[:, 0:T], scalar1=1.0 / w, scalar2=None, op0=ALU.mult)
                if tl0 == 0:
                    V("dve", "tensor_tensor", ["tmpA", "pcorr"], ["tmpA"], out=tmpA[:, 0:8], in0=tmpA[:, 0:8], in1=pcorr[:, f, 0:8], op=ALU.mult)
                if tl0 + T == Ls:
                    V("dve", "tensor_tensor", ["tmpA", "pcorr"], ["tmpA"], out=tmpA[:, T - 8:T], in0=tmpA[:, T - 8:T], in1=pcorr[:, f, 8:16], op=ALU.mult)
                V("dve", "tensor_tensor", ["tmpA", "G1"], ["tmpC"], out=tmpC[:, 0:T], in0=tmpA[:, 0:T],
                  in1=G1[:, f, cb:cb + T], op=ALU.subtract)
                MM(pB[:, 0:T], PW[:, f, :], tmpC[:, 0:T], True, True, ["PW", "tmpC"], ["pB"])
                V("dve", "scalar_tensor_tensor", ["pB", "pscale", "G3"], ["G3"], out=G3[:, f, t0:t0 + T], in0=pB[:, 0:T],
                  scalar=pscale[:, f:f + 1], in1=G3[:, f, t0:t0 + T], op0=ALU.mult, op1=ALU.mult)
        T = min(512, L)
        DMA("sp", brS[2, :, :, 0:L], G3[:, :, 0:L], ["G3"], ["brS"])
        stage_point(L)

        P.barrier()
        CV.reset()
        ssm_phase(l, kind, nseq, Ls, sidx0)
        DMA("sp", brS[3, :, :, 0:L], G3[:, :, 0:L], ["G3"], ["brS"])
        stage_point(L)

        P.barrier()
        CV.reset()
        if L <= 512:
            BR = CV.get([128, 16, 512], BF16)
            WBN = [CV.get([128, 4, 1024], BF16), CV.get([128, 4, 1024], BF16)]
            ACC = CV.get([128, 8, 512])
            WO = CV.get([128, 8, 1024], BF16)
            FG = CV.get([128, D])
            for n in range(4):
                DMA("sp", BR[:, n * 4:(n + 1) * 4, 0:L], brS[n, :, :, 0:L], ["brS"], ["BR0"])
            load_w(WT, "WT", l, C_MRG, 512, 0)
            load_w(WT2, "WT2", l, C_MRG + 512, 512, 0)
            DMA("pool", WBN[0], w_br[l, 0].rearrange("(kc p) c -> p kc c", p=128), [], ["WBN0"])
            for half in range(2):
                DMA("pool", WO[:, :, half * 512:(half + 1) * 512],
                    w_out[l, :, half * 512:(half + 1) * 512].rearrange("(k p) c -> p k c", p=128), [], ["WO"])
            if l == 1:
                DMA("sp", FG, row_bc(final_g, 128), [], ["FG"])
            for n in range(4):
                WBn_, WBk = WBN[n % 2], "WBN%d" % (n % 2)
                for f in range(8):
                    Wg, Wgn = (WT, "WT") if f < 4 else (WT2, "WT2")
                    col = (f % 4) * 128
                    pg, pgn = [(pA, "pA"), (pC, "pC")][f % 2]
                    pp, ppn = [(pB, "pB"), (pD, "pD")][f % 2]
                    sg = T2s[:, (f % 2) * 512:(f % 2) * 512 + 512]
                    sgn = "sg%d" % (f % 2)
                    for k in range(8):
                        MM(pg[:, 0:L], Wg[:, k, col:col + 128], hT[:, k, 0:L], k == 0, k == 7, [Wgn, "hT"], [pgn])
                    V("act", "activation", [pgn], [sgn], out=sg[:, 0:L], in_=pg[:, 0:L], func=ACT.Sigmoid)
                    for kc in range(4):
                        MM(pp[:, 0:L], WBn_[:, kc, f * 128:(f + 1) * 128], BR[:, n * 4 + kc, 0:L], kc == 0, kc == 3, [WBk, "BR0"], [ppn])
                    ak = "ACC%d" % f
                    if n == 0:
                        V("dve", "tensor_tensor", [sgn, ppn], [ak], out=ACC[:, f, 0:L], in0=sg[:, 0:L], in1=pp[:, 0:L], op=ALU.mult)
                    else:
                        V("dve", "tensor_tensor", [sgn, ppn], [sgn], out=sg[:, 0:L], in0=sg[:, 0:L], in1=pp[:, 0:L], op=ALU.mult)
                        V("dve", "tensor_tensor", [sgn, ak], [ak], out=ACC[:, f, 0:L], in0=sg[:, 0:L], in1=ACC[:, f, 0:L], op=ALU.add)
                    if n == 3:
                        if f < 4:
                            V("act", "activation", [ak], ["G1"], out=G1[:, f, 0:L], in_=ACC[:, f, 0:L], func=ACT.Copy)
                        else:
                            V("act", "activation", [ak], ["G2"], out=G2[:, f - 4, 0:L], in_=ACC[:, f, 0:L], func=ACT.Copy)
                    if n < 3 and f == 3:
                        load_w(WT, "WT", l, C_MRG + (n + 1) * 1024, 512, 0)
                        DMA("pool", WBN[(n + 1) % 2], w_br[l, n + 1].rearrange("(kc p) c -> p kc c", p=128), [], ["WBN%d" % ((n + 1) % 2)])
                    if n < 3 and f == 7:
                        load_w(WT2, "WT2", l, C_MRG + (n + 1) * 1024 + 512, 512, 0)
        else:
            BRs = [CV.get([128, 16, 512], BF16), CV.get([128, 16, 512], BF16)]
            WBs = [CV.get([128, 16, 128], BF16), CV.get([128, 16, 128], BF16)]
            WO = CV.get([128, 8, 1024], BF16)
            FG = CV.get([128, D])
            for half in range(2):
                DMA("pool", WO[:, :, half * 512:(half + 1) * 512],
                    w_out[l, :, half * 512:(half + 1) * 512].rearrange("(k p) c -> p k c", p=128), [], ["WO"])
            if l == 1:
                DMA("sp", FG, row_bc(final_g, 128), [], ["FG"])
            brr = [0]
            for f in range(8):
                WTf, WTn = [(WT, "WT"), (WT2, "WT2")][f % 2]
                WB, WBn = WBs[f % 2], "WB%d" % (f % 2)
                for n in range(4):
                    load_w(WTf, WTn, l, C_MRG + n * 1024 + f * 128, 128, n * 128)
                    DMA("pool", WB[:, n * 4:(n + 1) * 4, :],
                        w_br[l, n, :, f * 128:(f + 1) * 128].rearrange("(kc p) c -> p kc c", p=128), [], [WBn])
                for t0 in range(0, L, T):
                    BR, BRn = BRs[brr[0] % 2], "BR%d" % (brr[0] % 2)
                    brr[0] += 1
                    for n in range(4):
                        DMA("sp", BR[:, n * 4:(n + 1) * 4, 0:T], brS[n, :, :, t0:t0 + T], ["brS"], [BRn])
                    for n in range(4):
                        pg, pgn = [(pA, "pA"), (pC, "pC")][n % 2]
                        pp, ppn = [(pB, "pB"), (pD, "pD")][n % 2]
                        sg = T2s[:, (n % 2) * 512:(n % 2) * 512 + 512]
                        sgn = "sg%d" % (n % 2)
                        for k in range(8):
                            MM(pg[:, 0:T], WTf[:, k, n * 128:(n + 1) * 128], hT[:, k, t0:t0 + T], k == 0, k == 7, [WTn, "hT"], [pgn])
                        V("act", "activation", [pgn], [sgn], out=sg[:, 0:T], in_=pg[:, 0:T], func=ACT.Sigmoid)
                        for kc in range(4):
                            MM(pp[:, 0:T], WB[:, n * 4 + kc, :], BR[:, n * 4 + kc, 0:T], kc == 0, kc == 3, [WBn, BRn], [ppn])
                        if n == 0:
                            V("dve", "tensor_tensor", [sgn, ppn], ["tmpB"], out=tmpB[:, 0:T], in0=sg[:, 0:T], in1=pp[:, 0:T], op=ALU.mult)
                        else:
                            V("dve", "tensor_tensor", [sgn, ppn], [sgn], out=sg[:, 0:T], in0=sg[:, 0:T], in1=pp[:, 0:T], op=ALU.mult)
                            V("dve", "tensor_tensor", [sgn, "tmpB"], ["tmpB"], out=tmpB[:, 0:T], in0=sg[:, 0:T], in1=tmpB[:, 0:T], op=ALU.add)
                    if f < 4:
                        V("act", "activation", ["tmpB"], ["G1"], out=G1[:, f, t0:t0 + T], in_=tmpB[:, 0:T], func=ACT.Copy)
                    else:
                        V("act", "activation", ["tmpB"], ["G2"], out=G2[:, f - 4, t0:t0 + T], in_=tmpB[:, 0:T], func=ACT.Copy)


        V("dve", "tensor_copy", ["G1"], ["G3"], out=G3[:, :, 0:L], in_=G1[:, :, 0:L]) if stage is not None else None
        stage_point(L)
        P.barrier()
        for t in range(nt):
            xb, xk = xbuf(t)
            DMA("sp", xb, xrow(x_srcs, t), ["xsrc"], [xk])
            for half in range(2):
                po_, pon = [(pA, "pA"), (pB, "pB")][half]
                for k in range(8):
                    mg = G1[:, k, t * 128:(t + 1) * 128] if k < 4 else G2[:, k - 4, t * 128:(t + 1) * 128]
                    MM(po_[:], mg, WO[:, k, half * 512:(half + 1) * 512], k == 0, k == 7, ["G1", "G2", "WO"], [pon])
                V("dve", "tensor_tensor", [pon, "GT"], ["tmpA"], out=tmpA[:], in0=po_[:], in1=GT[:, half * 512:(half + 1) * 512], op=ALU.mult)
                V("dve", "tensor_tensor", ["tmpA", xk], [xk], out=xb[:, half * 512:(half + 1) * 512], in0=tmpA[:],
                  in1=xb[:, half * 512:(half + 1) * 512], op=ALU.add)
            if l == 0:
                DMA("sp", xrow(x_dsts, t), xb, [xk], ["xdst"])
            else:
                rms_stats(xb, xk)
                V("dve", "scalar_tensor_tensor", [xk, "stat", "FG"], ["xtok2"], out=xtok2, in0=xb,
                  scalar=stat[:, 2:3], in1=FG, op0=ALU.mult, op1=ALU.mult)
                DMA("sp", xrow(x_dsts, t), xtok2, ["xtok2"], ["xdst"])
        P.barrier()

    def ssm_phase(l, kind, nseq, Ls, sidx0):
        L = nseq * Ls
        T = min(512, L)
        BTr = CV.get([128, 32, 128], BF16)
        BTi = CV.get([128, 32, 128], BF16)
        CPr = CV.get([128, 32, 128], BF16)
        CPi = CV.get([128, 32, 128], BF16)
        GW = CV.get([128, 4, 1024], BF16)
        YG = xtok2.bitcast(BF16).rearrange("p (a b) -> p a b", a=4)
        S = {}
        for nm in ("LR", "LI", "DT", "a", "th", "er", "cs", "sn", "lr", "li", "kr", "ki", "t1", "t2", "t3", "t4",
                   "cr", "ci", "q1", "q2", "q3", "q4"):
            S[nm] = CV.get([128, 32])
        ti = CV.get([128, 32], I32)
        RREG = CV.get([128, 3, 2048])
        rflat = RREG.rearrange("p a b -> p (a b)")
        Bt = [rflat[:, i * 512:(i + 1) * 512].rearrange("p (g c) -> p g c", g=32) for i in range(2)]
        Be = [rflat[:, (2 + i) * 512:(3 + i) * 512].rearrange("p (g c) -> p g c", g=32) for i in range(2)]
        Bx = rflat[:, 4 * 512:5 * 512].rearrange("p (g c) -> p g c", g=32)
        CN = [rflat[:, (5 + i) * 512:(6 + i) * 512].rearrange("p (f c) -> p f c", f=4) for i in range(2)]
        dsk = CV.get([128, 4])

        def tt(out, a, b, op, eng="dve"):
            V(eng, "tensor_tensor", ["ssmS"], ["ssmS"], out=out, in0=a, in1=b, op=op)

        for d in range(2):
            DMA("sp", S["LR"][d * 64:(d + 1) * 64, :], lam_re[l, d].rearrange("g p -> p g"), [], ["ssmS"], allow_slow_non_contiguous=True)
            DMA("sp", S["LI"][d * 64:(d + 1) * 64, :], lam_im[l, d].rearrange("g p -> p g"), [], ["ssmS"], allow_slow_non_contiguous=True)
            DMA("sp", S["DT"][d * 64:(d + 1) * 64, :], row_bc(log_dt[l, d], 64), [], ["ssmS"])
            for ri, src_ in enumerate((b_re, b_im)):
                DMA("sp", Bt[ri][d * 64:(d + 1) * 64, :, :], src_[l, d].rearrange("g p c -> p g c"), [], ["ssmS"],
                    allow_slow_non_contiguous=True)
            for ri, src_ in enumerate((c_re, c_im)):
                for f in range(4):
                    DMA("sp", CN[ri][:, f, d * 64:(d + 1) * 64], src_[l, d, 8 * f:8 * f + 8].rearrange("g c p -> (g c) p"), [], ["ssmS"])
        DMA("sp", dsk, ssm_d[l].rearrange("(f p) -> p f", p=128), [], ["ssmS"], allow_slow_non_contiguous=True)
        DMA("pool", GW, glu_w[l].rearrange("(k p) c -> p k c", p=128), [], ["GW"])
        V("act", "activation", ["ssmS"], ["ssmS"], out=S["DT"], in_=S["DT"], func=ACT.Exp)
        tt(S["a"], S["LR"], S["DT"], ALU.mult)
        tt(S["th"], S["LI"], S["DT"], ALU.mult)
        V("act", "activation", ["ssmS"], ["ssmS"], out=S["er"], in_=S["a"], func=ACT.Exp)
        TWO_PI = 2.0 * math.pi
        for nm, shift in (("sn", 0.0), ("cs", 0.25)):
            V("dve", "tensor_scalar", ["ssmS"], ["ssmS"], out=S["t1"], in0=S["th"], scalar1=1.0 / TWO_PI, scalar2=shift,
              op0=ALU.mult, op1=ALU.add)
            V("dve", "tensor_copy", ["ssmS"], ["ssmS"], out=ti, in_=S["t1"])
            V("dve", "tensor_copy", ["ssmS"], ["ssmS"], out=S["t2"], in_=ti)
            tt(S["t1"], S["t1"], S["t2"], ALU.subtract)
            V("act", "activation", ["ssmS"], ["ssmS"], out=S[nm], in_=S["t1"], func=ACT.Sin, scale=TWO_PI)
        tt(S["lr"], S["er"], S["cs"], ALU.mult)
        tt(S["li"], S["er"], S["sn"], ALU.mult)
        V("dve", "tensor_scalar", ["ssmS"], ["ssmS"], out=S["t1"], in0=S["lr"], scalar1=-1.0, scalar2=None, op0=ALU.add)
        tt(S["t2"], S["LR"], S["LR"], ALU.mult)
        tt(S["t3"], S["LI"], S["LI"], ALU.mult)
        tt(S["t2"], S["t2"], S["t3"], ALU.add)
        V("dve", "reciprocal", ["ssmS"], ["ssmS"], out=S["t2"], in_=S["t2"])
        tt(S["t3"], S["t1"], S["LR"], ALU.mult)
        tt(S["t4"], S["li"], S["LI"], ALU.mult)
        tt(S["t3"], S["t3"], S["t4"], ALU.add)
        tt(S["kr"], S["t3"], S["t2"], ALU.mult)
        tt(S["t3"], S["li"], S["LR"], ALU.mult)
        tt(S["t4"], S["t1"], S["LI"], ALU.mult)
        tt(S["t3"], S["t3"], S["t4"], ALU.subtract)
        tt(S["ki"], S["t3"], S["t2"], ALU.mult)

        def bcast_c(a):
            return AP(a.tensor, a.offset, [[a.ap[0][0], 128], [1, 32], [0, 16]])
        tt(Be[0], Bt[0], bcast_c(S["kr"]), ALU.mult)
        tt(Bx, Bt[1], bcast_c(S["ki"]), ALU.mult)
        tt(Be[0], Be[0], Bx, ALU.subtract)
        tt(Be[1], Bt[1], bcast_c(S["kr"]), ALU.mult)
        tt(Bx, Bt[0], bcast_c(S["ki"]), ALU.mult)
        tt(Be[1], Be[1], Bx, ALU.add)
        for ri, BTx in enumerate((BTr, BTi)):
            for f in range(4):
                P.op("pe", lambda e, ri=ri, f=f: e.transpose(pF[:, 0:128], Be[ri][:, 8 * f:8 * f + 8, :].rearrange("p a b -> p (a b)"), identf[:]),
                     reads=["ssmS", "identf"], writes=["pF"])
                for j in range(8):
                    V("dve", "tensor_scalar", ["pF", "bmask"], ["BT"], out=BTx[:, 8 * f + j, :], in0=pF[:, 0:128],
                      scalar1=bmask[:, j:j + 1], scalar2=None, op0=ALU.mult)
        V("pool", "memset", [], ["CP"], CPr, 0.0)
        V("pool", "memset", [], ["CP"], CPi, 0.0)
        for ri, CPx in enumerate((CPr, CPi)):
            for f in range(4):
                P.op("pe", lambda e, ri=ri, f=f: e.transpose(pF[:, 0:128], CN[ri][:, f, :], identf[:]),
                     reads=["ssmS", "identf"], writes=["pF"])
                base = CPx[:, 8 * f, 0:16]
                dg = AP(base.tensor, base.offset, [[base.ap[0][0], 128], [128 + 16, 8], [1, 16]])
                if ri == 0:
                    V("dve", "tensor_copy", ["pF"], ["CP"], out=dg, in_=pF[:, 0:128].rearrange("p (j c) -> p j c", j=8))
                else:
                    V("dve", "tensor_scalar", ["pF"], ["CP"], out=dg, in0=pF[:, 0:128].rearrange("p (j c) -> p j c", j=8),
                      scalar1=-1.0, scalar2=None, op0=ALU.mult)
        if kind == "s":
            for d in range(2):
                DMA("sp", S["cr"][d * 64:(d + 1) * 64, :], st0[l, d, 0].rearrange("g p -> p g"), ["ssmS"], ["ssmS"], allow_slow_non_contiguous=True)
                DMA("sp", S["ci"][d * 64:(d + 1) * 64, :], st0[l, d, 1].rearrange("g p -> p g"), ["ssmS"], ["ssmS"], allow_slow_non_contiguous=True)

        load_w(WT, "WT", l, C_XSSM, 512)

        def cx_(j, t0, T_, p_, pn):
            V("act", "activation", [pn], ["G1"], out=G1[:, j, t0:t0 + T_], in_=p_, func=ACT.Copy)
        proj(WT, "WT", 512, L, cx_)
        P.barrier()

        NG = 16
        TC = 128
        nC = Ls // TC
        g2f = G2[:].rearrange("p a b -> p (a b)").bitcast(F32)
        v3 = lambda a: a.rearrange("p (g j) -> p g j", g=NG)
        bu = [v3(g2f[:, 0:NG * TC]), v3(g2f[:, NG * TC:2 * NG * TC])]
        bu2 = [v3(MODT[:, 0:2, :].rearrange("p a b -> p (a b)")), v3(XT[:].rearrange("p a b -> p (a b)"))]
        busets = [(bu[0], bu[1], "bur0", "bui0"), (bu2[0], bu2[1], "bur1", "bui1")]
        w2 = WT2[:].rearrange("p a b -> p (a b)")
        hbf = [v3(w2[:, 0:NG * TC]), v3(w2[:, NG * TC:2 * NG * TC])]
        YSb = G3
        T1 = v3(WT[:].rearrange("p a b -> p (a b)").bitcast(F32))
        T2 = v3(T2s[:])
        T2i = v3(T2s[:].bitcast(I32))
        RC, RS, RHO = v3(RREG[:, 0, :]), v3(RREG[:, 1, :]), v3(RREG[:, 2, :])
        fl = lambda a: a.rearrange("p g j -> p (g j)")
        pbanks = [(pA, "pA"), (pB, "pB"), (pC, "pC"), (pD, "pD")]
        ybanks = [(pO0, "pO0"), (pO1, "pO1")]
        yrr = [0]
        for gh in range(2):
            gsl = slice(gh * NG, (gh + 1) * NG)
            ths, ers = S["th"][:, gsl], S["er"][:, gsl]
            thb = AP(ths.tensor, ths.offset, [[ths.ap[0][0], 128], [1, NG], [0, TC]])
            erb = AP(ers.tensor, ers.offset, [[ers.ap[0][0], 128], [1, NG], [0, TC]])
            jb = AP(JT[:].tensor, JT[:].offset, [[JT[:].ap[0][0], 128], [0, NG], [1, TC]])
            for tab, shift in ((RS, 0.0), (RC, 0.25)):
                V("dve", "tensor_tensor", ["ssmS", "JT", "BT", "CP"], ["RT"], out=tab, in0=thb, in1=jb, op=ALU.mult)
                V("dve", "tensor_scalar", ["RT"], ["RT"], out=tab, in0=tab, scalar1=1.0 / TWO_PI, scalar2=shift, op0=ALU.mult, op1=ALU.add)
                V("dve", "tensor_copy", ["RT"], ["T2"], out=T2i, in_=tab)
                V("dve", "tensor_copy", ["T2"], ["T1"], out=T1, in_=T2i)
                V("dve", "tensor_tensor", ["RT", "T1"], ["RT"], out=tab, in0=tab, in1=T1, op=ALU.subtract)
                V("act", "activation", ["RT"], ["RT"], out=tab, in_=tab, func=ACT.Sin, scale=TWO_PI)
            V("dve", "tensor_copy", ["ssmS", "RT"], ["RT"], out=RHO, in_=erb)
            V("dve", "memset", ["RT"], ["RT"], RHO[:, :, 0], 0.0)
            lr_, li_ = S["lr"][:, gsl], S["li"][:, gsl]
            q1, q2, q3, q4 = S["q1"][:, gsl], S["q2"][:, gsl], S["q3"][:, gsl], S["q4"][:, gsl]
            for sq_ in range(nseq):
                tokb = sq_ * Ls
                cr_, ci_ = S["cr"][:, gsl], S["ci"][:, gsl]
                if kind == "p":
                    V("dve", "memset", ["ssmS"], ["ssmS"], cr_, 0.0)
                    V("dve", "memset", ["ssmS"], ["ssmS"], ci_, 0.0)
                def emit_bu(c):
                    tf0 = tokb + c * TC
                    tb0 = tokb + (nC - 1 - c) * TC
                    st_ = busets[c % 2]
                    for ri, BTx in enumerate((BTr, BTi)):
                        bn = st_[2 + ri]
                        for q in range(4):
                            pt_, pn = pbanks[q]
                            for j in range(4):
                                g = gh * NG + 4 * q + j
                                f = g // 8
                                MM(pt_[0:64, j * TC:(j + 1) * TC], BTx[:, g, 0:64], G1[:, f, tf0:tf0 + TC], True, True, ["BT", "G1"], [pn])
                                MM(pt_[64:128, j * TC:(j + 1) * TC], BTx[:, g, 64:128], G1[:, f, tb0:tb0 + TC], True, True, ["BT", "G1"], [pn])
                            V("act", "activation", [pn], [bn], out=st_[ri][0:64, 4 * q:4 * q + 4, :],
                              in_=pt_[0:64, :].rearrange("p (g j) -> p g j", g=4), func=ACT.Copy)
                            ob = st_[ri][64:128, 4 * q:4 * q + 4, :]
                            obr = AP(ob.tensor, ob.offset + (TC - 1), [[ob.ap[0][0], 64], [TC, 4], [-1, TC]])
                            V("act", "activation", [pn], [bn], out=obr, in_=pt_[64:128, :].rearrange("p (g j) -> p g j", g=4), func=ACT.Copy)

                def emit_chain(c):
                    bur, bui, kr_, ki_ = busets[c % 2]
                    V("dve", "tensor_tensor", [kr_, "RT"], ["T1"], out=fl(T1), in0=fl(bur), in1=fl(RC), op=ALU.mult)
                    V("dve", "tensor_tensor", [ki_, "RT"], ["T2"], out=fl(T2), in0=fl(bui), in1=fl(RS), op=ALU.mult)
                    V("dve", "tensor_tensor", ["T1", "T2"], ["T1"], out=fl(T1), in0=fl(T1), in1=fl(T2), op=ALU.add)
                    V("dve", "tensor_tensor", [kr_, "RT"], [kr_], out=fl(bur), in0=fl(bur), in1=fl(RS), op=ALU.mult)
                    V("dve", "tensor_tensor", [ki_, "RT", "T2"], [ki_], out=fl(bui), in0=fl(bui), in1=fl(RC), op=ALU.mult)
                    V("dve", "tensor_tensor", [kr_, ki_], [ki_], out=fl(bui), in0=fl(bui), in1=fl(bur), op=ALU.subtract)
                    V("dve", "tensor_tensor", ["ssmS"], ["q1"], out=q1, in0=lr_, in1=cr_, op=ALU.mult)
                    V("dve", "tensor_tensor", ["ssmS"], ["q2"], out=q2, in0=li_, in1=ci_, op=ALU.mult)
                    V("dve", "tensor_tensor", ["q1", "q2"], ["q1"], out=q1, in0=q1, in1=q2, op=ALU.subtract)
                    V("dve", "tensor_tensor", ["q1", "T1"], ["T1"], out=T1[:, :, 0], in0=T1[:, :, 0], in1=q1, op=ALU.add)
                    V("dve", "tensor_tensor", ["ssmS"], ["q3"], out=q3, in0=li_, in1=cr_, op=ALU.mult)
                    V("dve", "tensor_tensor", ["ssmS"], ["q4"], out=q4, in0=lr_, in1=ci_, op=ALU.mult)
                    V("dve", "tensor_tensor", ["q3", "q4"], ["q3"], out=q3, in0=q3, in1=q4, op=ALU.add)
                    V("dve", "tensor_tensor", ["q3", ki_], [ki_], out=bui[:, :, 0], in0=bui[:, :, 0], in1=q3, op=ALU.add)
                    V("dve", "tensor_tensor_scan", ["T1", "RT", kr_], [kr_], out=fl(bur), data0=fl(RHO), data1=fl(T1), initial=0.0,
                      op0=ALU.mult, op1=ALU.add)
                    V("dve", "tensor_tensor_scan", [ki_, "RT", "T2", "T1"], ["T2"], out=fl(T2), data0=fl(RHO), data1=fl(bui), initial=0.0,
                      op0=ALU.mult, op1=ALU.add)

                def emit_chainB(c):
                    bur, bui, kr_, ki_ = busets[c % 2]
                    V("dve", "tensor_tensor", [kr_, "RT"], ["T1"], out=fl(T1), in0=fl(bur), in1=fl(RC), op=ALU.mult)
                    V("dve", "tensor_tensor", ["T2", "RT"], [ki_], out=fl(bui), in0=fl(T2), in1=fl(RS), op=ALU.mult)
                    V("dve", "tensor_tensor", ["T1", ki_], ["T1"], out=fl(T1), in0=fl(T1), in1=fl(bui), op=ALU.subtract)
                    V("dve", "tensor_tensor", [kr_, "RT"], [kr_], out=fl(bur), in0=fl(bur), in1=fl(RS), op=ALU.mult)
                    V("dve", "tensor_tensor", ["T2", "RT"], ["T2"], out=fl(T2), in0=fl(T2), in1=fl(RC), op=ALU.mult)
                    V("dve", "tensor_tensor", ["T2", kr_], ["T2"], out=fl(T2), in0=fl(T2), in1=fl(bur), op=ALU.add)
                    V("dve", "tensor_copy", ["T1"], ["ssmS"], out=cr_, in_=T1[:, :, TC - 1])
                    V("dve", "tensor_copy", ["T2"], ["ssmS"], out=ci_, in_=T2[:, :, TC - 1])
                    for ri, Hh, hn in ((0, T1, "T1"), (1, T2, "T2")):
                        V("act", "activation", [hn], ["hbf"], out=hbf[ri][0:64, :, :], in_=Hh[0:64, :, :], func=ACT.Copy)
                        ib = Hh[64:128, :, :]
                        ibr = AP(ib.tensor, ib.offset + (TC - 1), [[ib.ap[0][0], 64], [TC, NG], [-1, TC]])
                        V("act", "activation", [hn], ["hbf"], out=hbf[ri][64:128, :, :], in_=ibr, func=ACT.Copy)

                def emit_y(c):
                    tf0 = tokb + c * TC
                    tb0 = tokb + (nC - 1 - c) * TC
                    pend = []
                    for half, tk0 in ((0, tf0), (1, tb0)):
                        first = (c < nC - 1 - c)
                        pr = slice(64 * half, 64 * half + 64)
                        for f in (2 * gh, 2 * gh + 1):
                            pt_, pn = ybanks[yrr[0] % 2]
                            yrr[0] += 1
                            n_ = 0
                            for j in range(8):
                                g = 8 * f + j
                                gl = g - gh * NG
                                for ri, CPx in enumerate((CPr, CPi)):
                                    MM(pt_[:, 0:TC], CPx[pr, g, :], hbf[ri][pr, gl, :], n_ == 0, n_ == 15, ["CP", "hbf"], [pn])
                                    n_ += 1
                            yo = yrr[0] % 4
                            yt = tmpA[:, yo * TC:(yo + 1) * TC]
                            ytn = "ytmp%d" % yo
                            V("act", "activation", [pn], [ytn], out=yt, in_=pt_[:, 0:TC], func=ACT.Copy)
                            pend.append((first, f, tk0, yo, yt, ytn))
                    return pend

                def emit_ys(pend):
                    for (first, f, tk0, yo, yt, ytn) in pend:
                        if first:
                            yx = tmpB[:, (yo % 2) * TC:(yo % 2) * TC + TC]
                            yxn = "yx%d" % (yo % 2)
                            V("act", "activation", ["G1", "ssmS"], [yxn], out=yx, in_=G1[:, f, tk0:tk0 + TC],
                              func=ACT.Copy, scale=dsk[:, f:f + 1])
                            V("dve", "tensor_tensor", [yxn, ytn], ["YS", "G3"], out=YSb[:, f, tk0:tk0 + TC], in0=yx, in1=yt, op=ALU.add)
                        else:
                            V("dve", "tensor_tensor", [ytn, "YS", "G3"], ["YS", "G3"], out=YSb[:, f, tk0:tk0 + TC],
                              in0=YSb[:, f, tk0:tk0 + TC], in1=yt, op=ALU.add)

                emit_bu(0)
                pend_ = None
                for c in range(nC):
                    if c + 1 < nC:
                        emit_bu(c + 1)
                    emit_chain(c)
                    if pend_ is not None:
                        emit_ys(pend_)
                    emit_chainB(c)
                    pend_ = emit_y(c)
                emit_ys(pend_)
                if kind == "p":
                    for d in range(2):
                        DMA("sp", nst[sidx0 + sq_, l, d, 0, gsl, :].rearrange("g p -> p g"), S["cr"][d * 64:(d + 1) * 64, gsl], ["ssmS"], ["nstout"], allow_slow_non_contiguous=True)
                        DMA("sp", nst[sidx0 + sq_, l, d, 1, gsl, :].rearrange("g p -> p g"), S["ci"][d * 64:(d + 1) * 64, gsl], ["ssmS"], ["nstout"], allow_slow_non_contiguous=True)
        P.barrier()
        load_w(WT, "WT", l, C_GSSM, 512)

        def cg2(j, t0, T_, p_, pn):
            V("act", "activation", [pn], ["G1"], out=G1[:, j, t0:t0 + T_], in_=p_, func=ACT.Silu)
        proj(WT, "WT", 512, L, cg2)
        for t0 in range(0, L, T):
            V("act", "activation", ["YS", "G3"], ["YG", "xtok2"], out=YG[:, :, 0:T], in_=YSb[:, :, t0:t0 + T], func=ACT.Gelu)
            for m in range(4):
                for k in range(4):
                    MM(pA[:, 0:T], GW[:, k, m * 128:(m + 1) * 128], YG[:, k, 0:T], k == 0, k == 3, ["GW", "YG", "xtok2"], ["pA"])
                for k in range(4):
                    MM(pB[:, 0:T], GW[:, k, 512 + m * 128:512 + (m + 1) * 128], YG[:, k, 0:T], k == 0, k == 3, ["GW", "YG", "xtok2"], ["pB"])
                V("act", "activation", ["pB"], ["tmpA"], out=tmpA[:, 0:T], in_=pB[:, 0:T], func=ACT.Sigmoid)
                V("dve", "tensor_tensor", ["pA", "tmpA"], ["tmpA"], out=tmpA[:, 0:T], in0=pA[:, 0:T], in1=tmpA[:, 0:T], op=ALU.mult)
                V("dve", "tensor_tensor", ["tmpA", "G1", "YG"], ["G3"], out=G3[:, m, t0:t0 + T], in0=G1[:, m, t0:t0 + T], in1=tmpA[:, 0:T], op=ALU.mult)

    try:
      for l in range(2):
          CV.reset()
          ngb = layer_mod(l)
          set_mod(l, 0, ngb)
          chk("mod")
          srcs = [xp[0], xp[1]] if l == 0 else [x1p[0], x1p[1]]
          dsts = [x1p[0], x1p[1]] if l == 0 else [yp[0], yp[1]]
          seq_pass(l, "p", 2, LP, srcs, dsts, 0, ngb)
          CV.reset()
          ngb = CV.get([128, D])
          set_mod(l, 1, ngb)
          src = xs if l == 0 else x1s
          dst = x1s if l == 0 else ys
          seq_pass(l, "s", 1, LS, [src], [dst], 0, ngb)
    except _Stop:
        pass
    P.finish("sp")
    P.emit(st)
    st.close()
    return nc


_IN_NAMES = ["norm_g", "w_ada", "b_ada", "w_in", "attn_sink", "conv_dw", "conv_db", "conv_ln_g", "conv_ln_b",
             "pool_w", "pool_scale", "ssm_lam_re", "ssm_lam_im", "ssm_log_dt", "ssm_b_re", "ssm_b_im",
             "ssm_c_re", "ssm_c_im", "ssm_d", "ssm_glu_w", "w_br", "w_out", "final_g"]


def _rope_tables():
    t = np.arange(LS)
    row = (t // 64).astype(np.float32)
    col = (t % 64).astype(np.float32)
    inv = (10000.0 ** (-np.arange(16, dtype=np.float32) / 16)).astype(np.float32)
    ang = np.concatenate([row[:, None] * inv, col[:, None] * inv], axis=-1)
    c = np.cos(ang).T.astype(np.float32)
    s = np.sin(ang).T.astype(np.float32)
    cos64 = np.concatenate([c, c], axis=0)
    sin64 = np.concatenate([-s, s], axis=0)
    return (np.ascontiguousarray(np.concatenate([cos64, cos64], 0)),
            np.ascontiguousarray(np.concatenate([sin64, sin64], 0)))


def _pool_corr():
    pc = np.ones((4, 16), np.float32)
    Lq = 64
    for g, w in enumerate((2, 4, 8, 16)):
        for e in range(16):
            tt = e if e < 8 else Lq - 16 + e
            lo = max(tt - w // 2, 0)
            hi = min(tt + w - w // 2, Lq)
            pc[g, e] = w / float(hi - lo)
    return pc.reshape(1, 64)


_NC_CACHE = {}


def kernel(**inputs):
    f = {k: np.ascontiguousarray(np.asarray(v, dtype=np.float32)) for k, v in inputs.items()}
    if "nc" not in _NC_CACHE:
        _NC_CACHE["nc"] = build_program()
    nc = _NC_CACHE["nc"]
    rc, rs = _rope_tables()
    pc = _pool_corr()
    in_maps = []
    for c in range(8):
        sq = c % 2
        m = {n: f[n] for n in _IN_NAMES}
        m["xp"] = f["x_prompt"][2 * c:2 * c + 2]
        m["xs"] = f["x_sample"][sq]
        m["ck"] = f["cache_k"][sq].reshape(2, 256, 128)
        m["cv"] = f["cache_v"][sq].reshape(2, 256, 128)
        m["st0"] = f["state_ssm"][sq]
        m["cvec"] = np.stack([f["c_ctx"], f["c"][sq]], 0)
        m["ropec"] = rc
        m["ropes"] = rs
        m["poolcorr"] = pc
        in_maps.append({k: np.ascontiguousarray(v) for k, v in m.items()})
    res = run_bass_kernel_spmd(nc, in_maps, core_ids=list(range(8)))
    R = res.results
    y_prompt = np.concatenate([R[c]["yp"] for c in range(8)], 0).astype(np.float32)
    y_sample = np.stack([R[0]["ys"], R[1]["ys"]], 0).astype(np.float32)
    new_k = np.concatenate([R[c]["nk"] for c in range(8)], 0).reshape(16, 2, 256, 2, 64).astype(np.float32)
    new_v = np.concatenate([R[c]["nv"] for c in range(8)], 0).reshape(16, 2, 256, 2, 64).astype(np.float32)
    new_st = np.concatenate([R[c]["nst"] for c in range(8)], 0).astype(np.float32)
    return (y_prompt, y_sample, new_k, new_v, new_st)
```

```python
import math
from contextlib import ExitStack
import numpy as np
import concourse.bass as bass
import concourse.mybir as mybir
from concourse.bass_utils import run_bass_kernel_spmd
from concourse.ap import AP

F32 = mybir.dt.float32
BF16 = mybir.dt.bfloat16
I32 = mybir.dt.int32
ALU = mybir.AluOpType
ACT = mybir.ActivationFunctionType

D = 1024
NCOL = 8960
LS = 2048
LP = 256
EPS = 1e-6
NDMASLOT = 24

C_Q, C_K, C_V, C_GATT = 0, 512, 640, 768
C_ACONV, C_GCONV = 1280, 2304
C_XPOOL, C_GPOOL = 2816, 3328
C_XSSM, C_GSSM = 3840, 4352
C_MRG = 4864


class Prog:
    ENGS = ("pe", "act", "dve", "pool", "sp")

    def __init__(self, nc):
        self.nc = nc
        self.ops = []
        self.cnt = {e: 0 for e in self.ENGS}
        self.dma_n = {"sp": 0, "pool": 0}
        self.slot_uses = {}
        self.slot_last = {}
        self.lastw = {}
        self.readers = {}
        self.info = []
        self.waited = {e: {} for e in self.ENGS}
        self.barrier_vals = {}

    def _deps(self, reads, writes):
        d = set()
        for k in list(reads) + list(writes):
            if k in self.lastw:
                d.add(self.lastw[k])
        for k in writes:
            for r in self.readers.get(k, ()):
                d.add(r)
        return d

    def _commit(self, oid, reads, writes):
        for k in reads:
            self.readers.setdefault(k, []).append(oid)
        for k in writes:
            self.lastw[k] = oid
            self.readers[k] = []

    def barrier(self):
        for s, u in self.slot_uses.items():
            self.barrier_vals[("d", s)] = 16 * u
        for e in self.ENGS:
            if self.cnt[e]:
                self.barrier_vals[("c", e)] = self.cnt[e]

    def _waits_for(self, eng, deps):
        ws = {}
        for semkey, val in self.barrier_vals.items():
            if self.waited[eng].get(semkey, 0) < val:
                ws[semkey] = val
        for d in deps:
            semkey, val = self.info[d]
            if semkey == ("c", eng) and eng in self.NO_SELF_WAIT:
                continue
            if self.waited[eng].get(semkey, 0) >= val:
                continue
            ws[semkey] = max(ws.get(semkey, 0), val)
        for k, v in ws.items():
            self.waited[eng][k] = v
        return sorted(ws.items(), key=lambda kv: str(kv[0]))

    NO_SELF_WAIT = ("pe",)
    PSUM_KEYS = ("pA", "pB", "pC", "pD", "pT", "pO0", "pO1", "pF")

    def op(self, eng, fn, reads=(), writes=()):
        extra = [k for k in reads if k in self.PSUM_KEYS and k not in writes]
        if extra:
            writes = list(writes) + extra
        deps = self._deps(reads, writes)
        waits = self._waits_for(eng, deps)
        self.cnt[eng] += 1
        oid = len(self.ops)
        self.info.append((("c", eng), self.cnt[eng]))
        self.ops.append((eng, fn, waits, ("c", eng), 1))
        self._commit(oid, reads, writes)
        return oid

    def dma(self, eng, fn, reads=(), writes=()):
        slot = (eng, self.dma_n[eng] % NDMASLOT)
        self.dma_n[eng] += 1
        deps = self._deps(reads, writes)
        if self.slot_last.get(slot) is not None:
            deps.add(self.slot_last[slot])
        waits = self._waits_for(eng, deps)
        self.slot_uses[slot] = self.slot_uses.get(slot, 0) + 1
        oid = len(self.ops)
        self.info.append((("d", slot), 16 * self.slot_uses[slot]))
        self.ops.append((eng, fn, waits, ("d", slot), 16))
        self.slot_last[slot] = oid
        self._commit(oid, reads, writes)
        return oid

    def finish(self, final_eng="sp"):
        waits = []
        for s, u in self.slot_uses.items():
            waits.append((("d", s), 16 * u))
        for e in self.ENGS:
            if self.cnt[e]:
                waits.append((("c", e), self.cnt[e]))
        self.ops.append((final_eng, None, waits, None, 0))

    def emit(self, stack):
        nc = self.nc
        sems = {}
        for e in self.ENGS:
            sems[("c", e)] = stack.enter_context(nc.semaphore("c_" + e))
        for q in ("sp", "pool"):
            for s in range(NDMASLOT):
                sems[("d", (q, s))] = stack.enter_context(nc.semaphore("d_%s_%d" % (q, s)))
        block = stack.enter_context(nc.Block())
        engmap = {"pe": block.tensor, "act": block.scalar, "dve": block.vector,
                  "pool": block.gpsimd, "sp": block.sync}
        for e in self.ENGS:
            mine = [o for o in self.ops if o[0] == e]
            if not mine:
                continue

            def body(engine, mine=mine):
                for (_, fn, waits, sig, inc) in mine:
                    for (k, v) in waits:
                        engine.wait_ge(sems[k], v)
                    if fn is not None:
                        ins = fn(engine)
                        ins.then_inc(sems[sig], inc)
            engmap[e](body)


def rev_free(ap2d, n):
    return AP(ap2d.tensor, ap2d.offset + (n - 1), [[ap2d.ap[0][0], ap2d.ap[0][1]], [-1, n]])


class _Stop(Exception):
    pass


def build_program(stage=None):
    nc = bass.Bass("TRN2", target_bir_lowering=False)

    def din(name, shape):
        return nc.dram_tensor(name, shape, F32, kind="ExternalInput").ap()

    def dout(name, shape):
        return nc.dram_tensor(name, shape, F32, kind="ExternalOutput").ap()

    xp = din("xp", [2, LP, D])
    xs = din("xs", [LS, D])
    ck = din("ck", [2, 256, 128])
    cv = din("cv", [2, 256, 128])
    st0 = din("st0", [2, 2, 2, 32, 64])
    cvec = din("cvec", [2, D])
    norm_g = din("norm_g", [2, D])
    w_ada = din("w_ada", [2, D, 3 * D])
    b_ada = din("b_ada", [2, 3 * D])
    w_in = din("w_in", [2, D, NCOL])
    sink = din("attn_sink", [2, 8])
    conv_dw = din("conv_dw", [2, 31, 512])
    conv_db = din("conv_db", [2, 512])
    conv_ln_g = din("conv_ln_g", [2, 512])
    conv_ln_b = din("conv_ln_b", [2, 512])
    pool_w = din("pool_w", [2, 4, 128, 128])
    pool_scale = din("pool_scale", [2, 512])
    lam_re = din("ssm_lam_re", [2, 2, 32, 64])
    lam_im = din("ssm_lam_im", [2, 2, 32, 64])
    log_dt = din("ssm_log_dt", [2, 2, 32])
    b_re = din("ssm_b_re", [2, 2, 32, 64, 16])
    b_im = din("ssm_b_im", [2, 2, 32, 64, 16])
    c_re = din("ssm_c_re", [2, 2, 32, 16, 64])
    c_im = din("ssm_c_im", [2, 2, 32, 16, 64])
    ssm_d = din("ssm_d", [2, 512])
    glu_w = din("ssm_glu_w", [2, 512, 1024])
    w_br = din("w_br", [2, 4, 512, D])
    w_out = din("w_out", [2, D, D])
    final_g = din("final_g", [D])
    ropec = din("ropec", [128, LS])
    ropes = din("ropes", [128, LS])
    poolcorr = din("poolcorr", [1, 64])

    yp = dout("yp", [2, LP, D])
    ys = dout("ys", [LS, D])
    nk = dout("nk", [2, 2, LP, 128])
    nv = dout("nv", [2, 2, LP, 128])
    nst = dout("nst", [2, 2, 2, 2, 32, 64])
    dbg = dout("dbg", [128, 4, LS]) if stage is not None else None

    x1p = nc.dram_tensor("x1p", [2, LP, D], F32, kind="Internal").ap()
    x1s = nc.dram_tensor("x1s", [LS, D], F32, kind="Internal").ap()
    brS = nc.dram_tensor("brS", [4, 128, 4, LS], BF16, kind="Internal").ap()
    moddram = nc.dram_tensor("moddram", [2, 3 * D], F32, kind="Internal").ap()

    P = Prog(nc)
    st = ExitStack()

    def sb(name, shape, dt=F32):
        return st.enter_context(nc.sbuf_tensor(name, shape, dt))

    def ps(name, shape, dt=F32):
        return st.enter_context(nc.psum_tensor(name, shape, dt))

    def V(eng, method, reads, writes, *a, **kw):
        return P.op(eng, lambda e: getattr(e, method)(*a, **kw), reads=reads, writes=writes)

    def DMA(eng, out, in_, reads, writes, **kw):
        return P.dma(eng, lambda e: e.dma_start(out=out, in_=in_, **kw), reads=reads, writes=writes)

    def MM(out, lhsT, rhs, start, stop, reads, writes):
        return P.op("pe", lambda e: e.matmul(out, lhsT=lhsT, rhs=rhs, start=start, stop=stop),
                    reads=reads, writes=writes)

    identf = sb("identf", [128, 128])
    identb = sb("identb", [128, 128], BF16)
    onesf = sb("onesf", [128, 128])
    V("pool", "memset", [], ["identf"], identf[:], 1.0)
    V("pool", "affine_select", ["identf"], ["identf"], out=identf[:], in_=identf[:], pattern=[[-1, 128]],
      compare_op=ALU.is_equal, fill=0.0, base=0, channel_multiplier=1)
    V("dve", "tensor_copy", ["identf"], ["identb"], out=identb[:], in_=identf[:])
    V("pool", "memset", [], ["onesf"], onesf[:], 1.0)
    mprev = sb("mprev", [128, 128], BF16)
    mnext = sb("mnext", [128, 128], BF16)
    mtmp = sb("mtmp", [128, 128])
    V("pool", "memset", [], ["mtmp"], mtmp[:], 1.0)
    V("pool", "affine_select", ["mtmp"], ["mtmp"], out=mtmp[:], in_=mtmp[:], pattern=[[-1, 128]],
      compare_op=ALU.is_ge, fill=0.0, base=0, channel_multiplier=1)
    V("dve", "tensor_copy", ["mtmp"], ["mprev"], out=mprev[:], in_=mtmp[:])
    V("pool", "memset", ["mtmp"], ["mtmp"], mtmp[:], 1.0)
    V("pool", "affine_select", ["mtmp"], ["mtmp"], out=mtmp[:], in_=mtmp[:], pattern=[[1, 128]],
      compare_op=ALU.is_ge, fill=0.0, base=0, channel_multiplier=-1)
    V("dve", "tensor_copy", ["mtmp"], ["mnext"], out=mnext[:], in_=mtmp[:])
    bmask = sb("bmask", [128, 8])
    V("pool", "memset", [], ["bmask"], bmask[:], 1.0)
    V("pool", "affine_select", ["bmask"], ["bmask"], out=bmask[:], in_=bmask[:], pattern=[[-16, 8]],
      compare_op=ALU.is_ge, fill=0.0, base=0, channel_multiplier=1)
    V("pool", "affine_select", ["bmask"], ["bmask"], out=bmask[:], in_=bmask[:], pattern=[[16, 8]],
      compare_op=ALU.is_ge, fill=0.0, base=15, channel_multiplier=-1)

    pA = ps("pA", [128, 512])
    pB = ps("pB", [128, 512])
    pC = ps("pC", [128, 512])
    pD = ps("pD", [128, 512])
    pT = ps("pT", [128, 1024], BF16)
    pO0 = ps("pO0", [128, 512])
    pO1 = ps("pO1", [128, 512])
    pF = ps("pF", [128, 512])

    hT = sb("hT", [128, 8, LS], BF16)
    G1 = sb("G1", [128, 4, LS + 64], BF16)
    G2 = sb("G2", [128, 4, LS], BF16)
    G3 = sb("G3", [128, 4, LS], BF16)
    WT = sb("WT", [128, 8, 512], BF16)
    WT2 = sb("WT2", [128, 8, 512], BF16)
    XT = sb("XT", [128, 2, D])
    xtok = XT[:, 0, :]
    xtok2 = XT[:, 1, :]
    hb = sb("hb", [128, D], BF16)
    MODT = sb("MODT", [128, 3, D])
    SC = MODT[:, 0, :]
    SH = MODT[:, 1, :]
    GT = MODT[:, 2, :]
    tmpA = sb("tmpA", [128, 512])
    tmpB = sb("tmpB", [128, 512])
    tmpC = sb("tmpC", [128, 512], BF16)
    stat = sb("stat", [128, 8])
    T2s = sb("T2s", [128, 2048])
    JT = sb("JT", [128, 128])
    V("pool", "iota", [], ["JT"], JT[:], pattern=[[1, 128]], base=0, channel_multiplier=0,
      allow_small_or_imprecise_dtypes=True)
    csil = sb("csil", [128, 8, 2], BF16)
    cld = sb("cld", [128, 8, 2])
    ARB = 69632
    ARENA = sb("ARENA", [128, ARB // 2], BF16)

    class Carver:
        def __init__(self):
            self.off = 0

        def reset(self):
            self.off = 0

        def get(self, shape, dt=F32):
            n = 1
            for s_ in shape[1:]:
                n *= s_
            nb = n * (2 if dt == BF16 else 4)
            nb = (nb + 63) // 64 * 64
            assert self.off + nb <= ARB, ("arena overflow", self.off, nb)
            a = ARENA[0:shape[0], self.off // 2:(self.off + nb) // 2]
            self.off += nb
            if dt == BF16:
                a = a[:, 0:n]
            else:
                a = a.bitcast(dt)[:, 0:n]
            if len(shape) == 3:
                a = a.rearrange("p (a b) -> p a b", a=shape[1])
            elif len(shape) == 4:
                a = a.rearrange("p (a b c) -> p a b c", a=shape[1], b=shape[2])
            return a
    CV = Carver()

    for v_ in range(2):
        DMA("sp", cld[:, :, v_], cvec[v_].rearrange("(k p) -> p k", p=128), [], ["cld"],
            allow_slow_non_contiguous=True)
    V("act", "activation", ["cld"], ["csil"], out=csil[:], in_=cld[:], func=ACT.Silu)

    def row_bc(vec_ap, n):
        return vec_ap.rearrange("(o d) -> o d", o=1).partition_broadcast(n)

    def layer_mod(l):
        ngb = CV.get([128, D])
        for blk in range(6):
            Wm, Wmn = [(WT, "WT"), (WT2, "WT2")][blk % 2]
            DMA("pool", Wm[:], w_ada[l, :, blk * 512:(blk + 1) * 512].rearrange("(k p) c -> p k c", p=128),
                [], [Wmn])
            for k in range(8):
                MM(pA[0:2, :], csil[:, k, :], Wm[:, k, :], k == 0, k == 7, [Wmn, "csil"], ["pA"])
            DMA("sp", tmpA[0:2, :], row_bc(b_ada[l, blk * 512:(blk + 1) * 512], 2), [], ["tmpA"])
            V("dve", "tensor_add", ["tmpA", "pA"], ["tmpB"], out=tmpB[0:2, :], in0=pA[0:2, :], in1=tmpA[0:2, :])
            DMA("sp", moddram[:, blk * 512:(blk + 1) * 512], tmpB[0:2, :], ["tmpB"], ["moddram"])
        return ngb

    def set_mod(l, v, ngb):
        DMA("sp", ngb, row_bc(norm_g[l], 128), [], ["ngb"])
        DMA("sp", SH, row_bc(moddram[v, 0:D], 128), ["moddram"], ["SH"])
        DMA("sp", SC, row_bc(moddram[v, D:2 * D], 128), ["moddram"], ["SC"])
        DMA("sp", GT, row_bc(moddram[v, 2 * D:3 * D], 128), ["moddram"], ["GT"])
        V("dve", "scalar_tensor_tensor", ["SC", "ngb"], ["SC"], out=SC, in0=SC, scalar=1.0, in1=ngb,
          op0=ALU.add, op1=ALU.mult)

    def rms_stats(xb=None, xk="xtok"):
        if xb is None:
            xb = xtok
        V("act", "activation", [xk], ["xtok2", "stat"], out=xtok2, in_=xb, func=ACT.Square,
          accum_out=stat[:, 0:1])
        V("dve", "tensor_scalar", ["stat"], ["stat"], out=stat[:, 1:2], in0=stat[:, 0:1], scalar1=1.0 / D,
          scalar2=EPS, op0=ALU.mult, op1=ALU.add)
        V("act", "activation", ["stat"], ["stat"], out=stat[:, 3:4], in_=stat[:, 1:2], func=ACT.Sqrt)
        V("dve", "reciprocal", ["stat"], ["stat"], out=stat[:, 2:3], in_=stat[:, 3:4])

    def xbuf(t):
        return (xtok, "xtok") if t % 2 == 0 else (T2s[:, 0:D], "xB")

    def load_norm(src_rows, t=0):
        xb, xk = xbuf(t)
        DMA("sp", xb, src_rows, ["xsrc"], [xk])
        rms_stats(xb, xk)
        V("dve", "scalar_tensor_tensor", [xk, "stat", "SC"], ["xtok2"], out=xtok2, in0=xb,
          scalar=stat[:, 2:3], in1=SC, op0=ALU.mult, op1=ALU.mult)
        V("dve", "tensor_add", ["xtok2", "SH"], ["hb"], out=hb[:], in0=xtok2, in1=SH)

    def to_hT(t0):
        for k in range(8):
            P.op("pe", lambda e, k=k: e.transpose(pT[:, k * 128:(k + 1) * 128], hb[:, k * 128:(k + 1) * 128], identb[:]),
                 reads=["hb", "identb"], writes=["pT"])
        V("act", "activation", ["pT"], ["hT"], out=hT[:, :, t0:t0 + 128],
          in_=pT[:].rearrange("p (k t) -> p k t", k=8), func=ACT.Copy)

    def load_w(dst, dname, l, col0, ncols, dcol0=0):
        DMA("pool", dst[:, :, dcol0:dcol0 + ncols],
            w_in[l, :, col0:col0 + ncols].rearrange("(k p) c -> p k c", p=128), [], [dname])

    def load_w_rot(dst, dname, l, col0, nheads, dcol0=0):
        for k in range(8):
            for two in range(2):
                src = w_in[l, k * 128:(k + 1) * 128, col0:col0 + nheads * 64].rearrange(
                    "p (h two j) -> p h two j", two=2, j=32)[:, :, 1 - two, :]
                dd = dst[:, k, dcol0:dcol0 + nheads * 64].rearrange("p (h two j) -> p h two j", two=2, j=32)[:, :, two, :]
                DMA("pool", dd, src, [], [dname])

    psum_rr = [0]

    def proj(wt, wname, ncols, L, consume, ps_list=None, col_off=0):
        if ps_list is None:
            ps_list = [(pA, "pA"), (pB, "pB"), (pC, "pC"), (pD, "pD")]
        T = min(512, L)
        for t0 in range(0, L, T):
            for j in range(ncols // 128):
                pt_, pn = ps_list[psum_rr[0] % len(ps_list)]
                psum_rr[0] += 1
                for k in range(8):
                    MM(pt_[:, 0:T], wt[:, k, col_off + j * 128:col_off + (j + 1) * 128], hT[:, k, t0:t0 + T],
                       k == 0, k == 7, [wname, "hT"], [pn])
                consume(j, t0, T, pt_[:, 0:T], pn)

    nstage = [0]

    def chk(label):
        if stage == label:
            raise _Stop()

    def stage_point(L):
        nstage[0] += 1
        if stage is not None and nstage[0] == stage:
            DMA("pool", dbg[:, :, 0:L], G3[:, :, 0:L], ["G3", "G1", "G2"], [])
            raise _Stop()

    def seq_pass(l, kind, nseq, Ls, x_srcs, x_dsts, sidx0, ngb):
        L = nseq * Ls

        def xrow(aps, t):
            s_ = (t * 128) // Ls
            lo = t * 128 - s_ * Ls
            return aps[s_][lo:lo + 128, :]

        def seq_split(t0, T_):
            out = []
            a = t0
            while a < t0 + T_:
                s_ = a // Ls
                e_ = min(t0 + T_, (s_ + 1) * Ls)
                out.append((s_, a - s_ * Ls, a, e_ - a))
                a = e_
            return out
        SEG = Ls + 32
        SEGP = Ls + 16
        Tc = min(512, Ls)
        nt = L // 128
        T = min(512, L)
        for t in range(nt):
            load_norm(xrow(x_srcs, t), t)
            to_hT(t * 128)

        chk("ph1")
        CV.reset()
        vaug = CV.get([128, 16, 2, 65], BF16)
        ETall = CV.get([128, 5, 512], BF16)
        obf = CV.get([128, 8, 64], BF16)
        kvtok = CV.get([128, 256])
        esink = CV.get([128, 8])
        den = CV.get([128, 8])
        V("pool", "memset", [], ["vaug", "ngb"], vaug[:], 1.0)
        DMA("sp", esink, row_bc(sink[l], 128), [], ["esink"])
        V("act", "activation", ["esink"], ["esink"], out=esink, in_=esink, func=ACT.Exp)
        V("pool", "memset", [], ["G2"], G2[:], 0.0)
        if kind == "s":
            ROC = CV.get([128, LS])
            ROS = CV.get([128, LS])
            CKv = CV.get([128, 4, 256], BF16)
            cvaug = CV.get([128, 2, 2, 65], BF16)
            cktok = CV.get([128, 2, 2, 128])
            DMA("sp", ROC, ropec, [], ["ROC"])
            DMA("sp", ROS, ropes, [], ["ROS"])
            V("pool", "memset", [], ["CKv"], CKv[:], 0.0)
            V("pool", "memset", [], ["cvaug"], cvaug[:], 1.0)
            for stl in range(2):
                DMA("sp", cktok[:, 0, stl, :], ck[l, stl * 128:(stl + 1) * 128, :], [], ["cktok"])
                DMA("sp", cktok[:, 1, stl, 0:64], ck[l, stl * 128:(stl + 1) * 128, 64:128], [], ["cktok"])
                DMA("sp", cktok[:, 1, stl, 64:128], ck[l, stl * 128:(stl + 1) * 128, 0:64], [], ["cktok"])
                DMA("pool", cvaug[:, stl, :, 0:64],
                    cv[l, stl * 128:(stl + 1) * 128, :].rearrange("s (k d) -> s k d", k=2), ["cvaug"], ["cvaug"])
            for stl in range(2):
                for var in range(2):
                    P.op("pe", lambda e, stl=stl, var=var: e.transpose(pF[:, 0:128], cktok[:, var, stl, :], identf[:]),
                         reads=["cktok", "identf"], writes=["pF"])
                    if var == 0:
                        V("dve", "tensor_copy", ["pF"], ["CKv"], out=CKv[0:64, 0, stl * 128:(stl + 1) * 128], in_=pF[0:64, 0:128])
                        V("dve", "tensor_copy", ["pF"], ["CKv"], out=CKv[64:128, 3, stl * 128:(stl + 1) * 128], in_=pF[64:128, 0:128])
                    else:
                        V("dve", "tensor_copy", ["pF"], ["CKv"], out=CKv[0:64, 2, stl * 128:(stl + 1) * 128], in_=pF[0:64, 0:128])
                        V("dve", "tensor_copy", ["pF"], ["CKv"], out=CKv[64:128, 1, stl * 128:(stl + 1) * 128], in_=pF[64:128, 0:128])

        chk("attsetup")
        load_w(WT, "WT", l, C_Q, 512)
        if kind == "p":
            def cq(j, t0, T_, p_, pn):
                V("act", "activation", [pn], ["G1"], out=G1[:, j, t0:t0 + T_], in_=p_, func=ACT.Copy)
            proj(WT, "WT", 512, L, cq)
        else:
            load_w_rot(WT2, "WT2", l, C_Q, 8)
            for t0 in range(0, L, T):
                for j in range(4):
                    for k in range(8):
                        MM(pA[:, 0:T], WT[:, k, j * 128:(j + 1) * 128], hT[:, k, t0:t0 + T], k == 0, k == 7, ["WT", "hT"], ["pA"])
                    for k in range(8):
                        MM(pB[:, 0:T], WT2[:, k, j * 128:(j + 1) * 128], hT[:, k, t0:t0 + T], k == 0, k == 7, ["WT2", "hT"], ["pB"])
                    V("dve", "tensor_tensor", ["pA", "ROC"], ["tmpA"], out=tmpA[:, 0:T], in0=pA[:, 0:T], in1=ROC[:, t0:t0 + T], op=ALU.mult)
                    V("dve", "tensor_tensor", ["pB", "ROS"], ["tmpB"], out=tmpB[:, 0:T], in0=pB[:, 0:T], in1=ROS[:, t0:t0 + T], op=ALU.mult)
                    V("dve", "tensor_tensor", ["tmpA", "tmpB"], ["G1"], out=G1[:, j, t0:t0 + T], in0=tmpA[:, 0:T], in1=tmpB[:, 0:T], op=ALU.add)
        chk("q")
        load_w(WT, "WT", l, C_K, 128, 0)
        load_w(WT, "WT", l, C_K + 64, 64, 128)
        load_w(WT, "WT", l, C_K, 64, 192)
        if kind == "s":
            load_w_rot(WT2, "WT2", l, C_K, 2, 0)
            load_w_rot(WT2, "WT2", l, C_K + 64, 1, 128)
            load_w_rot(WT2, "WT2", l, C_K, 1, 192)
        for t0 in range(0, L, T):
            for var in range(2):
                for k in range(8):
                    MM(pA[:, 0:T], WT[:, k, var * 128:(var + 1) * 128], hT[:, k, t0:t0 + T], k == 0, k == 7, ["WT", "hT"], ["pA"])
                if kind == "s":
                    for k in range(8):
                        MM(pB[:, 0:T], WT2[:, k, var * 128:(var + 1) * 128], hT[:, k, t0:t0 + T], k == 0, k == 7, ["WT2", "hT"], ["pB"])
                    V("dve", "tensor_tensor", ["pA", "ROC"], ["tmpA"], out=tmpA[:, 0:T], in0=pA[:, 0:T], in1=ROC[:, t0:t0 + T], op=ALU.mult)
                    V("dve", "tensor_tensor", ["pB", "ROS"], ["tmpB"], out=tmpB[:, 0:T], in0=pB[:, 0:T], in1=ROS[:, t0:t0 + T], op=ALU.mult)
                    V("dve", "tensor_tensor", ["tmpA", "tmpB"], ["tmpA"], out=tmpA[:, 0:T], in0=tmpA[:, 0:T], in1=tmpB[:, 0:T], op=ALU.add)
                    src, sn = tmpA, "tmpA"
                else:
                    src, sn = pA, "pA"
                lo_slot, hi_slot = (0, 3) if var == 0 else (2, 1)
                V("dve", "tensor_copy", [sn], ["G2"], out=G2[0:64, lo_slot, t0:t0 + T], in_=src[0:64, 0:T])
                V("dve", "tensor_copy", [sn], ["G2"], out=G2[64:128, hi_slot, t0:t0 + T], in_=src[64:128, 0:T])
        chk("kfm")
        load_w(WT2, "WT2", l, C_K, 256)
        for t in range(nt):
            for k in range(8):
                MM(pB[:, 0:256], hT[:, k, t * 128:(t + 1) * 128], WT2[:, k, 0:256], k == 0, k == 7, ["hT", "WT2"], ["pB"])
            V("dve", "tensor_copy", ["pB"], ["vaug"], out=vaug[:, t, :, 0:64],
              in_=pB[:, 128:256].rearrange("p (k d) -> p k d", k=2))
            if kind == "p":
                V("dve", "tensor_copy", ["pB"], ["kvtok"], out=kvtok, in_=pB[:, 0:256])
                s_ = (t * 128) // Ls
                lo_ = t * 128 - s_ * Ls
                DMA("sp", nk[sidx0 + s_, l, lo_:lo_ + 128, :], kvtok[:, 0:128], ["kvtok"], [])
                DMA("sp", nv[sidx0 + s_, l, lo_:lo_ + 128, :], kvtok[:, 128:256], ["kvtok"], [])
        chk("kvtok")
        load_w(WT, "WT", l, C_GATT, 512)

        def cg(j, t0, T_, p_, pn):
            V("act", "activation", [pn], ["G3"], out=G3[:, j, t0:t0 + T_], in_=p_, func=ACT.Silu)
        proj(WT, "WT", 512, L, cg)

        chk("gate")
        rr = [0]
        for qb in range(nt):
            q0 = qb * 128
            chunks = []
            if kind == "p":
                s_ = q0 // Ls
                for c in range(s_ * Ls // 128, (s_ + 1) * Ls // 128):
                    chunks.append(("G2", c * 128, ("v", c), None))
            else:
                for c in range(2):
                    chunks.append(("CK", c * 128, ("c", c), None))
                if qb > 0:
                    chunks.append(("G2", (qb - 1) * 128, ("v", qb - 1), "prev"))
                chunks.append(("G2", qb * 128, ("v", qb), None))
                if qb < nt - 1:
                    chunks.append(("G2", (qb + 1) * 128, ("v", qb + 1), "next"))
            for kv in range(2):
                pO, pOn = (pO0, "pO0") if kv == 0 else (pO1, "pO1")
                for ci, (ksrc, koff, vsel, msk) in enumerate(chunks):
                    pS, pSn = [(pC, "pC"), (pD, "pD")][rr[0] % 2]
                    et = ETall[:, ci, :]
                    etn = "ET%d" % ci
                    rr[0] += 1
                    for slot in range(4):
                        h = 4 * kv + slot
                        j, half = h // 2, h % 2
                        if ksrc == "G2":
                            lhs = G2[:, 2 * kv + half, koff:koff + 128]
                            lname = "G2"
                        else:
                            lhs = CKv[:, 2 * kv + half, koff:koff + 128]
                            lname = "CKv"
                        MM(pS[:, slot * 128:(slot + 1) * 128], lhs, G1[:, j, q0:q0 + 128], True, True,
                           [lname, "G1"], [pSn])
                    V("act", "activation", [pSn], [etn], out=et, in_=pS[:], func=ACT.Exp, scale=0.125)
                    if msk is not None:
                        m_ = mprev if msk == "prev" else mnext
                        mb = AP(m_[:].tensor, m_[:].offset, [[m_[:].ap[0][0], 128], [0, 4], [1, 128]])
                        V("dve", "tensor_tensor", [etn, "mprev", "mnext"], [etn],
                          out=et.rearrange("p (s q) -> p s q", s=4), in0=et.rearrange("p (s q) -> p s q", s=4),
                          in1=mb, op=ALU.mult)
                for slot in range(4):
                    for ci, (ksrc, koff, vsel, msk) in enumerate(chunks):
                        if vsel[0] == "v":
                            rhs = vaug[:, vsel[1], kv, :]
                            rn = "vaug"
                        else:
                            rhs = cvaug[:, vsel[1], kv, :]
                            rn = "cvaug"
                        MM(pO[:, slot * 65:(slot + 1) * 65], ETall[:, ci, slot * 128:(slot + 1) * 128], rhs,
                           ci == 0, ci == len(chunks) - 1, ["ET%d" % ci, rn], [pOn])
                pOv = pO[:, 0:260].rearrange("p (s e) -> p s e", s=4)
                V("dve", "tensor_tensor", [pOn, "esink"], ["den"], out=den[:, 4 * kv:4 * kv + 4],
                  in0=pOv[:, :, 64], in1=esink[:, 4 * kv:4 * kv + 4], op=ALU.add)
                V("dve", "reciprocal", ["den"], ["den"], out=den[:, 4 * kv:4 * kv + 4], in_=den[:, 4 * kv:4 * kv + 4])
                dsl = den[:, 4 * kv:4 * kv + 4]
                db = AP(dsl.tensor, dsl.offset, [[dsl.ap[0][0], 128], [1, 4], [0, 64]])
                V("dve", "tensor_tensor", [pOn, "den"], ["obf"], out=obf[:, 4 * kv:4 * kv + 4, :],
                  in0=pOv[:, :, 0:64], in1=db, op=ALU.mult)
            for j in range(4):
                P.op("pe", lambda e, j=j: e.transpose(pT[:, j * 128:(j + 1) * 128],
                                                      obf[:, 2 * j:2 * j + 2, :].rearrange("p a b -> p (a b)"), identb[:]),
                     reads=["obf", "identb"], writes=["pT"])
            V("dve", "tensor_tensor", ["pT", "G3"], ["G3"], out=G3[:, :, q0:q0 + 128], in0=G3[:, :, q0:q0 + 128],
              in1=pT[:, 0:512].rearrange("p (j t) -> p j t", j=4), op=ALU.mult)
        DMA("sp", brS[0, :, :, 0:L], G3[:, :, 0:L], ["G3"], ["brS"])
        stage_point(L)

        P.barrier()
        CV.reset()
        DG = CV.get([128, 4, 31, 128], BF16)
        YC = CV.get([128, 4, 512])
        YQ = CV.get([128, 4, 512])
        cw = CV.get([128, 4, 31])
        cpar = CV.get([128, 3, 4])
        for f in range(4):
            DMA("sp", cw[:, f, :], conv_dw[l, :, f * 128:(f + 1) * 128].rearrange("k p -> p k"), [], ["cw"],
                allow_slow_non_contiguous=True)
        for i_, src_ in enumerate((conv_db, conv_ln_g, conv_ln_b)):
            DMA("sp", cpar[:, i_, :], src_[l].rearrange("(f p) -> p f", p=128), [], ["cpar"], allow_slow_non_contiguous=True)
        for f in range(4):
            for k in range(31):
                V("dve", "tensor_scalar", ["identf", "cw"], ["DG"], out=DG[:, f, k, :], in0=identf[:],
                  scalar1=cw[:, f, k:k + 1], scalar2=None, op0=ALU.mult)
        V("pool", "memset", [], ["G1"], G1[:], 0.0)
        load_w(WT, "WT", l, C_ACONV + 512, 512)

        def cs(j, t0, T_, p_, pn):
            V("act", "activation", [pn], ["G2"], out=G2[:, j, t0:t0 + T_], in_=p_, func=ACT.Sigmoid)
        proj(WT, "WT", 512, L, cs)
        load_w(WT2, "WT2", l, C_ACONV, 512)

        def cx(j, t0, T_, p_, pn):
            for (s_, lo_, g0_, n_) in seq_split(t0, T_):
                V("dve", "tensor_tensor", [pn, "G2"], ["G1"], out=G1[:, j, s_ * SEG + 16 + lo_:s_ * SEG + 16 + lo_ + n_],
                  in0=p_[:, g0_ - t0:g0_ - t0 + n_], in1=G2[:, j, g0_:g0_ + n_], op=ALU.mult)
        proj(WT2, "WT2", 512, L, cx)
        load_w(WT, "WT", l, C_GCONV, 512)
        proj(WT, "WT", 512, L, cg)
        for (s_, tl0) in [(s__, a__) for s__ in range(nseq) for a__ in range(0, Ls, Tc)]:
            t0 = s_ * Ls + tl0
            cb = s_ * SEG + tl0
            T = Tc
            for f in range(4):
                for k in range(31):
                    MM(pA[:, 0:T], DG[:, f, k, :], G1[:, f, cb + k + 1:cb + k + 1 + T], k == 0, k == 30, ["DG", "G1"], ["pA"])
                V("act", "activation", ["pA", "cpar"], ["YC"], out=YC[:, f, 0:T], in_=pA[:, 0:T], func=ACT.Identity,
                  bias=cpar[:, 0, f:f + 1])
                V("dve", "tensor_tensor", ["YC"], ["YQ"], out=YQ[:, f, 0:T], in0=YC[:, f, 0:T], in1=YC[:, f, 0:T], op=ALU.mult)
            for f in range(4):
                MM(pB[:, 0:T], onesf[:], YC[:, f, 0:T], f == 0, f == 3, ["onesf", "YC"], ["pB"])
            for f in range(4):
                MM(pC[:, 0:T], onesf[:], YQ[:, f, 0:T], f == 0, f == 3, ["onesf", "YQ"], ["pC"])
            V("dve", "tensor_scalar", ["pB"], ["tmpA"], out=tmpA[:, 0:T], in0=pB[:, 0:T], scalar1=1.0 / 512, scalar2=None, op0=ALU.mult)
            V("dve", "tensor_tensor", ["tmpA"], ["tmpB"], out=tmpB[:, 0:T], in0=tmpA[:, 0:T], in1=tmpA[:, 0:T], op=ALU.mult)
            V("dve", "scalar_tensor_tensor", ["pC", "tmpB"], ["tmpB"], out=tmpB[:, 0:T], in0=pC[:, 0:T], scalar=1.0 / 512,
              in1=tmpB[:, 0:T], op0=ALU.mult, op1=ALU.subtract)
            V("dve", "tensor_scalar", ["tmpB"], ["tmpB"], out=tmpB[:, 0:T], in0=tmpB[:, 0:T], scalar1=EPS, scalar2=None, op0=ALU.add)
            V("act", "activation", ["tmpB"], ["tmpB"], out=tmpB[:, 0:T], in_=tmpB[:, 0:T], func=ACT.Sqrt)
            V("dve", "reciprocal", ["tmpB"], ["tmpB"], out=tmpB[:, 0:T], in_=tmpB[:, 0:T])
            for f in range(4):
                V("dve", "tensor_tensor", ["YC", "tmpA"], ["YC"], out=YC[:, f, 0:T], in0=YC[:, f, 0:T], in1=tmpA[:, 0:T], op=ALU.subtract)
                V("dve", "tensor_tensor", ["YC", "tmpB"], ["YC"], out=YC[:, f, 0:T], in0=YC[:, f, 0:T], in1=tmpB[:, 0:T], op=ALU.mult)
                V("act", "activation", ["YC", "cpar"], ["YQ"], out=YQ[:, f, 0:T], in_=YC[:, f, 0:T], func=ACT.Silu,
                  scale=cpar[:, 1, f:f + 1], bias=cpar[:, 2, f:f + 1])
                V("dve", "tensor_tensor", ["YQ", "G3"], ["G3"], out=G3[:, f, t0:t0 + T], in0=G3[:, f, t0:t0 + T],
                  in1=YQ[:, f, 0:T], op=ALU.mult)
        T = min(512, L)
        DMA("sp", brS[1, :, :, 0:L], G3[:, :, 0:L], ["G3"], ["brS"])
        stage_point(L)

        CV.reset()
        PW = CV.get([128, 4, 128], BF16)
        pcorr = CV.get([128, 4, 16])
        pscale = CV.get([128, 4])
        DMA("pool", PW, pool_w[l].rearrange("g c d -> c g d"), [], ["PW", "DG"])
        DMA("sp", pcorr.rearrange("p a b -> p (a b)"), poolcorr.partition_broadcast(128), [], ["pcorr", "DG"])
        DMA("sp", pscale, pool_scale[l].rearrange("(f p) -> p f", p=128), [], ["pscale", "DG"], allow_slow_non_contiguous=True)
        V("pool", "memset", [], ["G1"], G1[:], 0.0)
        load_w(WT, "WT", l, C_XPOOL, 512)

        def cp_(j, t0, T_, p_, pn):
            for (s_, lo_, g0_, n_) in seq_split(t0, T_):
                V("act", "activation", [pn], ["G1"], out=G1[:, j, s_ * SEGP + 8 + lo_:s_ * SEGP + 8 + lo_ + n_],
                  in_=p_[:, g0_ - t0:g0_ - t0 + n_], func=ACT.Copy)
        proj(WT, "WT", 512, L, cp_)
        load_w(WT2, "WT2", l, C_GPOOL, 512)
        proj(WT2, "WT2", 512, L, cg)
        for (s_, tl0) in [(s__, a__) for s__ in range(nseq) for a__ in range(0, Ls, Tc)]:
            t0 = s_ * Ls + tl0
            cb = s_ * SEGP + 8 + tl0
            T = Tc
            for f, w in enumerate((2, 4, 8, 16)):
                for i_, kk in enumerate(range(-(w // 2), w // 2)):
                    MM(pA[:, 0:T], identb[:], G1[:, f, cb + kk:cb + kk + T], i_ == 0, i_ == w - 1, ["identb", "G1"], ["pA"])
                V("dve", "tensor_scalar", ["pA"], ["tmpA"], out=tmpA[:, 0:T], in0=pA[:, 0:T], scalar1=1.0 / w, scalar2=None, op0=ALU.mult)
                if tl0 == 0:
                    V("dve", "tensor_tensor", ["tmpA", "pcorr"], ["tmpA"], out=tmpA[:, 0:8], in0=tmpA[:, 0:8], in1=pcorr[:, f, 0:8], op=ALU.mult)
                if tl0 + T == Ls:
                    V("dve", "tensor_tensor", ["tmpA", "pcorr"], ["tmpA"], out=tmpA[:, T - 8:T], in0=tmpA[:, T - 8:T], in1=pcorr[:, f, 8:16], op=ALU.mult)
                V("dve", "tensor_tensor", ["tmpA", "G1"], ["tmpC"], out=tmpC[:, 0:T], in0=tmpA[:, 0:T],
                  in1=G1[:, f, cb:cb + T], op=ALU.subtract)
                MM(pB[:, 0:T], PW[:, f, :], tmpC[:, 0:T], True, True, ["PW", "tmpC"], ["pB"])
                V("dve", "scalar_tensor_tensor", ["pB", "pscale", "G3"], ["G3"], out=G3[:, f, t0:t0 + T], in0=pB[:, 0:T],
                  scalar=pscale[:, f:f + 1], in1=G3[:, f, t0:t0 + T], op0=ALU.mult, op1=ALU.mult)
        T = min(512, L)
        DMA("sp", brS[2, :, :, 0:L], G3[:, :, 0:L], ["G3"], ["brS"])
        stage_point(L)

        P.barrier()
        CV.reset()
        ssm_phase(l, kind, nseq, Ls, sidx0)
        DMA("sp", brS[3, :, :, 0:L], G3[:, :, 0:L], ["G3"], ["brS"])
        stage_point(L)

        P.barrier()
        CV.reset()
        if L <= 512:
            BR = CV.get([128, 16, 512], BF16)
            WBN = [CV.get([128, 4, 1024], BF16), CV.get([128, 4, 1024], BF16)]
            ACC = CV.get([128, 8, 512])
            WO = CV.get([128, 8, 1024], BF16)
            FG = CV.get([128, D])
            for n in range(4):
                DMA("sp", BR[:, n * 4:(n + 1) * 4, 0:L], brS[n, :, :, 0:L], ["brS"], ["BR0"])
            load_w(WT, "WT", l, C_MRG, 512, 0)
            load_w(WT2, "WT2", l, C_MRG + 512, 512, 0)
            DMA("pool", WBN[0], w_br[l, 0].rearrange("(kc p) c -> p kc c", p=128), [], ["WBN0"])
            for half in range(2):
                DMA("pool", WO[:, :, half * 512:(half + 1) * 512],
                    w_out[l, :, half * 512:(half + 1) * 512].rearrange("(k p) c -> p k c", p=128), [], ["WO"])
            if l == 1:
                DMA("sp", FG, row_bc(final_g, 128), [], ["FG"])
            for n in range(4):
                WBn_, WBk = WBN[n % 2], "WBN%d" % (n % 2)
                for f in range(8):
                    Wg, Wgn = (WT, "WT") if f < 4 else (WT2, "WT2")
                    col = (f % 4) * 128
                    pg, pgn = [(pA, "pA"), (pC, "pC")][f % 2]
                    pp, ppn = [(pB, "pB"), (pD, "pD")][f % 2]
                    sg = T2s[:, (f % 2) * 512:(f % 2) * 512 + 512]
                    sgn = "sg%d" % (f % 2)
                    for k in range(8):
                        MM(pg[:, 0:L], Wg[:, k, col:col + 128], hT[:, k, 0:L], k == 0, k == 7, [Wgn, "hT"], [pgn])
                    V("act", "activation", [pgn], [sgn], out=sg[:, 0:L], in_=pg[:, 0:L], func=ACT.Sigmoid)
                    for kc in range(4):
                        MM(pp[:, 0:L], WBn_[:, kc, f * 128:(f + 1) * 128], BR[:, n * 4 + kc, 0:L], kc == 0, kc == 3, [WBk, "BR0"], [ppn])
                    ak = "ACC%d" % f
                    if n == 0:
                        V("dve", "tensor_tensor", [sgn, ppn], [ak], out=ACC[:, f, 0:L], in0=sg[:, 0:L], in1=pp[:, 0:L], op=ALU.mult)
                    else:
                        V("dve", "tensor_tensor", [sgn, ppn], [sgn], out=sg[:, 0:L], in0=sg[:, 0:L], in1=pp[:, 0:L], op=ALU.mult)
                        V("dve", "tensor_tensor", [sgn, ak], [ak], out=ACC[:, f, 0:L], in0=sg[:, 0:L], in1=ACC[:, f, 0:L], op=ALU.add)
                    if n == 3:
                        if f < 4:
                            V("act", "activation", [ak], ["G1"], out=G1[:, f, 0:L], in_=ACC[:, f, 0:L], func=ACT.Copy)
                        else:
                            V("act", "activation", [ak], ["G2"], out=G2[:, f - 4, 0:L], in_=ACC[:, f, 0:L], func=ACT.Copy)
                    if n < 3 and f == 3:
                        load_w(WT, "WT", l, C_MRG + (n + 1) * 1024, 512, 0)
                        DMA("pool", WBN[(n + 1) % 2], w_br[l, n + 1].rearrange("(kc p) c -> p kc c", p=128), [], ["WBN%d" % ((n + 1) % 2)])
                    if n < 3 and f == 7:
                        load_w(WT2, "WT2", l, C_MRG + (n + 1) * 1024 + 512, 512, 0)
        else:
            BRs = [CV.get([128, 16, 512], BF16), CV.get([128, 16, 512], BF16)]
            WBs = [CV.get([128, 16, 128], BF16), CV.get([128, 16, 128], BF16)]
            WO = CV.get([128, 8, 1024], BF16)
            FG = CV.get([128, D])
            for half in range(2):
                DMA("pool", WO[:, :, half * 512:(half + 1) * 512],
                    w_out[l, :, half * 512:(half + 1) * 512].rearrange("(k p) c -> p k c", p=128), [], ["WO"])
            if l == 1:
                DMA("sp", FG, row_bc(final_g, 128), [], ["FG"])
            brr = [0]
            for f in range(8):
                WTf, WTn = [(WT, "WT"), (WT2, "WT2")][f % 2]
                WB, WBn = WBs[f % 2], "WB%d" % (f % 2)
                for n in range(4):
                    load_w(WTf, WTn, l, C_MRG + n * 1024 + f * 128, 128, n * 128)
                    DMA("pool", WB[:, n * 4:(n + 1) * 4, :],
                        w_br[l, n, :, f * 128:(f + 1) * 128].rearrange("(kc p) c -> p kc c", p=128), [], [WBn])
                for t0 in range(0, L, T):
                    BR, BRn = BRs[brr[0] % 2], "BR%d" % (brr[0] % 2)
                    brr[0] += 1
                    for n in range(4):
                        DMA("sp", BR[:, n * 4:(n + 1) * 4, 0:T], brS[n, :, :, t0:t0 + T], ["brS"], [BRn])
                    for n in range(4):
                        pg, pgn = [(pA, "pA"), (pC, "pC")][n % 2]
                        pp, ppn = [(pB, "pB"), (pD, "pD")][n % 2]
                        sg = T2s[:, (n % 2) * 512:(n % 2) * 512 + 512]
                        sgn = "sg%d" % (n % 2)
                        for k in range(8):
                            MM(pg[:, 0:T], WTf[:, k, n * 128:(n + 1) * 128], hT[:, k, t0:t0 + T], k == 0, k == 7, [WTn, "hT"], [pgn])
                        V("act", "activation", [pgn], [sgn], out=sg[:, 0:T], in_=pg[:, 0:T], func=ACT.Sigmoid)
                        for kc in range(4):
                            MM(pp[:, 0:T], WB[:, n * 4 + kc, :], BR[:, n * 4 + kc, 0:T], kc == 0, kc == 3, [WBn, BRn], [ppn])
                        if n == 0:
                            V("dve", "tensor_tensor", [sgn, ppn], ["tmpB"], out=tmpB[:, 0:T], in0=sg[:, 0:T], in1=pp[:, 0:T], op=ALU.mult)
                        else:
                            V("dve", "tensor_tensor", [sgn, ppn], [sgn], out=sg[:, 0:T], in0=sg[:, 0:T], in1=pp[:, 0:T], op=ALU.mult)
                            V("dve", "tensor_tensor", [sgn, "tmpB"], ["tmpB"], out=tmpB[:, 0:T], in0=sg[:, 0:T], in1=tmpB[:, 0:T], op=ALU.add)
                    if f < 4:
                        V("act", "activation", ["tmpB"], ["G1"], out=G1[:, f, t0:t0 + T], in_=tmpB[:, 0:T], func=ACT.Copy)
                    else:
                        V("act", "activation", ["tmpB"], ["G2"], out=G2[:, f - 4, t0:t0 + T], in_=tmpB[:, 0:T], func=ACT.Copy)


        V("dve", "tensor_copy", ["G1"], ["G3"], out=G3[:, :, 0:L], in_=G1[:, :, 0:L]) if stage is not None else None
        stage_point(L)
        for t in range(nt):
            xb, xk = xbuf(t)
            DMA("sp", xb, xrow(x_srcs, t), ["xsrc"], [xk] + (["sg0", "sg1"] if xk == "xB" else []))
            for half in range(2):
                po_, pon = [(pA, "pA"), (pB, "pB")][half]
                for k in range(8):
                    mg = G1[:, k, t * 128:(t + 1) * 128] if k < 4 else G2[:, k - 4, t * 128:(t + 1) * 128]
                    MM(po_[:], mg, WO[:, k, half * 512:(half + 1) * 512], k == 0, k == 7, ["G1", "G2", "WO"], [pon])
                V("dve", "tensor_tensor", [pon, "GT"], ["tmpA"], out=tmpA[:], in0=po_[:], in1=GT[:, half * 512:(half + 1) * 512], op=ALU.mult)
                V("dve", "tensor_tensor", ["tmpA", xk], [xk], out=xb[:, half * 512:(half + 1) * 512], in0=tmpA[:],
                  in1=xb[:, half * 512:(half + 1) * 512], op=ALU.add)
            if l == 0:
                DMA("sp", xrow(x_dsts, t), xb, [xk], ["xdst"])
            else:
                rms_stats(xb, xk)
                V("dve", "scalar_tensor_tensor", [xk, "stat", "FG"], ["xtok2"], out=xtok2, in0=xb,
                  scalar=stat[:, 2:3], in1=FG, op0=ALU.mult, op1=ALU.mult)
                DMA("sp", xrow(x_dsts, t), xtok2, ["xtok2"], ["xdst"])
        P.barrier()

    def ssm_phase(l, kind, nseq, Ls, sidx0):
        L = nseq * Ls
        T = min(512, L)
        BTr = CV.get([128, 32, 128], BF16)
        BTi = CV.get([128, 32, 128], BF16)
        CPr = CV.get([128, 32, 128], BF16)
        CPi = CV.get([128, 32, 128], BF16)
        GW = CV.get([128, 4, 1024], BF16)
        YG = xtok2.bitcast(BF16).rearrange("p (a b) -> p a b", a=4)
        S = {}
        for nm in ("LR", "LI", "DT", "a", "th", "er", "cs", "sn", "lr", "li", "kr", "ki", "t1", "t2", "t3", "t4",
                   "cr", "ci", "q1", "q2", "q3", "q4"):
            S[nm] = CV.get([128, 32])
        ti = CV.get([128, 32], I32)
        RREG = CV.get([128, 3, 2048])
        rflat = RREG.rearrange("p a b -> p (a b)")
        Bt = [rflat[:, i * 512:(i + 1) * 512].rearrange("p (g c) -> p g c", g=32) for i in range(2)]
        Be = [rflat[:, (2 + i) * 512:(3 + i) * 512].rearrange("p (g c) -> p g c", g=32) for i in range(2)]
        Bx = rflat[:, 4 * 512:5 * 512].rearrange("p (g c) -> p g c", g=32)
        CN = [rflat[:, (5 + i) * 512:(6 + i) * 512].rearrange("p (f c) -> p f c", f=4) for i in range(2)]
        dsk = CV.get([128, 4])

        def tt(out, a, b, op, eng="dve"):
            V(eng, "tensor_tensor", ["ssmS"], ["ssmS"], out=out, in0=a, in1=b, op=op)

        for d in range(2):
            DMA("sp", S["LR"][d * 64:(d + 1) * 64, :], lam_re[l, d].rearrange("g p -> p g"), [], ["ssmS"], allow_slow_non_contiguous=True)
            DMA("sp", S["LI"][d * 64:(d + 1) * 64, :], lam_im[l, d].rearrange("g p -> p g"), [], ["ssmS"], allow_slow_non_contiguous=True)
            DMA("sp", S["DT"][d * 64:(d + 1) * 64, :], row_bc(log_dt[l, d], 64), [], ["ssmS"])
            for ri, src_ in enumerate((b_re, b_im)):
                DMA("sp", Bt[ri][d * 64:(d + 1) * 64, :, :], src_[l, d].rearrange("g p c -> p g c"), [], ["ssmS"],
                    allow_slow_non_contiguous=True)
            for ri, src_ in enumerate((c_re, c_im)):
                for f in range(4):
                    DMA("sp", CN[ri][:, f, d * 64:(d + 1) * 64], src_[l, d, 8 * f:8 * f + 8].rearrange("g c p -> (g c) p"), [], ["ssmS"])
        DMA("sp", dsk, ssm_d[l].rearrange("(f p) -> p f", p=128), [], ["ssmS"], allow_slow_non_contiguous=True)
        DMA("pool", GW, glu_w[l].rearrange("(k p) c -> p k c", p=128), [], ["GW"])
        V("act", "activation", ["ssmS"], ["ssmS"], out=S["DT"], in_=S["DT"], func=ACT.Exp)
        tt(S["a"], S["LR"], S["DT"], ALU.mult)
        tt(S["th"], S["LI"], S["DT"], ALU.mult)
        V("act", "activation", ["ssmS"], ["ssmS"], out=S["er"], in_=S["a"], func=ACT.Exp)
        TWO_PI = 2.0 * math.pi
        for nm, shift in (("sn", 0.0), ("cs", 0.25)):
            V("dve", "tensor_scalar", ["ssmS"], ["ssmS"], out=S["t1"], in0=S["th"], scalar1=1.0 / TWO_PI, scalar2=shift,
              op0=ALU.mult, op1=ALU.add)
            V("dve", "tensor_copy", ["ssmS"], ["ssmS"], out=ti, in_=S["t1"])
            V("dve", "tensor_copy", ["ssmS"], ["ssmS"], out=S["t2"], in_=ti)
            tt(S["t1"], S["t1"], S["t2"], ALU.subtract)
            V("act", "activation", ["ssmS"], ["ssmS"], out=S[nm], in_=S["t1"], func=ACT.Sin, scale=TWO_PI)
        tt(S["lr"], S["er"], S["cs"], ALU.mult)
        tt(S["li"], S["er"], S["sn"], ALU.mult)
        V("dve", "tensor_scalar", ["ssmS"], ["ssmS"], out=S["t1"], in0=S["lr"], scalar1=-1.0, scalar2=None, op0=ALU.add)
        tt(S["t2"], S["LR"], S["LR"], ALU.mult)
        tt(S["t3"], S["LI"], S["LI"], ALU.mult)
        tt(S["t2"], S["t2"], S["t3"], ALU.add)
        V("dve", "reciprocal", ["ssmS"], ["ssmS"], out=S["t2"], in_=S["t2"])
        tt(S["t3"], S["t1"], S["LR"], ALU.mult)
        tt(S["t4"], S["li"], S["LI"], ALU.mult)
        tt(S["t3"], S["t3"], S["t4"], ALU.add)
        tt(S["kr"], S["t3"], S["t2"], ALU.mult)
        tt(S["t3"], S["li"], S["LR"], ALU.mult)
        tt(S["t4"], S["t1"], S["LI"], ALU.mult)
        tt(S["t3"], S["t3"], S["t4"], ALU.subtract)
        tt(S["ki"], S["t3"], S["t2"], ALU.mult)

        def bcast_c(a):
            return AP(a.tensor, a.offset, [[a.ap[0][0], 128], [1, 32], [0, 16]])
        tt(Be[0], Bt[0], bcast_c(S["kr"]), ALU.mult)
        tt(Bx, Bt[1], bcast_c(S["ki"]), ALU.mult)
        tt(Be[0], Be[0], Bx, ALU.subtract)
        tt(Be[1], Bt[1], bcast_c(S["kr"]), ALU.mult)
        tt(Bx, Bt[0], bcast_c(S["ki"]), ALU.mult)
        tt(Be[1], Be[1], Bx, ALU.add)
        for ri, BTx in enumerate((BTr, BTi)):
            for f in range(4):
                P.op("pe", lambda e, ri=ri, f=f: e.transpose(pF[:, 0:128], Be[ri][:, 8 * f:8 * f + 8, :].rearrange("p a b -> p (a b)"), identf[:]),
                     reads=["ssmS", "identf"], writes=["pF"])
                for j in range(8):
                    V("dve", "tensor_scalar", ["pF", "bmask"], ["BT"], out=BTx[:, 8 * f + j, :], in0=pF[:, 0:128],
                      scalar1=bmask[:, j:j + 1], scalar2=None, op0=ALU.mult)
        V("pool", "memset", [], ["CP"], CPr, 0.0)
        V("pool", "memset", [], ["CP"], CPi, 0.0)
        for ri, CPx in enumerate((CPr, CPi)):
            for f in range(4):
                P.op("pe", lambda e, ri=ri, f=f: e.transpose(pF[:, 0:128], CN[ri][:, f, :], identf[:]),
                     reads=["ssmS", "identf"], writes=["pF"])
                base = CPx[:, 8 * f, 0:16]
                dg = AP(base.tensor, base.offset, [[base.ap[0][0], 128], [128 + 16, 8], [1, 16]])
                if ri == 0:
                    V("dve", "tensor_copy", ["pF"], ["CP"], out=dg, in_=pF[:, 0:128].rearrange("p (j c) -> p j c", j=8))
                else:
                    V("dve", "tensor_scalar", ["pF"], ["CP"], out=dg, in0=pF[:, 0:128].rearrange("p (j c) -> p j c", j=8),
                      scalar1=-1.0, scalar2=None, op0=ALU.mult)
        if kind == "s":
            for d in range(2):
                DMA("sp", S["cr"][d * 64:(d + 1) * 64, :], st0[l, d, 0].rearrange("g p -> p g"), ["ssmS"], ["ssmS"], allow_slow_non_contiguous=True)
                DMA("sp", S["ci"][d * 64:(d + 1) * 64, :], st0[l, d, 1].rearrange("g p -> p g"), ["ssmS"], ["ssmS"], allow_slow_non_contiguous=True)

        load_w(WT, "WT", l, C_XSSM, 512)

        def cx_(j, t0, T_, p_, pn):
            V("act", "activation", [pn], ["G1"], out=G1[:, j, t0:t0 + T_], in_=p_, func=ACT.Copy)
        proj(WT, "WT", 512, L, cx_)
        P.barrier()

        NG = 16
        TC = 128
        nC = Ls // TC
        g2f = G2[:].rearrange("p a b -> p (a b)").bitcast(F32)
        v3 = lambda a: a.rearrange("p (g j) -> p g j", g=NG)
        bu = [v3(g2f[:, 0:NG * TC]), v3(g2f[:, NG * TC:2 * NG * TC])]
        bu2 = [v3(MODT[:, 0:2, :].rearrange("p a b -> p (a b)")), v3(XT[:].rearrange("p a b -> p (a b)"))]
        busets = [(bu[0], bu[1], "bur0", "bui0"), (bu2[0], bu2[1], "bur1", "bui1")]
        w2 = WT2[:].rearrange("p a b -> p (a b)")
        hbf = [v3(w2[:, 0:NG * TC]), v3(w2[:, NG * TC:2 * NG * TC])]
        YSb = G3
        T1 = v3(WT[:].rearrange("p a b -> p (a b)").bitcast(F32))
        T2 = v3(T2s[:])
        T2i = v3(T2s[:].bitcast(I32))
        RC, RS, RHO = v3(RREG[:, 0, :]), v3(RREG[:, 1, :]), v3(RREG[:, 2, :])
        fl = lambda a: a.rearrange("p g j -> p (g j)")
        pbanks = [(pA, "pA"), (pB, "pB"), (pC, "pC"), (pD, "pD")]
        ybanks = [(pO0, "pO0"), (pO1, "pO1")]
        yrr = [0]
        for gh in range(2):
            gsl = slice(gh * NG, (gh + 1) * NG)
            ths, ers = S["th"][:, gsl], S["er"][:, gsl]
            thb = AP(ths.tensor, ths.offset, [[ths.ap[0][0], 128], [1, NG], [0, TC]])
            erb = AP(ers.tensor, ers.offset, [[ers.ap[0][0], 128], [1, NG], [0, TC]])
            jb = AP(JT[:].tensor, JT[:].offset, [[JT[:].ap[0][0], 128], [0, NG], [1, TC]])
            for tab, shift in ((RS, 0.0), (RC, 0.25)):
                V("dve", "tensor_tensor", ["ssmS", "JT", "BT", "CP"], ["RT"], out=tab, in0=thb, in1=jb, op=ALU.mult)
                V("dve", "tensor_scalar", ["RT"], ["RT"], out=tab, in0=tab, scalar1=1.0 / TWO_PI, scalar2=shift, op0=ALU.mult, op1=ALU.add)
                V("dve", "tensor_copy", ["RT"], ["T2"], out=T2i, in_=tab)
                V("dve", "tensor_copy", ["T2"], ["T1"], out=T1, in_=T2i)
                V("dve", "tensor_tensor", ["RT", "T1"], ["RT"], out=tab, in0=tab, in1=T1, op=ALU.subtract)
                V("act", "activation", ["RT"], ["RT"], out=tab, in_=tab, func=ACT.Sin, scale=TWO_PI)
            V("dve", "tensor_copy", ["ssmS", "RT"], ["RT"], out=RHO, in_=erb)
            V("dve", "memset", ["RT"], ["RT"], RHO[:, :, 0], 0.0)
            lr_, li_ = S["lr"][:, gsl], S["li"][:, gsl]
            q1, q2, q3, q4 = S["q1"][:, gsl], S["q2"][:, gsl], S["q3"][:, gsl], S["q4"][:, gsl]
            for sq_ in range(nseq):
                tokb = sq_ * Ls
                cr_, ci_ = S["cr"][:, gsl], S["ci"][:, gsl]
                if kind == "p":
                    V("dve", "memset", ["ssmS"], ["ssmS"], cr_, 0.0)
                    V("dve", "memset", ["ssmS"], ["ssmS"], ci_, 0.0)
                def emit_bu(c):
                    tf0 = tokb + c * TC
                    tb0 = tokb + (nC - 1 - c) * TC
                    st_ = busets[c % 2]
                    for ri, BTx in enumerate((BTr, BTi)):
                        bn = st_[2 + ri]
                        for q in range(4):
                            pt_, pn = pbanks[q]
                            for j in range(4):
                                g = gh * NG + 4 * q + j
                                f = g // 8
                                MM(pt_[0:64, j * TC:(j + 1) * TC], BTx[:, g, 0:64], G1[:, f, tf0:tf0 + TC], True, True, ["BT", "G1"], [pn])
                                MM(pt_[64:128, j * TC:(j + 1) * TC], BTx[:, g, 64:128], G1[:, f, tb0:tb0 + TC], True, True, ["BT", "G1"], [pn])
                            V("act", "activation", [pn], [bn], out=st_[ri][0:64, 4 * q:4 * q + 4, :],
                              in_=pt_[0:64, :].rearrange("p (g j) -> p g j", g=4), func=ACT.Copy)
                            ob = st_[ri][64:128, 4 * q:4 * q + 4, :]
                            obr = AP(ob.tensor, ob.offset + (TC - 1), [[ob.ap[0][0], 64], [TC, 4], [-1, TC]])
                            V("act", "activation", [pn], [bn], out=obr, in_=pt_[64:128, :].rearrange("p (g j) -> p g j", g=4), func=ACT.Copy)

                def emit_chain(c):
                    bur, bui, kr_, ki_ = busets[c % 2]
                    V("dve", "tensor_tensor", [kr_, "RT"], ["T1"], out=fl(T1), in0=fl(bur), in1=fl(RC), op=ALU.mult)
                    V("dve", "tensor_tensor", [ki_, "RT"], ["T2"], out=fl(T2), in0=fl(bui), in1=fl(RS), op=ALU.mult)
                    V("dve", "tensor_tensor", ["T1", "T2"], ["T1"], out=fl(T1), in0=fl(T1), in1=fl(T2), op=ALU.add)
                    V("dve", "tensor_tensor", [kr_, "RT"], [kr_], out=fl(bur), in0=fl(bur), in1=fl(RS), op=ALU.mult)
                    V("dve", "tensor_tensor", [ki_, "RT", "T2"], [ki_], out=fl(bui), in0=fl(bui), in1=fl(RC), op=ALU.mult)
                    V("dve", "tensor_tensor", [kr_, ki_], [ki_], out=fl(bui), in0=fl(bui), in1=fl(bur), op=ALU.subtract)
                    V("dve", "tensor_tensor", ["ssmS"], ["q1"], out=q1, in0=lr_, in1=cr_, op=ALU.mult)
                    V("dve", "tensor_tensor", ["ssmS"], ["q2"], out=q2, in0=li_, in1=ci_, op=ALU.mult)
                    V("dve", "tensor_tensor", ["q1", "q2"], ["q1"], out=q1, in0=q1, in1=q2, op=ALU.subtract)
                    V("dve", "tensor_tensor", ["q1", "T1"], ["T1"], out=T1[:, :, 0], in0=T1[:, :, 0], in1=q1, op=ALU.add)
                    V("dve", "tensor_tensor", ["ssmS"], ["q3"], out=q3, in0=li_, in1=cr_, op=ALU.mult)
                    V("dve", "tensor_tensor", ["ssmS"], ["q4"], out=q4, in0=lr_, in1=ci_, op=ALU.mult)
                    V("dve", "tensor_tensor", ["q3", "q4"], ["q3"], out=q3, in0=q3, in1=q4, op=ALU.add)
                    V("dve", "tensor_tensor", ["q3", ki_], [ki_], out=bui[:, :, 0], in0=bui[:, :, 0], in1=q3, op=ALU.add)
                    V("dve", "tensor_tensor_scan", ["T1", "RT", kr_], [kr_], out=fl(bur), data0=fl(RHO), data1=fl(T1), initial=0.0,
                      op0=ALU.mult, op1=ALU.add)
                    V("dve", "tensor_tensor_scan", [ki_, "RT", "T2", "T1"], ["T2"], out=fl(T2), data0=fl(RHO), data1=fl(bui), initial=0.0,
                      op0=ALU.mult, op1=ALU.add)

                def emit_chainB(c):
                    bur, bui, kr_, ki_ = busets[c % 2]
                    V("dve", "tensor_tensor", [kr_, "RT"], ["T1"], out=fl(T1), in0=fl(bur), in1=fl(RC), op=ALU.mult)
                    V("dve", "tensor_tensor", ["T2", "RT"], [ki_], out=fl(bui), in0=fl(T2), in1=fl(RS), op=ALU.mult)
                    V("dve", "tensor_tensor", ["T1", ki_], ["T1"], out=fl(T1), in0=fl(T1), in1=fl(bui), op=ALU.subtract)
                    V("dve", "tensor_tensor", [kr_, "RT"], [kr_], out=fl(bur), in0=fl(bur), in1=fl(RS), op=ALU.mult)
                    V("dve", "tensor_tensor", ["T2", "RT"], ["T2"], out=fl(T2), in0=fl(T2), in1=fl(RC), op=ALU.mult)
                    V("dve", "tensor_tensor", ["T2", kr_], ["T2"], out=fl(T2), in0=fl(T2), in1=fl(bur), op=ALU.add)
                    V("dve", "tensor_copy", ["T1"], ["ssmS"], out=cr_, in_=T1[:, :, TC - 1])
                    V("dve", "tensor_copy", ["T2"], ["ssmS"], out=ci_, in_=T2[:, :, TC - 1])
                    for ri, Hh, hn in ((0, T1, "T1"), (1, T2, "T2")):
                        V("act", "activation", [hn], ["hbf"], out=hbf[ri][0:64, :, :], in_=Hh[0:64, :, :], func=ACT.Copy)
                        ib = Hh[64:128, :, :]
                        ibr = AP(ib.tensor, ib.offset + (TC - 1), [[ib.ap[0][0], 64], [TC, NG], [-1, TC]])
                        V("act", "activation", [hn], ["hbf"], out=hbf[ri][64:128, :, :], in_=ibr, func=ACT.Copy)

                def emit_y(c):
                    tf0 = tokb + c * TC
                    tb0 = tokb + (nC - 1 - c) * TC
                    pend = []
                    for half, tk0 in ((0, tf0), (1, tb0)):
                        first = (c < nC - 1 - c)
                        pr = slice(64 * half, 64 * half + 64)
                        for f in (2 * gh, 2 * gh + 1):
                            pt_, pn = ybanks[yrr[0] % 2]
                            yrr[0] += 1
                            n_ = 0
                            for j in range(8):
                                g = 8 * f + j
                                gl = g - gh * NG
                                for ri, CPx in enumerate((CPr, CPi)):
                                    MM(pt_[:, 0:TC], CPx[pr, g, :], hbf[ri][pr, gl, :], n_ == 0, n_ == 15, ["CP", "hbf"], [pn])
                                    n_ += 1
                            yo = yrr[0] % 4
                            yt = tmpA[:, yo * TC:(yo + 1) * TC]
                            ytn = "ytmp%d" % yo
                            V("act", "activation", [pn], [ytn], out=yt, in_=pt_[:, 0:TC], func=ACT.Copy)
                            pend.append((first, f, tk0, yo, yt, ytn))
                    return pend

                def emit_ys(pend):
                    for (first, f, tk0, yo, yt, ytn) in pend:
                        if first:
                            yx = tmpB[:, (yo % 2) * TC:(yo % 2) * TC + TC]
                            yxn = "yx%d" % (yo % 2)
                            V("act", "activation", ["G1", "ssmS"], [yxn], out=yx, in_=G1[:, f, tk0:tk0 + TC],
                              func=ACT.Copy, scale=dsk[:, f:f + 1])
                            V("dve", "tensor_tensor", [yxn, ytn], ["YS", "G3"], out=YSb[:, f, tk0:tk0 + TC], in0=yx, in1=yt, op=ALU.add)
                        else:
                            V("dve", "tensor_tensor", [ytn, "YS", "G3"], ["YS", "G3"], out=YSb[:, f, tk0:tk0 + TC],
                              in0=YSb[:, f, tk0:tk0 + TC], in1=yt, op=ALU.add)

                emit_bu(0)
                pend_ = None
                for c in range(nC):
                    if c + 1 < nC:
                        emit_bu(c + 1)
                    emit_chain(c)
                    if pend_ is not None:
                        emit_ys(pend_)
                    emit_chainB(c)
                    pend_ = emit_y(c)
                emit_ys(pend_)
                if kind == "p":
                    for d in range(2):
                        DMA("sp", nst[sidx0 + sq_, l, d, 0, gsl, :].rearrange("g p -> p g"), S["cr"][d * 64:(d + 1) * 64, gsl], ["ssmS"], ["nstout"], allow_slow_non_contiguous=True)
                        DMA("sp", nst[sidx0 + sq_, l, d, 1, gsl, :].rearrange("g p -> p g"), S["ci"][d * 64:(d + 1) * 64, gsl], ["ssmS"], ["nstout"], allow_slow_non_contiguous=True)
        P.barrier()
        load_w(WT, "WT", l, C_GSSM, 512)

        def cg2(j, t0, T_, p_, pn):
            V("act", "activation", [pn], ["G1"], out=G1[:, j, t0:t0 + T_], in_=p_, func=ACT.Silu)
        proj(WT, "WT", 512, L, cg2)
        for t0 in range(0, L, T):
            V("act", "activation", ["YS", "G3"], ["YG", "xtok2"], out=YG[:, :, 0:T], in_=YSb[:, :, t0:t0 + T], func=ACT.Gelu)
            for m in range(4):
                for k in range(4):
                    MM(pA[:, 0:T], GW[:, k, m * 128:(m + 1) * 128], YG[:, k, 0:T], k == 0, k == 3, ["GW", "YG", "xtok2"], ["pA"])
                for k in range(4):
                    MM(pB[:, 0:T], GW[:, k, 512 + m * 128:512 + (m + 1) * 128], YG[:, k, 0:T], k == 0, k == 3, ["GW", "YG", "xtok2"], ["pB"])
                V("act", "activation", ["pB"], ["tmpA"], out=tmpA[:, 0:T], in_=pB[:, 0:T], func=ACT.Sigmoid)
                V("dve", "tensor_tensor", ["pA", "tmpA"], ["tmpA"], out=tmpA[:, 0:T], in0=pA[:, 0:T], in1=tmpA[:, 0:T], op=ALU.mult)
                V("dve", "tensor_tensor", ["tmpA", "G1", "YG"], ["G3"], out=G3[:, m, t0:t0 + T], in0=G1[:, m, t0:t0 + T], in1=tmpA[:, 0:T], op=ALU.mult)

    try:
      for l in range(2):
          CV.reset()
          ngb = layer_mod(l)
          set_mod(l, 0, ngb)
          chk("mod")
          srcs = [xp[0], xp[1]] if l == 0 else [x1p[0], x1p[1]]
          dsts = [x1p[0], x1p[1]] if l == 0 else [yp[0], yp[1]]
          seq_pass(l, "p", 2, LP, srcs, dsts, 0, ngb)
          CV.reset()
          ngb = CV.get([128, D])
          set_mod(l, 1, ngb)
          src = xs if l == 0 else x1s
          dst = x1s if l == 0 else ys
          seq_pass(l, "s", 1, LS, [src], [dst], 0, ngb)
    except _Stop:
        pass
    P.finish("sp")
    P.emit(st)
    st.close()
    return nc


_IN_NAMES = ["norm_g", "w_ada", "b_ada", "w_in", "attn_sink", "conv_dw", "conv_db", "conv_ln_g", "conv_ln_b",
             "pool_w", "pool_scale", "ssm_lam_re", "ssm_lam_im", "ssm_log_dt", "ssm_b_re", "ssm_b_im",
             "ssm_c_re", "ssm_c_im", "ssm_d", "ssm_glu_w", "w_br", "w_out", "final_g"]


def _rope_tables():
    t = np.arange(LS)
    row = (t // 64).astype(np.float32)
    col = (t % 64).astype(np.float32)
    inv = (10000.0 ** (-np.arange(16, dtype=np.float32) / 16)).astype(np.float32)
    ang = np.concatenate([row[:, None] * inv, col[:, None] * inv], axis=-1)
    c = np.cos(ang).T.astype(np.float32)
    s = np.sin(ang).T.astype(np.float32)
    cos64 = np.concatenate([c, c], axis=0)
    sin64 = np.concatenate([-s, s], axis=0)
    return (np.ascontiguousarray(np.concatenate([cos64, cos64], 0)),
            np.ascontiguousarray(np.concatenate([sin64, sin64], 0)))


def _pool_corr():
    pc = np.ones((4, 16), np.float32)
    Lq = 64
    for g, w in enumerate((2, 4, 8, 16)):
        for e in range(16):
            tt = e if e < 8 else Lq - 16 + e
            lo = max(tt - w // 2, 0)
            hi = min(tt + w - w // 2, Lq)
            pc[g, e] = w / float(hi - lo)
    return pc.reshape(1, 64)


_NC_CACHE = {}


def kernel(**inputs):
    f = {k: np.ascontiguousarray(np.asarray(v, dtype=np.float32)) for k, v in inputs.items()}
    if "nc" not in _NC_CACHE:
        _NC_CACHE["nc"] = build_program()
    nc = _NC_CACHE["nc"]
    rc, rs = _rope_tables()
    pc = _pool_corr()
    in_maps = []
    for c in range(8):
        sq = c % 2
        m = {n: f[n] for n in _IN_NAMES}
        m["xp"] = f["x_prompt"][2 * c:2 * c + 2]
        m["xs"] = f["x_sample"][sq]
        m["ck"] = f["cache_k"][sq].reshape(2, 256, 128)
        m["cv"] = f["cache_v"][sq].reshape(2, 256, 128)
        m["st0"] = f["state_ssm"][sq]
        m["cvec"] = np.stack([f["c_ctx"], f["c"][sq]], 0)
        m["ropec"] = rc
        m["ropes"] = rs
        m["poolcorr"] = pc
        in_maps.append({k: np.ascontiguousarray(v) for k, v in m.items()})
    res = run_bass_kernel_spmd(nc, in_maps, core_ids=list(range(8)))
    R = res.results
    y_prompt = np.concatenate([R[c]["yp"] for c in range(8)], 0).astype(np.float32)
    y_sample = np.stack([R[0]["ys"], R[1]["ys"]], 0).astype(np.float32)
    new_k = np.concatenate([R[c]["nk"] for c in range(8)], 0).reshape(16, 2, 256, 2, 64).astype(np.float32)
    new_v = np.concatenate([R[c]["nv"] for c in range(8)], 0).reshape(16, 2, 256, 2, 64).astype(np.float32)
    new_st = np.concatenate([R[c]["nst"] for c in range(8)], 0).astype(np.float32)
    return (y_prompt, y_sample, new_k, new_v, new_st)
```
